# Optimizing a Trainium2 kernel written in Bass

```python
import jax, jax.numpy as jnp
from jax import lax
import numpy as np

D_MODEL = 2048
BATCH = 4
SEQ = 2048
DEPTH = 1

MIX_WIDTH = D_MODEL
DIFF_WIDTH = MIX_WIDTH // 2
MLA_WIDTH = MIX_WIDTH - DIFF_WIDTH
DIFF_HEAD_DIM = 64
DIFF_V_DIM = 2 * DIFF_HEAD_DIM
DIFF_HEADS = DIFF_WIDTH // DIFF_V_DIM
DIFF_ROT_DIM = DIFF_HEAD_DIM // 4
ROPE_THETA = 500000.0
LAMBDA_INIT_STD = 0.1
MLA_V_DIM = 128
MLA_HEADS = MLA_WIDTH // MLA_V_DIM
MLA_NOPE_DIM = 128
MLA_ROPE_DIM = 64
MLA_ROPE_THETA = 10000.0
Q_LORA_RANK = D_MODEL // 4
KV_LORA_RANK = D_MODEL // 8
D_FF = 4 * D_MODEL
Q_BLOCK = 128
NORM_EPS = 1e-6
N_MOD = 6

DIFF_Q_COLS = DIFF_HEADS * 2 * DIFF_HEAD_DIM
DIFF_K_COLS = DIFF_HEADS * 2 * DIFF_HEAD_DIM
DIFF_V_COLS = DIFF_HEADS * DIFF_V_DIM
IN_COLS = DIFF_Q_COLS + DIFF_K_COLS + DIFF_V_COLS + Q_LORA_RANK + KV_LORA_RANK + MLA_ROPE_DIM
IN_SPLITS = (DIFF_Q_COLS,
             DIFF_Q_COLS + DIFF_K_COLS,
             DIFF_Q_COLS + DIFF_K_COLS + DIFF_V_COLS,
             DIFF_Q_COLS + DIFF_K_COLS + DIFF_V_COLS + Q_LORA_RANK,
             DIFF_Q_COLS + DIFF_K_COLS + DIFF_V_COLS + Q_LORA_RANK + KV_LORA_RANK)

kernel_name = "hybrid_diffattn_mla_sqrelu_adaln_encoder"


def rmsnorm(x, g):
    x32 = x.astype(jnp.float32)
    y = x32 * lax.rsqrt(jnp.mean(x32 * x32, axis=-1, keepdims=True) + NORM_EPS)
    return (y * g.astype(jnp.float32)).astype(x.dtype)


def modulate(h, shift, scale):
    return h * (1.0 + scale[:, None, :]) + shift[:, None, :]


def rope_cos_sin(positions, dim, theta):
    inv = 1.0 / (theta ** (jnp.arange(0, dim, 2, dtype=jnp.float32) / dim))
    ang = positions.astype(jnp.float32)[..., None] * inv
    ang = jnp.concatenate([ang, ang], axis=-1)
    return jnp.cos(ang), jnp.sin(ang)


def apply_rope(x, cos, sin):
    half = x.shape[-1] // 2
    x1, x2 = x[..., :half], x[..., half:]
    rot = jnp.concatenate([-x2, x1], axis=-1)
    return (x.astype(jnp.float32) * cos + rot.astype(jnp.float32) * sin).astype(x.dtype)


def partial_rope(x, cos, sin, rot_dim):
    return jnp.concatenate([apply_rope(x[..., :rot_dim], cos, sin), x[..., rot_dim:]], axis=-1)


def to_blocks(a):
    b, s = a.shape[0], a.shape[1]
    a = a.reshape((b, s // Q_BLOCK, Q_BLOCK) + a.shape[2:])
    return jnp.moveaxis(a, 1, 0)


def from_blocks(a):
    a = jnp.moveaxis(a, 0, 1)
    return a.reshape((a.shape[0], a.shape[1] * a.shape[2]) + a.shape[3:])


def diff_attention(q, k, v, lam):
    scale = DIFF_HEAD_DIM ** -0.5
    lam32 = lam.astype(jnp.float32)

    def block(qb):
        s = jnp.einsum('bqhcd,bkhcd->bhcqk', qb * scale, k).astype(jnp.float32)
        p = jax.nn.softmax(s, axis=-1)
        a = p[:, :, 0] - lam32 * p[:, :, 1]
        return jnp.einsum('bhqk,bkhe->bqhe', a.astype(v.dtype), v)

    return from_blocks(lax.map(block, to_blocks(q)))


def mla_attention(q_nope, q_rope, k_nope, k_rope, v):
    scale = (MLA_NOPE_DIM + MLA_ROPE_DIM) ** -0.5

    def block(args):
        qn, qr = args
        s = (jnp.einsum('bqhd,bkhd->bhqk', qn, k_nope)
             + jnp.einsum('bqhr,bkr->bhqk', qr, k_rope)).astype(jnp.float32) * scale
        p = jax.nn.softmax(s, axis=-1)
        return jnp.einsum('bhqk,bkhe->bqhe', p.astype(v.dtype), v)

    return from_blocks(lax.map(block, (to_blocks(q_nope), to_blocks(q_rope))))


def setup_inputs(seed: int = 0) -> dict:
    key = jax.random.key(seed)
    ks = jax.random.split(key, 24)
    f32 = jnp.float32

    def nrm(k, shape, fan_in):
        return jax.random.normal(k, shape, f32) * (fan_in ** -0.5)

    def gain(k, shape):
        return 1.0 + 0.02 * jax.random.normal(k, shape, f32)

    x = jax.random.normal(ks[0], (BATCH, SEQ, D_MODEL), f32)
    c = jax.random.normal(ks[1], (BATCH, D_MODEL), f32)
    positions = (jnp.arange(SEQ, dtype=jnp.int32)[None, :]
                 + jax.random.randint(ks[2], (BATCH, 1), 0, 1024, dtype=jnp.int32))
    return {
        "x": x,
        "c": c,
        "positions": positions,
        "w_ada": nrm(ks[3], (DEPTH, D_MODEL, N_MOD * D_MODEL), D_MODEL),
        "b_ada": 0.02 * jax.random.normal(ks[4], (DEPTH, N_MOD * D_MODEL), f32),
        "g_norm_mix": gain(ks[5], (DEPTH, D_MODEL)),
        "w_in": nrm(ks[6], (DEPTH, D_MODEL, IN_COLS), D_MODEL),
        "lambda_q1": LAMBDA_INIT_STD * jax.random.normal(ks[7], (DEPTH, DIFF_HEAD_DIM), f32),
        "lambda_k1": LAMBDA_INIT_STD * jax.random.normal(ks[8], (DEPTH, DIFF_HEAD_DIM), f32),
        "lambda_q2": LAMBDA_INIT_STD * jax.random.normal(ks[9], (DEPTH, DIFF_HEAD_DIM), f32),
        "lambda_k2": LAMBDA_INIT_STD * jax.random.normal(ks[10], (DEPTH, DIFF_HEAD_DIM), f32),
        "g_diff_sub": gain(ks[11], (DEPTH, DIFF_V_DIM)),
        "g_q_a": gain(ks[12], (DEPTH, Q_LORA_RANK)),
        "w_q_b": nrm(ks[13], (DEPTH, Q_LORA_RANK, MLA_HEADS * (MLA_NOPE_DIM + MLA_ROPE_DIM)), Q_LORA_RANK),
        "g_kv_a": gain(ks[14], (DEPTH, KV_LORA_RANK)),
        "w_kv_b": nrm(ks[15], (DEPTH, KV_LORA_RANK, MLA_HEADS * (MLA_NOPE_DIM + MLA_V_DIM)), KV_LORA_RANK),
        "w_out": nrm(ks[16], (DEPTH, MIX_WIDTH, D_MODEL), MIX_WIDTH),
        "g_norm_ffn": gain(ks[17], (DEPTH, D_MODEL)),
        "w_ff1": nrm(ks[18], (DEPTH, D_MODEL, D_FF), D_MODEL),
        "w_ff2": nrm(ks[19], (DEPTH, D_FF, D_MODEL), D_FF),
        "g_final": gain(ks[20], (D_MODEL,)),
    }


def reference(x, c, positions, w_ada, b_ada, g_norm_mix, w_in, lambda_q1, lambda_k1,
              lambda_q2, lambda_k2, g_diff_sub, g_q_a, w_q_b, g_kv_a, w_kv_b, w_out,
              g_norm_ffn, w_ff1, w_ff2, g_final):
    b, s, _ = x.shape
    cos_d, sin_d = rope_cos_sin(positions, DIFF_ROT_DIM, ROPE_THETA)
    cos_d, sin_d = cos_d[:, :, None, None, :], sin_d[:, :, None, None, :]
    cos_m, sin_m = rope_cos_sin(positions, MLA_ROPE_DIM, MLA_ROPE_THETA)
    silu_c = jax.nn.silu(c)

    for l in range(DEPTH):
        lambda_init = 0.8 - 0.6 * float(np.exp(-0.3 * l))
        mod = silu_c @ w_ada[l] + b_ada[l]
        shift_m, scale_m, gate_m, shift_f, scale_f, gate_f = jnp.split(mod, N_MOD, axis=-1)

        h = modulate(rmsnorm(x, g_norm_mix[l]), shift_m, scale_m)
        proj = h @ w_in[l]
        dq, dk, dv, q_lat, kv_lat, k_pe = jnp.split(proj, IN_SPLITS, axis=-1)

        dq = partial_rope(dq.reshape(b, s, DIFF_HEADS, 2, DIFF_HEAD_DIM), cos_d, sin_d, DIFF_ROT_DIM)
        dk = partial_rope(dk.reshape(b, s, DIFF_HEADS, 2, DIFF_HEAD_DIM), cos_d, sin_d, DIFF_ROT_DIM)
        dv = dv.reshape(b, s, DIFF_HEADS, DIFF_V_DIM)
        lam = (jnp.exp(jnp.sum(lambda_q1[l].astype(jnp.float32) * lambda_k1[l].astype(jnp.float32)))
               - jnp.exp(jnp.sum(lambda_q2[l].astype(jnp.float32) * lambda_k2[l].astype(jnp.float32)))
               + lambda_init)
        o_diff = diff_attention(dq, dk, dv, lam)
        o_diff = rmsnorm(o_diff, g_diff_sub[l]) * (1.0 - lambda_init)
        o_diff = o_diff.reshape(b, s, DIFF_WIDTH)

        cq = rmsnorm(q_lat, g_q_a[l])
        q = (cq @ w_q_b[l]).reshape(b, s, MLA_HEADS, MLA_NOPE_DIM + MLA_ROPE_DIM)
        q_nope = q[..., :MLA_NOPE_DIM]
        q_rope = apply_rope(q[..., MLA_NOPE_DIM:], cos_m[:, :, None, :], sin_m[:, :, None, :])
        ckv = rmsnorm(kv_lat, g_kv_a[l])
        kv = (ckv @ w_kv_b[l]).reshape(b, s, MLA_HEADS, MLA_NOPE_DIM + MLA_V_DIM)
        k_nope, v_mla = kv[..., :MLA_NOPE_DIM], kv[..., MLA_NOPE_DIM:]
        k_rope = apply_rope(k_pe, cos_m, sin_m)
        o_mla = mla_attention(q_nope, q_rope, k_nope, k_rope, v_mla).reshape(b, s, MLA_WIDTH)

        mix = jnp.concatenate([o_diff, o_mla], axis=-1) @ w_out[l]
        x = x + gate_m[:, None, :] * mix

        h = modulate(rmsnorm(x, g_norm_ffn[l]), shift_f, scale_f)
        f = jnp.square(jax.nn.relu(h @ w_ff1[l])) @ w_ff2[l]
        x = x + gate_f[:, None, :] * f

    return rmsnorm(x, g_final)
```

```python
import numpy as np
import concourse.bass as bass
import concourse.mybir as mybir
from concourse.bass_utils import run_bass_kernel_spmd

F32 = mybir.dt.float32
BF16 = mybir.dt.bfloat16
I32 = mybir.dt.int32
AF = mybir.ActivationFunctionType
ALU = mybir.AluOpType
AX = mybir.AxisListType

D = 2048
KC = 16
NTOK = 2048
NOWN = 1024
EPS = 1e-6
W_IN_COLS = 3968
TWO_PI = float(2.0 * np.pi)
C1 = 6.28125
C2 = float(2.0 * np.pi - 6.28125)
PI_LO = 3.1415925


class Res:
    __slots__ = ("name", "w", "ws", "r")

    def __init__(self, name):
        self.name = name
        self.w = None
        self.ws = {}
        self.r = {}


class Eng:
    def __init__(self, name, handle, sem, is_pe=False):
        self.name = name
        self.h = handle
        self.sem = sem
        self.count = 0
        self.waited = {}
        self.is_pe = is_pe
        self.nwaits = 0
        self.nins = 0


class Prog:
    def __init__(self, nc, n_dma_sems=24):
        self.nc = nc
        self.sems = {}
        self.engs = {}
        for nm, h in (("pe", nc.tensor), ("act", nc.scalar), ("dve", nc.vector), ("pool", nc.gpsimd), ("sp", nc.sync)):
            s = nc.alloc_semaphore("s_" + nm)
            self.sems[nm] = s
            self.engs[nm] = Eng(nm, h, s, is_pe=(nm == "pe"))
        self.dma_sems = []
        self.dma_pools = {"sp": [], "pool": []}
        for q, n in (("sp", n_dma_sems), ("pool", 8)):
            for i in range(n):
                k = "d%s%d" % (q, i)
                self.sems[k] = nc.alloc_semaphore("s_" + k)
                ent = [k, 0]
                self.dma_sems.append(ent)
                self.dma_pools[q].append(ent)
        self.dma_i = {"sp": 0, "pool": 0}
        self.res = {}

    def R(self, name):
        r = self.res.get(name)
        if r is None:
            r = Res(name)
            self.res[name] = r
        return r

    def _deps(self, reads, writes, shared=(), me=None):
        deps = {}

        def add(k, v):
            if deps.get(k, 0) < v:
                deps[k] = v
        for r in reads:
            r = self.R(r)
            if r.w is not None:
                add(*r.w)
            for k, v in r.ws.items():
                add(k, v)
            if r.name.startswith("ps") and me is not None:
                for k, v in r.r.items():
                    if k != me:
                        add(k, v)
        for w in writes:
            w = self.R(w)
            if w.w is not None:
                add(*w.w)
            for k, v in w.ws.items():
                add(k, v)
            for k, v in w.r.items():
                add(k, v)
        for w in shared:
            w = self.R(w)
            if w.w is not None:
                add(*w.w)
            for k, v in w.r.items():
                add(k, v)
        return deps

    def _wait(self, e, deps):
        for k, v in deps.items():
            if e.is_pe and k == "pe":
                continue
            if e.waited.get(k, 0) >= v:
                continue
            e.h.wait_ge(self.sems[k], v)
            e.waited[k] = v
            e.nwaits += 1

    def _mark(self, reads, writes, key, val, shared=()):
        for r in reads:
            r = self.R(r)
            if r.r.get(key, 0) < val:
                r.r[key] = val
        for w in writes:
            w = self.R(w)
            w.w = (key, val)
            w.ws = {}
            w.r = {}
        for w in shared:
            w = self.R(w)
            if w.ws.get(key, 0) < val:
                w.ws[key] = val

    def op(self, eng, fn, reads=(), writes=(), shared=()):
        e = self.engs[eng]
        self._wait(e, self._deps(reads, writes, shared, me=eng))
        ins = fn(e.h)
        e.count += 1
        e.nins += 1
        ins.then_inc(e.sem, 1)
        self._mark(reads, writes, eng, e.count, shared)
        return ins

    def dma(self, queue, out, in_, reads=(), writes=(), shared=()):
        e = self.engs[queue]
        pool = self.dma_pools[queue]
        slot = pool[self.dma_i[queue] % len(pool)]
        self.dma_i[queue] += 1
        k, uses = slot
        deps = self._deps(reads, writes, shared)
        if uses > 0:
            deps[k] = max(deps.get(k, 0), 16 * uses)
        self._wait(e, deps)
        e.h.dma_start(out=out, in_=in_).then_inc(self.sems[k], 16)
        slot[1] = uses + 1
        e.nins += 1
        self._mark(reads, writes, k, 16 * (uses + 1), shared)

    def barrier(self, engs=("pe", "act", "dve", "pool", "sp")):
        for nm in engs:
            e = self.engs[nm]
            deps = {}
            for o in ("pe", "act", "dve", "pool"):
                if o != nm and self.engs[o].count > 0:
                    deps[o] = self.engs[o].count
            for k, uses in self.dma_sems:
                if uses > 0:
                    deps[k] = 16 * uses
            self._wait(e, deps)

    def finish(self, eng="sp"):
        e = self.engs[eng]
        deps = {}
        for k, uses in self.dma_sems:
            if uses > 0:
                deps[k] = 16 * uses
        for nm, en in self.engs.items():
            if nm != eng and en.count > 0:
                deps[nm] = en.count
        self._wait(e, deps)


class Region:
    def __init__(self, nc, name, base, size):
        self.nc, self.name, self.base, self.size, self.off = nc, name, base, size, 0
        self.n = 0

    def reset(self):
        self.off = 0

    def alloc(self, name, shape, dtype):
        esz = 2 if dtype == BF16 else 4
        nbytes = esz
        for s in shape[1:]:
            nbytes *= s
        off = (self.off + 31) // 32 * 32
        assert off + nbytes <= self.size, (self.name, name, off, nbytes, self.size)
        self.off = off + nbytes
        self.n += 1
        return self.nc.alloc_sbuf_tensor_at("%s_%s_%d" % (self.name, name, self.n), list(shape), dtype, offset=self.base + off)


def build(debug=False, stop_after=None):
    nc = bass.Bass("TRN2", target_bir_lowering=False)

    def dram(name, shape, dt, out=False):
        return nc.dram_tensor(name, list(shape), dt, kind="ExternalOutput" if out else "ExternalInput").ap()

    x_d = dram("x", [NTOK, D], F32)
    pos_d = dram("pos", [1, NTOK], I32)
    c_d = dram("c", [128, KC], F32)
    cols_d = dram("cols", [128, 136], F32)
    cmat_d = dram("cmat", [128, 384], F32)
    lam_d = dram("lam", [4, 64], F32)
    gdiff_d = dram("gdiff", [1, 128], F32)
    gfin_d = dram("gfin", [1, D], F32)
    w_ada_d = dram("w_ada", [D, 6 * D], F32)
    w_in_d = dram("w_in", [D, W_IN_COLS], F32)
    w_qb_d = dram("w_qb", [512, 1536], F32)
    w_kvb_d = dram("w_kvb", [256, 2048], F32)
    w_out_d = dram("w_out", [D, D], F32)
    w_ff1_d = dram("w_ff1", [D, 4 * D], F32)
    w_ff2_d = dram("w_ff2", [4 * D, D], F32)
    y_d = dram("y", [NOWN, D], F32, out=True)
    dbg = {}
    if debug:
        dbg["hT"] = dram("dbg_hT", [128, KC * 1024], BF16, out=True)
        dbg["K"] = dram("dbg_K", [128, 8 * 2048], BF16, out=True)
        dbg["Q"] = dram("dbg_Q", [128, 8 * 1024], BF16, out=True)
        dbg["V"] = dram("dbg_V", [128, 16 * 8 * 130], BF16, out=True)
        dbg["L"] = dram("dbg_L", [128, 10240], BF16, out=True)
        dbg["O"] = dram("dbg_O", [128, 2 * KC * 512], BF16, out=True)
        dbg["QR"] = dram("dbg_QR", [128, 4 * 1024], BF16, out=True)
        dbg["mod"] = dram("dbg_mod", [128, 96], F32, out=True)

    P = Prog(nc)
    base = (nc.sbuf_base + 63) // 64 * 64
    RP = Region(nc, "P", base + 0, 6144)
    RO = Region(nc, "O", base + 6144, 32768)
    RK = Region(nc, "K", base + 38912, 32768)
    RQ = Region(nc, "Q", base + 71680, 16384)
    RV = Region(nc, "V", base + 88064, 33280)
    RL = Region(nc, "L", base + 121344, 20480)
    RW = Region(nc, "W", base + 141824, 32768)
    RX = Region(nc, "X", base + 174592, 36864)
    assert base + 174592 + 36864 <= nc.sbuf_top, (base, nc.sbuf_top)

    bfm = RP.alloc("bfm", [128, 384], BF16)
    ident_b, Rd_b, Rm_b = bfm[:, 0:128], bfm[:, 128:256], bfm[:, 256:384]
    ident_f = RP.alloc("ident_f", [128, 128], F32)
    ones_f = RP.alloc("ones_f", [128, 128], F32)
    modcols = RP.alloc("modcols", [128, 96], F32)
    cols = RP.alloc("cols", [128, 136], F32)
    a1 = RP.alloc("a1", [128, KC], F32)
    a2 = RP.alloc("a2", [128, KC], F32)
    cT = RP.alloc("cT", [128, KC], F32)
    silu_b = RP.alloc("silu_b", [128, KC], BF16)
    g08_bc = RP.alloc("g08", [128, 128], F32)
    lamt = RP.alloc("lamt", [128, 4, 64], F32)
    lc = RP.alloc("lc", [128, 8], F32)
    ss = RP.alloc("ss", [128, 16], F32)
    ms = RP.alloc("ms", [128, 16], F32)
    rstd = RP.alloc("rstd", [128, 16], F32)
    neghalf = RP.alloc("neghalf", [128, 16], F32)
    rcp = RP.alloc("rcp", [128, 16], F32)

    psum_all = nc.alloc_psum_tensor("psall", [128, 4096], F32)
    psum_all_b = psum_all.bitcast(BF16).reshape([128, 64, 128])
    psum = [psum_all[:, i * 512:(i + 1) * 512] for i in range(8)]
    psum_b = [psum_all_b[:, i * 8:(i + 1) * 8, :] for i in range(8)]
    psn = ["ps%d" % i for i in range(8)]
    bank_rr = [0]

    def next_bank(allowed=(0, 1, 2, 3, 4, 5, 6, 7)):
        b = allowed[bank_rr[0] % len(allowed)]
        bank_rr[0] += 1
        return b

    Kt = RK.alloc("K", [128, 8, NTOK], BF16)
    Qt = RQ.alloc("Q", [128, 8, NOWN], BF16)
    Vt = RV.alloc("V", [128, 16, 8, 130], BF16)
    cqT = RL.alloc("cqT", [128, 4, NOWN], BF16)
    ckvT = RL.alloc("ckvT", [128, 2, NTOK], BF16)
    krT = RL.alloc("krT", [128, NTOK], BF16)
    wslots = [RW.alloc("w%d" % i, [128, 8192], BF16) for i in range(2)]
    wslot_i = [0]

    def next_wslot():
        i = wslot_i[0] % 2
        wslot_i[0] += 1
        return wslots[i], "wslot%d" % i

    pend = []

    def defer(fn):
        pend.append(fn)

    def flush(keep=0):
        while len(pend) > keep:
            pend.pop(0)()

    def load_w_cols(s3, sname, src, ncols=512):
        h = 256
        P.dma("pool", s3[:, :, 0:h], src[:, :, 0:h], writes=[sname + "a"], shared=[sname + "fa", sname + "fb"])
        P.dma("pool", s3[:, :, h:ncols], src[:, :, h:ncols], writes=[sname + "b"], shared=[sname + "fa", sname + "fb"])

    def wn(sname, c0):
        return sname + ("a" if c0 < 256 else "b")

    RX.reset()
    cstage = nc.alloc_sbuf_tensor_at("K_cstage", [128, 384], F32, offset=RK.base)
    P.dma("sp", cstage[:], cmat_d, writes=["cstage"])
    P.dma("sp", cols[:], cols_d, writes=["cols"])
    P.dma("sp", cT[:], c_d, writes=["cT"])
    for i in range(4):
        P.dma("sp", lamt[:, i, :], lam_d[i:i + 1, :].partition_broadcast(128), writes=["lamt"])
    P.dma("sp", g08_bc[:], gdiff_d.partition_broadcast(128), writes=["g08"])
    P.op("dve", lambda e: e.tensor_copy(out=bfm[:], in_=cstage[:]), reads=["cstage"], writes=["bfm"])
    P.op("dve", lambda e: e.tensor_copy(out=ident_f[:], in_=cstage[:, 0:128]), reads=["cstage"], writes=["ident_f"])
    P.op("pool", lambda e: e.memset(ones_f[:], 1.0), writes=["ones_f"])
    P.op("pool", lambda e: e.memset(neghalf[:], -0.5), writes=["neghalf"])
    P.op("pool", lambda e: e.memset(modcols[:], 0.0), writes=["modcols"])
    P.op("pool", lambda e: e.memset(Vt[:, :, :, 128:130], 1.0), writes=["Vones"])
    P.op("act", lambda e: e.activation(out=silu_b[:], in_=cT[:], func=AF.Silu), reads=["cT"], writes=["silu_b"])
    P.op("dve", lambda e: e.tensor_tensor(out=lamt[:, 0, :], in0=lamt[:, 0, :], in1=lamt[:, 1, :], op=ALU.mult), reads=["lamt"], writes=["lamt"])
    P.op("dve", lambda e: e.tensor_tensor(out=lamt[:, 2, :], in0=lamt[:, 2, :], in1=lamt[:, 3, :], op=ALU.mult), reads=["lamt"], writes=["lamt"])
    P.op("dve", lambda e: e.reduce_sum(out=lc[:, 0:1], in_=lamt[:, 0, :], axis=AX.X), reads=["lamt"], writes=["lc"])
    P.op("dve", lambda e: e.reduce_sum(out=lc[:, 1:2], in_=lamt[:, 2, :], axis=AX.X), reads=["lamt"], writes=["lc"])
    P.op("act", lambda e: e.activation(out=lc[:, 2:4], in_=lc[:, 0:2], func=AF.Exp), reads=["lc"], writes=["lc"])
    P.op("dve", lambda e: e.tensor_tensor(out=lc[:, 4:5], in0=lc[:, 2:3], in1=lc[:, 3:4], op=ALU.subtract), reads=["lc"], writes=["lc"])
    P.op("dve", lambda e: e.tensor_scalar(out=lc[:, 5:6], in0=lc[:, 4:5], scalar1=-1.0, scalar2=-0.2, op0=ALU.mult, op1=ALU.add), reads=["lc"], writes=["lc"])
    neglam = lc[:, 5:6]
    P.op("dve", lambda e: e.tensor_scalar(out=g08_bc[:], in0=g08_bc[:], scalar1=0.8, scalar2=None, op0=ALU.mult), reads=["g08"], writes=["g08"])

    w_ada_v = w_ada_d.rearrange("(kc p) n -> p kc n", p=128)
    w_in_v = w_in_d.rearrange("(kc p) n -> p kc n", p=128)

    ada_slots = {}

    def ada_load(t):
        slot, sname = next_wslot()
        s3 = slot.reshape([128, KC, 512])
        load_w_cols(s3, sname, w_ada_v[:, :, t * 512:(t + 1) * 512])
        ada_slots[t] = (s3, sname)

    def ada_compute(t, bank):
        s3, sname = ada_slots.pop(t)
        for cc in range(4):
            for kc in range(KC):
                P.op("pe", lambda e, cc=cc, kc=kc: e.matmul(psum[bank][:, cc:cc + 1], lhsT=s3[:, kc, cc * 128:(cc + 1) * 128],
                                                            rhs=silu_b[:, kc:kc + 1], start=(kc == 0), stop=(kc == KC - 1)),
                     reads=[wn(sname, cc * 128), "silu_b"], writes=[psn[bank]])
        P.op("dve", lambda e: e.tensor_tensor(out=modcols[:, t * 4:(t + 1) * 4], in0=psum[bank][:, 0:4],
                                              in1=cols[:, 40 + t * 4:40 + (t + 1) * 4], op=ALU.add),
             reads=[psn[bank], "cols"], shared=["modcols"])

    def ada_row_x(t, bank, row):
        s3, sname = ada_slots[t]
        for kc in range(KC):
            P.op("pe", lambda e, kc=kc: e.matmul(psum[bank][0:1, :], lhsT=silu_b[:, kc:kc + 1], rhs=s3[:, kc, :], start=(kc == 0), stop=(kc == KC - 1)),
                 reads=[sname + "a", sname + "b", "silu_b"], writes=[psn[bank]])
        P.op("dve", lambda e: e.tensor_copy(out=row, in_=psum[bank][0:1, :]), reads=[psn[bank]], writes=["ob0", "ob1", "ob2", "ob3"])

    def ada_row_y(t, bank, row):
        ada_slots.pop(t)
        for cc in range(4):
            P.op("pe", lambda e, cc=cc: e.matmul(psum[bank][:, cc:cc + 1], lhsT=row[0:1, cc * 128:(cc + 1) * 128], rhs=ones_f[0:1, 0:1], start=True, stop=True),
                 reads=["ob0", "ob1", "ob2", "ob3", "ones_f"], writes=[psn[bank]])
        P.op("dve", lambda e: e.tensor_tensor(out=modcols[:, t * 4:(t + 1) * 4], in0=psum[bank][:, 0:4],
                                              in1=cols[:, 40 + t * 4:40 + (t + 1) * 4], op=ALU.add),
             reads=[psn[bank], "cols"], shared=["modcols"])

    ada_load(0)

    preloaded = {}

    def inproj_load(t):
        slot, sname = next_wslot()
        ncols = 512 if t < 7 else 384
        s3 = slot.reshape([128, KC, 512])
        load_w_cols(s3, sname, w_in_v[:, :, t * 512:t * 512 + ncols], ncols)
        return s3, sname

    def ada_hook(i):
        if i + 1 < 8:
            ada_load(i + 1)
        else:
            preloaded["other"] = inproj_load(4)
        ada_compute(i, next_bank())

    def rsqrt_cols(n, dim, c0=8):
        P.op("dve", lambda e: e.tensor_scalar(out=ms[:, c0:c0 + n], in0=ss[:, c0:c0 + n], scalar1=1.0 / dim, scalar2=EPS, op0=ALU.mult, op1=ALU.add),
             reads=["ssA"], writes=["msA"])
        P.op("pool", lambda e: e.tensor_tensor(out=rstd[:, c0:c0 + n], in0=ms[:, c0:c0 + n], in1=neghalf[:, 0:n], op=ALU.pow),
             reads=["msA", "neghalf"], writes=["rstdA"])

    hT_ctr = [0]

    def make_hT(src_tiles, dst, dst_tok0, acols, shcols, acname, xn_bufs, junk, dst_name, plain=False, hook=None):
        cols_of = {}

        def stage_a(i):
            xt, xname, loader = src_tiles[i]
            if loader is not None:
                loader()
            c = hT_ctr[0] % 8
            hT_ctr[0] += 1
            cols_of[i] = c
            sc, mc, rc = "ss%d" % c, "ms%d" % c, "rstd%d" % c
            P.op("act", lambda e: e.activation(out=junk[:], in_=xt, func=AF.Square, accum_out=ss[:, c:c + 1]), reads=[xname], writes=["junk", sc])
            P.op("act", lambda e: e.activation(out=ms[:, c:c + 1], in_=ss[:, c:c + 1], func=AF.Sqrt, bias=epscol, scale=1.0 / D),
                 reads=[sc, "epscol"], writes=[mc])
            P.op("dve", lambda e: e.reciprocal(out=rstd[:, c:c + 1], in_=ms[:, c:c + 1]), reads=[mc], writes=[rc])
            xn, xnnames = xn_bufs[i % len(xn_bufs)]
            P.op("dve", lambda e: e.tensor_scalar(out=xn, in0=xt, scalar1=rstd[:, c:c + 1], scalar2=None, op0=ALU.mult),
                 reads=[xname, rc], writes=xnnames)

        def stage_b(i):
            xn, xnnames = xn_bufs[i % len(xn_bufs)]
            for g in range(2):
                b = next_bank()
                for j in range(8):
                    kc = g * 8 + j
                    P.op("pe", lambda e, j=j, kc=kc: e.transpose(out=psum_b[b][:, j, :], in_=xn[:, kc * 128:(kc + 1) * 128], identity=ident_b),
                         reads=xnnames + ["bfm"], writes=[psn[b]])
                for j in range(8):
                    kc = g * 8 + j
                    o = dst[:, kc, dst_tok0 + i * 128:dst_tok0 + (i + 1) * 128]
                    if plain:
                        if g == 0:
                            P.op("dve", lambda e, j=j, o=o: e.tensor_copy(out=o, in_=psum_b[b][:, j, :]), reads=[psn[b]], shared=[dst_name])
                        else:
                            P.op("act", lambda e, j=j, o=o: e.activation(out=o, in_=psum_b[b][:, j, :], func=AF.Copy), reads=[psn[b]], shared=[dst_name])
                    elif g == 0:
                        P.op("dve", lambda e, j=j, kc=kc, o=o: e.tensor_scalar(out=o, in0=psum_b[b][:, j, :], scalar1=acols[:, kc:kc + 1],
                                                                               scalar2=shcols[:, kc:kc + 1], op0=ALU.mult, op1=ALU.add),
                             reads=[psn[b], acname, "modcols"], shared=[dst_name])
                    else:
                        P.op("act", lambda e, j=j, kc=kc, o=o: e.activation(out=o, in_=psum_b[b][:, j, :], func=AF.Identity,
                                                                            bias=shcols[:, kc:kc + 1], scale=acols[:, kc:kc + 1]),
                             reads=[psn[b], acname, "modcols"], shared=[dst_name])

        n = len(src_tiles)
        stage_a(0)
        for i in range(n):
            if i + 1 < n:
                stage_a(i + 1)
            if hook is not None:
                hook(i)
            stage_b(i)

    def rope_tables(tok0, blk, invcol, ctab, stab, posi, posf, t1, t2, t3, tabname, ki=None):
        t0 = tok0 + blk * 512
        P.dma("sp", posi[:], pos_d[0:1, t0:t0 + 512].partition_broadcast(128), writes=["posi"])
        P.op("dve", lambda e: e.tensor_copy(out=posf[:], in_=posi[:]), reads=["posi"], writes=["posf"])
        P.op("dve", lambda e: e.tensor_scalar(out=t1[:], in0=posf[:], scalar1=invcol, scalar2=None, op0=ALU.mult), reads=["posf", "cols"], writes=["t1"])
        P.op("dve", lambda e: e.tensor_scalar(out=t2[:], in0=t1[:], scalar1=1.0 / TWO_PI, scalar2=None, op0=ALU.mult), reads=["t1"], writes=["t2"])
        kt_ = ki if ki is not None else posi
        kn_ = "ki" if ki is not None else "posi"
        P.op("dve", lambda e: e.tensor_copy(out=kt_[:], in_=t2[:]), reads=["t2"], writes=[kn_])
        P.op("dve", lambda e: e.tensor_copy(out=t2[:], in_=kt_[:]), reads=[kn_], writes=["t2"])
        P.op("dve", lambda e: e.scalar_tensor_tensor(out=t1[:], in0=t2[:], scalar=-C1, in1=t1[:], op0=ALU.mult, op1=ALU.add), reads=["t1", "t2"], writes=["t1"])
        P.op("dve", lambda e: e.scalar_tensor_tensor(out=t1[:], in0=t2[:], scalar=-C2, in1=t1[:], op0=ALU.mult, op1=ALU.add), reads=["t1", "t2"], writes=["t1"])
        for which, tab in ((0, stab), (1, ctab)):
            if which == 1:
                P.op("dve", lambda e: e.tensor_scalar(out=t1[:], in0=t1[:], scalar1=float(np.pi / 2), scalar2=None, op0=ALU.add), reads=["t1"], writes=["t1"])
            P.op("dve", lambda e: e.tensor_scalar(out=t3[:], in0=t1[:], scalar1=float(np.pi), scalar2=None, op0=ALU.is_gt), reads=["t1"], writes=["t3", "posf"])
            P.op("dve", lambda e: e.scalar_tensor_tensor(out=t2[:], in0=t3[:], scalar=-TWO_PI, in1=t1[:], op0=ALU.mult, op1=ALU.add), reads=["t1", "t3", "posf"], writes=["t2"])
            P.op("dve", lambda e: e.tensor_scalar(out=t3[:], in0=t2[:], scalar1=float(-np.pi), scalar2=None, op0=ALU.is_lt), reads=["t2"], writes=["t3", "posf"])
            P.op("dve", lambda e: e.scalar_tensor_tensor(out=t2[:], in0=t3[:], scalar=TWO_PI, in1=t2[:], op0=ALU.mult, op1=ALU.add), reads=["t2", "t3", "posf"], writes=["t2"])
            P.op("dve", lambda e: e.tensor_scalar(out=t2[:], in0=t2[:], scalar1=PI_LO, scalar2=-PI_LO, op0=ALU.min, op1=ALU.max), reads=["t2"], writes=["t2"])
            P.op("act", lambda e, tab=tab: e.activation(out=tab[:], in_=t2[:], func=AF.Sin), reads=["t2"], writes=[tabname])

    def rope_evac(b, dst, dst_name, ctab, stab, tabname, Rmat, abufs, banks=None):
        ab, abname = abufs[0]
        abufs.append(abufs.pop(0))
        P.op("dve", lambda e: e.tensor_tensor(out=ab[:, 0, :], in0=psum[b][:], in1=ctab[:], op=ALU.mult), reads=[psn[b], tabname], writes=[abname + "c"])
        P.op("dve", lambda e: e.tensor_tensor(out=ab[:, 1, :], in0=psum[b][:], in1=stab[:], op=ALU.mult), reads=[psn[b], tabname], writes=[abname + "s"])

        def later():
            b2 = next_bank(banks) if banks is not None else next_bank()
            P.op("pe", lambda e: e.matmul(psum[b2][:], lhsT=ident_b, rhs=ab[:, 0, :], start=True, stop=False), reads=[abname + "c", "bfm"], writes=[psn[b2]])
            P.op("pe", lambda e: e.matmul(psum[b2][:], lhsT=Rmat, rhs=ab[:, 1, :], start=False, stop=True), reads=[abname + "s", "bfm"], writes=[psn[b2]])
            P.op("act", lambda e: e.activation(out=dst, in_=psum[b2][:], func=AF.Copy), reads=[psn[b2]], shared=[dst_name])
        defer(later)

    evac_alt = [0]

    def copy_evac(out, in_, reads, writes, shared=(), eng=None):
        evac_alt[0] += 1
        if eng is None:
            eng = "act" if evac_alt[0] % 2 == 0 else "dve"
        if eng == "act":
            P.op("act", lambda e: e.activation(out=out, in_=in_, func=AF.Copy), reads=reads, writes=writes, shared=shared)
        else:
            P.op("dve", lambda e: e.tensor_copy(out=out, in_=in_), reads=reads, writes=writes, shared=shared)

    def latent_norm(nch, b_list, latf, sq, rbc, gcol0, dst, tok_sl, dst_name, dim):
        for cc, b in enumerate(b_list):
            P.op("act", lambda e, cc=cc, b=b: e.activation(out=latf[:, cc, :], in_=psum[b][:], func=AF.Copy), reads=[psn[b]], writes=["latf"])
        bs = next_bank()
        for cc in range(nch):
            P.op("dve", lambda e, cc=cc: e.tensor_tensor(out=sq[:], in0=latf[:, cc, :], in1=latf[:, cc, :], op=ALU.mult), reads=["latf"], writes=["sq"])
            P.op("pe", lambda e, cc=cc: e.matmul(psum[bs][:], lhsT=ones_f[:], rhs=sq[:], start=(cc == 0), stop=(cc == nch - 1)),
                 reads=["sq", "ones_f"], writes=[psn[bs]])
        P.op("act", lambda e: e.activation(out=rbc[:], in_=psum[bs][:], func=AF.Sqrt, bias=epscol, scale=1.0 / dim), reads=[psn[bs], "epscol"], writes=["rbc"])
        P.op("dve", lambda e: e.reciprocal(out=rbc[:], in_=rbc[:]), reads=["rbc"], writes=["rbc"])
        for cc in range(nch):
            P.op("dve", lambda e, cc=cc: e.scalar_tensor_tensor(out=dst[:, cc, tok_sl], in0=latf[:, cc, :], scalar=cols[:, gcol0 + cc:gcol0 + cc + 1],
                                                                in1=rbc[:], op0=ALU.mult, op1=ALU.mult),
                 reads=["latf", "rbc", "cols"], writes=[dst_name])

    epscol_t = RP.alloc("epscol", [128, 1], F32)
    epscol = epscol_t[:, 0:1]
    P.op("pool", lambda e: e.memset(epscol_t[:], EPS), writes=["epscol"])

    hT = RO.alloc("hT", [128, KC, 1024], BF16)

    def proj_fm(s3, sname, c0, blk, b):
        for kc in range(KC):
            P.op("pe", lambda e, kc=kc: e.matmul(psum[b][:], lhsT=s3[:, kc, c0:c0 + 128], rhs=hT[:, kc, blk * 512:(blk + 1) * 512],
                                                 start=(kc == 0), stop=(kc == KC - 1)),
                 reads=[wn(sname, c0), "hT"], writes=[psn[b]])
        flush()

    for pas, tok0 in (("other", NOWN), ("own", 0)):
        RX.reset()
        xst = [(RX.alloc("xst%d" % i, [128, D], F32), "xst%d" % i) for i in range(2)]
        xnb = [(RX.alloc("xn%d" % i, [128, D], BF16), "xn%d" % i) for i in range(2)]
        junk = RX.alloc("junk", [128, D], BF16)
        if pas == "own":
            preloaded["own"] = inproj_load(4)
            P.barrier()
        src = []
        for i in range(8):
            xt, xname = xst[i % 2]

            def loader(i=i, xt=xt, xname=xname):
                P.dma("sp", xt[:], x_d[tok0 + i * 128:tok0 + (i + 1) * 128, :], writes=[xname])
            src.append((xt[:], xname, loader))
        if pas == "other":
            make_hT(src, hT, 0, a1, modcols[:, 0:16], "a1", [(xb[:], [xbn]) for xb, xbn in xnb], junk, "hT", plain=True, hook=ada_hook)
            P.op("dve", lambda e: e.scalar_tensor_tensor(out=a1[:], in0=modcols[:, 16:32], scalar=1.0, in1=cols[:, 0:16], op0=ALU.add, op1=ALU.mult),
                 reads=["modcols", "cols"], writes=["a1"])
            for kc in range(KC):
                P.op("dve", lambda e, kc=kc: e.tensor_scalar(out=hT[:, kc, :], in0=hT[:, kc, :], scalar1=a1[:, kc:kc + 1], scalar2=modcols[:, kc:kc + 1],
                                                             op0=ALU.mult, op1=ALU.add), reads=["hT", "a1", "modcols"], writes=["hT"])
        else:
            make_hT(src, hT, 0, a1, modcols[:, 0:16], "a1", [(xb[:], [xbn]) for xb, xbn in xnb], junk, "hT")
        P.barrier()
        RX.reset()
        ctab = [RX.alloc("ctab%d" % i, [128, 512], F32) for i in range(2)]
        stab = [RX.alloc("stab%d" % i, [128, 512], F32) for i in range(2)]
        posi = RX.alloc("posi", [128, 512], I32)
        posf = RX.alloc("posf", [128, 512], F32)
        t1 = RX.alloc("t1", [128, 512], F32)
        t2 = RX.alloc("t2", [128, 512], F32)
        t3 = posf
        kiA = RX.alloc("kiA", [128, 512], I32)
        abufs = [(RX.alloc("ab%d" % i, [128, 2, 512], BF16), "ab%d" % i) for i in range(2)]
        latf = RX.alloc("latf", [128, 4, 512], F32)
        sq = RX.alloc("sq", [128, 512], F32)
        rbc = RX.alloc("rbc", [128, 512], F32)
        tiles = [4, 7, 5, 6, 0, 1, 2, 3] if pas == "own" else [4, 7, 5, 2, 3]
        for blk in range(2):
            rope_tables(tok0, blk, cols[:, 39:40], ctab[blk], stab[blk], posi, posf, t1, t2, t3, "tab%d" % blk, ki=kiA)
        for ti, t in enumerate(tiles):
            if ti == 0:
                assert t == 4
                s3, sname = preloaded.pop(pas)
            else:
                s3, sname = inproj_load(t)
            if t == 7:
                for blk in range(2):
                    bl = []
                    for cc in range(2):
                        b = next_bank()
                        proj_fm(s3, sname, cc * 128, blk, b)
                        bl.append(b)
                    latent_norm(2, bl, latf, sq, rbc, 36, ckvT, slice(tok0 + blk * 512, tok0 + (blk + 1) * 512), "ckvT", 256)
                    b = next_bank()
                    proj_fm(s3, sname, 256, blk, b)
                    rope_evac(b, krT[:, tok0 + blk * 512:tok0 + (blk + 1) * 512], "krT", ctab[blk], stab[blk], "tab%d" % blk, Rm_b, abufs)
                for blk in range(2):
                    rope_tables(tok0, blk, cols[:, 38:39], ctab[blk], stab[blk], posi, posf, t1, t2, t3, "tab%d" % blk, ki=kiA)
            elif t == 6:
                for blk in range(2):
                    bl = []
                    for cc in range(4):
                        b = next_bank()
                        proj_fm(s3, sname, cc * 128, blk, b)
                        bl.append(b)
                    latent_norm(4, bl, latf, sq, rbc, 32, cqT, slice(blk * 512, (blk + 1) * 512), "cqT", 512)
            elif t < 4:
                for cc in range(4):
                    h = (t % 2) * 4 + cc
                    for blk in range(2):
                        b = next_bank()
                        proj_fm(s3, sname, cc * 128, blk, b)
                        if t < 2:
                            dst, dname = Qt[:, h, blk * 512:(blk + 1) * 512], "Q%d" % h
                        else:
                            dst, dname = Kt[:, h, tok0 + blk * 512:tok0 + (blk + 1) * 512], "K%d" % h
                        rope_evac(b, dst, dname, ctab[blk], stab[blk], "tab%d" % blk, Rd_b, abufs)
            else:
                for tt in range(8):
                    b = next_bank()
                    kt = (tok0 // 128) + tt
                    for kc in range(KC):
                        P.op("pe", lambda e, kc=kc, tt=tt, b=b: e.matmul(psum[b][:], lhsT=hT[:, kc, tt * 128:(tt + 1) * 128], rhs=s3[:, kc, :],
                                                                         start=(kc == 0), stop=(kc == KC - 1)),
                             reads=[sname + "a", sname + "b", "hT"], writes=[psn[b]])
                    flush()
                    h0 = (t - 4) * 4
                    copy_evac(Vt[:, kt, h0:h0 + 4, 0:128], psum[b][:].rearrange("p (h e) -> p h e", h=4), [psn[b]], [], shared=["V%d" % kt], eng="act")
        flush()

    if debug:
        P.barrier()
        P.dma("sp", dbg["hT"], hT[:].rearrange("p a b -> p (a b)"), reads=["hT"])
        P.dma("sp", dbg["K"], Kt[:].rearrange("p a b -> p (a b)"), reads=["K%d" % h for h in range(8)])
        P.dma("sp", dbg["Q"], Qt[:].rearrange("p a b -> p (a b)"), reads=["Q%d" % h for h in range(8)])
        P.dma("sp", dbg["V"], Vt[:].rearrange("p a b c -> p (a b c)"), reads=["V%d" % k for k in range(16)] + ["Vones"])
        P.dma("sp", dbg["L"][:, 0:4096], cqT[:].rearrange("p a b -> p (a b)"), reads=["cqT"])
        P.dma("sp", dbg["L"][:, 4096:8192], ckvT[:].rearrange("p a b -> p (a b)"), reads=["ckvT"])
        P.dma("sp", dbg["L"][:, 8192:10240], krT[:], reads=["krT"])
        P.dma("sp", dbg["mod"], modcols[:], reads=["modcols"])
    if stop_after == "A":
        P.finish("sp")
        return nc, P

    P.barrier()
    RO.reset()
    oT = RO.alloc("oT", [128, 2, KC, 512], BF16)
    RX.reset()
    pt = [(RX.alloc("pt%d" % i, [128, 1024], BF16), "pt%d" % i) for i in range(3)]
    osb = RX.alloc("osb", [128, 8, 130], F32)
    ob = RX.alloc("ob", [128, 4, 128], F32)
    onb = RX.alloc("onb", [128, 8, 128], BF16)
    jk2 = RX.alloc("jk2", [128, 128], BF16)
    SB = [[0, 1], [2, 3]]
    OB = [4, 5, 6]
    TB = 7

    def acc_ap(a):
        return psum[OB[a // 3]][:, (a % 3) * 130:(a % 3) * 130 + 129]

    stages = {}

    def stage(kt, fn):
        stages.setdefault(kt, []).append(fn)

    def run_stages(kt):
        for fn in stages.pop(kt, []):
            fn()

    def attention_unit(qk_emit, scale, finalize):
        def QK(kt):
            qk_emit(kt, SB[kt % 2])

        QK(0)
        QK(1)
        started = set()
        for kt in range(16):
            pb, pbname = pt[kt % 3]
            b0 = SB[kt % 2][0]
            P.op("act", lambda e, b0=b0, pb=pb: e.activation(out=pb[:], in_=psum_all[:, b0 * 512:(b0 + 2) * 512], func=AF.Exp, scale=scale),
                 reads=[psn[b0], psn[b0 + 1]], writes=[pbname])
            for c in range(2):
                for j in range(4):
                    a = j * 2 + c
                    hv = vh[c]
                    st = (kt == 0 and OB[a // 3] not in started)
                    started.add(OB[a // 3])
                    P.op("pe", lambda e, a=a, c=c, j=j, kt=kt, hv=hv, pb=pb, st=st: e.matmul(
                        acc_ap(a), lhsT=pb[:, c * 512 + j * 128:c * 512 + (j + 1) * 128], rhs=Vt[:, kt, hv, 0:129],
                        start=st, stop=(kt == 15), skip_group_check=True),
                        reads=[pbname, "V%d" % kt, "Vones"], writes=[psn[OB[a // 3]]])
            if kt + 2 < 16:
                QK(kt + 2)
            run_stages(kt)
        for i, b in enumerate(OB):
            n = 3 if i < 2 else 2
            o = osb[:, i * 3:i * 3 + n, :]
            src = psum[b][:, 0:n * 130].rearrange("p (a e) -> p a e", a=n)
            P.op("dve", lambda e, o=o, src=src: e.tensor_copy(out=o, in_=src), reads=[psn[b]], shared=["osb"])
        finalize()

    vh = [0, 0]

    wq = wslots[0][:, 0:6144].rearrange("p (a b) -> p a b", a=4)
    wkv = wslots[1][:, 0:4096].rearrange("p (a b) -> p a b", a=2)
    qrT = nc.alloc_sbuf_tensor_at("W_qrT", [128, 4, NOWN], BF16, offset=RW.base + 16384 + 8192)
    ctabC = [RX.alloc("ctabC%d" % i, [128, 512], F32) for i in range(2)]
    stabC = [RX.alloc("stabC%d" % i, [128, 512], F32) for i in range(2)]
    posiC = RX.alloc("posiC", [128, 512], I32)
    kiC = RX.alloc("kiC", [128, 512], I32)
    posfC = RX.alloc("posfC", [128, 512], F32)
    t1C = RX.alloc("t1C", [128, 512], F32)
    t2C = RX.alloc("t2C", [128, 512], F32)
    abufsC = [(RX.alloc("abC%d" % i, [128, 2, 512], BF16), "abC%d" % i) for i in range(1)]

    def mla_prefetch_weights():
        P.dma("pool", wq, w_qb_d.rearrange("(kc p) n -> p kc n", p=128), writes=["wslot0"], shared=["wslot0a", "wslot0b"])
        P.dma("pool", wkv, w_kvb_d.rearrange("(kc p) n -> p kc n", p=128), writes=["wslot1"], shared=["wslot1a", "wslot1b"])

    def mla_tables(blk):
        rope_tables(0, blk, cols[:, 39:40], ctabC[blk], stabC[blk], posiC, posfC, t1C, t2C, posfC, "tabC%d" % blk, ki=kiC)

    for h in range(8):
        for qb in range(2):
            def qk_emit(kt, banks, h=h, qb=qb):
                for c in range(2):
                    P.op("pe", lambda e, c=c: e.matmul(psum[banks[c]][:], lhsT=Kt[c * 64:(c + 1) * 64, h, kt * 128:(kt + 1) * 128],
                                                       rhs=Qt[c * 64:(c + 1) * 64, h, qb * 512:(qb + 1) * 512], start=True, stop=True),
                         reads=["K%d" % h, "Q%d" % h], writes=[psn[banks[c]]])

            def finalize(h=h, qb=qb):
                def s1():
                    P.op("dve", lambda e: e.reciprocal(out=rcp[:, 0:8], in_=osb[:, :, 128]), reads=["osb"], writes=["rcp"])
                    P.op("dve", lambda e: e.tensor_scalar(out=rcp[:, 8:12], in0=rcp[:, 0:8].rearrange("p (j c) -> p j c", c=2)[:, :, 1], scalar1=neglam, scalar2=None, op0=ALU.mult),
                         reads=["rcp", "lc"], writes=["rcp2"])
                    for j in range(4):
                        P.op("dve", lambda e, j=j: e.tensor_scalar(out=ob[:, j, :], in0=osb[:, 2 * j + 1, 0:128], scalar1=rcp[:, 8 + j:9 + j], scalar2=None, op0=ALU.mult),
                             reads=["osb", "rcp2"], writes=["ob%d" % j])
                        P.op("dve", lambda e, j=j: e.scalar_tensor_tensor(out=ob[:, j, :], in0=osb[:, 2 * j, 0:128], scalar=rcp[:, 2 * j:2 * j + 1], in1=ob[:, j, :],
                                                                          op0=ALU.mult, op1=ALU.add),
                             reads=["osb", "rcp", "ob%d" % j], writes=["ob%d" % j])

                def s2():
                    for j in range(4):
                        P.op("act", lambda e, j=j: e.activation(out=jk2[:], in_=ob[:, j, :], func=AF.Square, accum_out=ss[:, 8 + j:9 + j]),
                             reads=["ob%d" % j], writes=["jk2"], shared=["ssA"])

                def s3():
                    rsqrt_cols(4, 128)
                    for j in range(4):
                        P.op("dve", lambda e, j=j: e.scalar_tensor_tensor(out=onb[:, j, :], in0=ob[:, j, :], scalar=rstd[:, 8 + j:9 + j], in1=g08_bc[:], op0=ALU.mult, op1=ALU.mult),
                             reads=["ob%d" % j, "rstdA", "g08"], shared=["onb"])

                def s4():
                    for j in range(4):
                        P.op("pe", lambda e, j=j: e.transpose(out=psum_b[TB][:, j, :], in_=onb[:, j, :], identity=ident_b), reads=["onb", "bfm"], writes=[psn[TB]])
                    P.op("dve", lambda e: e.tensor_copy(out=oT[:, qb, h, :], in_=psum_b[TB][:, 0:4, :].rearrange("p a b -> p (a b)")), reads=[psn[TB]], shared=["oT"])
                stage(1, s1)
                stage(4, s2)
                stage(7, s3)
                stage(10, s4)

            vh[0] = h
            vh[1] = h
            it = h * 2 + qb
            if it == 0:
                ada_load(8)
            if it < 8:
                arow = ob[0:1].rearrange("p a b -> p (a b)")
                stage(0, lambda it=it: ada_load(9 + 2 * it))
                stage(7, lambda it=it: ada_row_x(8 + 2 * it, TB, arow))
                if it < 7:
                    stage(8, lambda it=it: ada_load(10 + 2 * it))
                stage(9, lambda it=it: ada_row_y(8 + 2 * it, TB, arow))
                stage(13, lambda it=it: ada_row_x(9 + 2 * it, TB, arow))
                stage(15, lambda it=it: ada_row_y(9 + 2 * it, TB, arow))
            if it == 13:
                stage(0, mla_prefetch_weights)
            if it == 14:
                stage(2, lambda: mla_tables(0))
                stage(9, lambda: mla_tables(1))
            attention_unit(qk_emit, 0.125, finalize)
    for kt in range(16):
        run_stages(kt)

    if stop_after == "B":
        if debug:
            P.barrier()
            for qb in range(2):
                P.dma("sp", dbg["O"][:, qb * 8192:qb * 8192 + 4096], oT[:, qb, 0:8, :].rearrange("p b c -> p (b c)"), reads=["oT"])
            P.dma("sp", dbg["mod"], modcols[:], reads=["modcols"])
        P.finish("sp")
        return nc, P

    sname, sname2 = "wslot0", "wslot1"
    ctab, stab, abufs = ctabC, stabC, abufsC
    NB = (0, 1, 2, 3)
    for h in range(8):
        for blk in range(2):
            b = next_bank(NB)
            for kc in range(4):
                P.op("pe", lambda e, kc=kc, h=h, blk=blk, b=b: e.matmul(psum[b][:], lhsT=wq[:, kc, h * 128:(h + 1) * 128], rhs=cqT[:, kc, blk * 512:(blk + 1) * 512],
                                                                        start=(kc == 0), stop=(kc == 3)), reads=[sname, "cqT"], writes=[psn[b]])
            copy_evac(Qt[:, h, blk * 512:(blk + 1) * 512], psum[b][:], [psn[b]], ["Q%d" % h])
    for hp in range(4):
        for blk in range(2):
            b = next_bank(NB)
            for kc in range(4):
                P.op("pe", lambda e, kc=kc, hp=hp, blk=blk, b=b: e.matmul(psum[b][:], lhsT=wq[:, kc, 1024 + hp * 128:1024 + (hp + 1) * 128],
                                                                          rhs=cqT[:, kc, blk * 512:(blk + 1) * 512], start=(kc == 0), stop=(kc == 3)),
                     reads=[sname, "cqT"], writes=[psn[b]])
            ab, abname = abufs[0]
            abufs.append(abufs.pop(0))
            P.op("dve", lambda e, b=b, ab=ab, blk=blk: e.tensor_tensor(out=ab[:, 0, :], in0=psum[b][:], in1=ctab[blk][:], op=ALU.mult), reads=[psn[b], "tabC%d" % blk], writes=[abname])
            P.op("dve", lambda e, b=b, ab=ab, blk=blk: e.tensor_tensor(out=ab[:, 1, :], in0=psum[b][:], in1=stab[blk][:], op=ALU.mult), reads=[psn[b], "tabC%d" % blk], writes=[abname])
            b2 = next_bank(NB)
            P.op("pe", lambda e, b2=b2, ab=ab: e.matmul(psum[b2][:], lhsT=ident_b, rhs=ab[:, 0, :], start=True, stop=False), reads=[abname, "bfm"], writes=[psn[b2]])
            P.op("pe", lambda e, b2=b2, ab=ab: e.matmul(psum[b2][:], lhsT=Rm_b, rhs=ab[:, 1, :], start=False, stop=True), reads=[abname, "bfm"], writes=[psn[b2]])
            P.op("act", lambda e, b2=b2, hp=hp, blk=blk: e.activation(out=qrT[:, hp, blk * 512:(blk + 1) * 512], in_=psum[b2][:], func=AF.Copy), reads=[psn[b2]],
                 shared=["qrT", "wslot1a", "wslot1b"])
    for h in range(8):
        for blk in range(4):
            b = next_bank(NB)
            for kc in range(2):
                P.op("pe", lambda e, kc=kc, h=h, blk=blk, b=b: e.matmul(psum[b][:], lhsT=wkv[:, kc, h * 128:(h + 1) * 128], rhs=ckvT[:, kc, blk * 512:(blk + 1) * 512],
                                                                        start=(kc == 0), stop=(kc == 1)), reads=[sname2, "ckvT"], writes=[psn[b]])
            copy_evac(Kt[:, h, blk * 512:(blk + 1) * 512], psum[b][:], [psn[b]], ["K%d" % h])
    for tt in range(16):
        for half in range(2):
            b = next_bank(NB)
            for kc in range(2):
                P.op("pe", lambda e, kc=kc, tt=tt, half=half, b=b: e.matmul(psum[b][:], lhsT=ckvT[:, kc, tt * 128:(tt + 1) * 128],
                                                                            rhs=wkv[:, kc, 1024 + half * 512:1024 + (half + 1) * 512], start=(kc == 0), stop=(kc == 1)),
                     reads=[sname2, "ckvT"], writes=[psn[b]])
            copy_evac(Vt[:, tt, half * 4:half * 4 + 4, 0:128], psum[b][:].rearrange("p (h e) -> p h e", h=4), [psn[b]], ["V%d" % tt])

    flush()
    mla_scale = float(192.0 ** -0.5)
    for hp in range(4):
        for qb in range(2):
            def qk_emit(kt, banks, hp=hp, qb=qb):
                for c in range(2):
                    h = 2 * hp + c
                    P.op("pe", lambda e, c=c, h=h: e.matmul(psum[banks[c]][:], lhsT=Kt[:, h, kt * 128:(kt + 1) * 128], rhs=Qt[:, h, qb * 512:(qb + 1) * 512],
                                                            start=True, stop=False), reads=["K%d" % h, "Q%d" % h], writes=[psn[banks[c]]])
                for c in range(2):
                    P.op("pe", lambda e, c=c: e.matmul(psum[banks[c]][:], lhsT=krT[c * 64:(c + 1) * 64, kt * 128:(kt + 1) * 128],
                                                       rhs=qrT[c * 64:(c + 1) * 64, hp, qb * 512:(qb + 1) * 512], start=False, stop=True),
                         reads=["krT", "qrT"], writes=[psn[banks[c]]])

            def finalize(hp=hp, qb=qb):
                def s1():
                    P.op("dve", lambda e: e.reciprocal(out=rcp[:, 0:8], in_=osb[:, :, 128]), reads=["osb"], writes=["rcp"])
                    for c in range(2):
                        for j in range(4):
                            a = j * 2 + c
                            P.op("dve", lambda e, a=a, j=j, c=c: e.tensor_scalar(out=onb[:, c * 4 + j, :], in0=osb[:, a, 0:128], scalar1=rcp[:, a:a + 1], scalar2=None, op0=ALU.mult),
                                 reads=["osb", "rcp"], shared=["onb"])

                def s2():
                    for c in range(2):
                        for j in range(4):
                            P.op("pe", lambda e, j=j, c=c: e.transpose(out=psum_b[TB][:, c * 4 + j, :], in_=onb[:, c * 4 + j, :], identity=ident_b),
                                 reads=["onb", "bfm"], writes=[psn[TB]])
                    for c in range(2):
                        P.op("dve", lambda e, c=c: e.tensor_copy(out=oT[:, qb, 8 + 2 * hp + c, :], in_=psum_b[TB][:, c * 4:c * 4 + 4, :].rearrange("p a b -> p (a b)")),
                             reads=[psn[TB]], shared=["oT"])
                stage(1, s1)
                stage(5, s2)

            vh[0] = 2 * hp
            vh[1] = 2 * hp + 1
            attention_unit(qk_emit, mla_scale, finalize)
    for kt in range(16):
        run_stages(kt)

    if debug:
        P.barrier()
        P.dma("sp", dbg["O"], oT[:].rearrange("p a b c -> p (a b c)"), reads=["oT"])
        P.dma("sp", dbg["QR"], qrT[:].rearrange("p a b -> p (a b)"), reads=["qrT"])
        P.dma("sp", dbg["mod"], modcols[:], reads=["modcols"])
    if stop_after == "C":
        P.finish("sp")
        return nc, P

    P.barrier()
    P.op("dve", lambda e: e.scalar_tensor_tensor(out=a2[:], in0=modcols[:, 64:80], scalar=1.0, in1=cols[:, 16:32], op0=ALU.add, op1=ALU.mult),
         reads=["modcols", "cols"], writes=["a2"])
    RD = Region(nc, "D", base + 38912, 32768 + 16384 + 33280)
    hid = RD.alloc("hid", [128, 64, 512], BF16)
    gate_bc = RD.alloc("gate_bc", [128, D], F32)
    gfin_bc = RD.alloc("gfin_bc", [128, D], F32)
    RE = Region(nc, "E", base + 121344, 20480 + 32768 + 36864)
    xmid = RE.alloc("xmid", [128, 4, D], F32)
    wsl = [RE.alloc("wd%d" % i, [128, 8192], BF16) for i in range(2)]
    junk = RE.alloc("junkD", [128, D], BF16)
    xnb = [(RE.alloc("xnD", [128, D], BF16), "xnD")]
    yT = [(RE.alloc("yT%d" % i, [128, 512], F32)[:], ["yT%d" % i]) for i in range(2)]
    yT.append((junk.bitcast(F32)[:, 0:512], ["junk"]))
    yT.append((xnb[0][0].bitcast(F32)[:, 0:512], ["xnD"]))
    tmpd_all = RE.alloc("tmpd", [128, 1024], F32)
    tmpd = [(tmpd_all[:, i * 512:(i + 1) * 512], "tmpd%d" % i) for i in range(2)]
    xnD2 = tmpd_all.bitcast(BF16)
    rbf = [(RE.alloc("rbf%d" % i, [128, 512], BF16), "rbf%d" % i) for i in range(2)]
    diag = RE.alloc("diag", [128, 128], F32)
    wd_i = [0]

    def next_wd():
        i = wd_i[0] % 2
        wd_i[0] += 1
        return wsl[i], "wd%d" % i

    P.dma("sp", gfin_bc[:], gfin_d.partition_broadcast(128), writes=["gfin_bc"])
    for kc in range(KC):
        b = next_bank()
        P.op("dve", lambda e, kc=kc: e.tensor_scalar(out=diag[:], in0=ident_f[:], scalar1=modcols[:, 32 + kc:33 + kc], scalar2=None, op0=ALU.mult),
             reads=["ident_f", "modcols"], writes=["diag"])
        P.op("pe", lambda e, b=b: e.matmul(psum[b][:, 0:128], lhsT=ones_f[:], rhs=diag[:], start=True, stop=True), reads=["ones_f", "diag"], writes=[psn[b]])
        P.op("act", lambda e, b=b, kc=kc: e.activation(out=gate_bc[:, kc * 128:(kc + 1) * 128], in_=psum[b][:, 0:128], func=AF.Copy), reads=[psn[b]], writes=["gate_bc"])

    w_out_v = w_out_d.rearrange("(kc p) n -> p kc n", p=128)
    w_ff1_v = w_ff1_d.rearrange("(kc p) n -> p kc n", p=128)
    w_ff2_v = w_ff2_d.rearrange("(fc p) n -> p fc n", p=128)
    h2T = None
    for blk in range(2):
        for tt in range(4):
            r0 = blk * 512 + tt * 128
            P.dma("sp", xmid[:, tt, :], x_d[r0:r0 + 128, :], reads=[], writes=["xmid%d" % tt])
        for cb in range(4):
            slot, sname = next_wd()
            s3 = slot.reshape([128, KC, 512])
            load_w_cols(s3, sname, w_out_v[:, :, cb * 512:(cb + 1) * 512])
            for tt in range(4):
                b = next_bank()
                for kc in range(KC):
                    P.op("pe", lambda e, kc=kc, tt=tt, b=b, s3=s3: e.matmul(psum[b][:], lhsT=oT[:, blk, kc, tt * 128:(tt + 1) * 128], rhs=s3[:, kc, :],
                                                                            start=(kc == 0), stop=(kc == KC - 1)),
                         reads=[sname + "a", sname + "b", "oT"], writes=[psn[b]])
                tm, tmname = tmpd[(cb * 4 + tt) % 2]
                P.op("dve", lambda e, b=b, tm=tm, cb=cb: e.tensor_tensor(out=tm, in0=psum[b][:], in1=gate_bc[:, cb * 512:(cb + 1) * 512], op=ALU.mult),
                     reads=[psn[b], "gate_bc"], writes=[tmname])
                xs = xmid[:, tt, cb * 512:(cb + 1) * 512]
                P.op("dve", lambda e, xs=xs, tm=tm: e.tensor_tensor(out=xs, in0=xs, in1=tm, op=ALU.add), reads=[tmname, "xmid%d" % tt], writes=["xmid%d" % tt])
        h2T = oT[:, blk]
        make_hT([(xmid[:, tt, :], "xmid%d" % tt, None) for tt in range(4)], h2T, 0, a2, modcols[:, 48:64], "a2",
                [(xnb[0][0][:], ["xnD"]), (xnD2[:], ["tmpd0", "tmpd1"])], junk, "oT")
        for t in range(16):
            slot, sname = next_wd()
            s3 = slot.reshape([128, KC, 512])
            load_w_cols(s3, sname, w_ff1_v[:, :, t * 512:(t + 1) * 512])
            for cc in range(4):
                f = t * 4 + cc
                b = next_bank()
                for kc in range(KC):
                    P.op("pe", lambda e, kc=kc, cc=cc, b=b, s3=s3: e.matmul(psum[b][:], lhsT=s3[:, kc, cc * 128:(cc + 1) * 128], rhs=h2T[:, kc, :],
                                                                            start=(kc == 0), stop=(kc == KC - 1)),
                         reads=[wn(sname, cc * 128), "oT"], writes=[psn[b]])
                rb, rbname = rbf[f % 2]
                P.op("act", lambda e, b=b, rb=rb: e.activation(out=rb[:], in_=psum[b][:], func=AF.Relu), reads=[psn[b]], writes=[rbname])
                P.op("dve", lambda e, b=b, rb=rb, f=f: e.scalar_tensor_tensor(out=hid[:, f, :], in0=psum[b][:], scalar=0.0, in1=rb[:], op0=ALU.max, op1=ALU.mult),
                     reads=[psn[b], rbname], shared=["hid"])
        for cbk in range(4):
            banks = [(cbk % 2) * 4 + j for j in range(4)]
            for fg in range(4):
                slot, sname = next_wd()
                s3 = slot.reshape([128, 16, 512])
                src = w_ff2_v[:, fg * 16:(fg + 1) * 16, cbk * 512:(cbk + 1) * 512]
                P.dma("pool", s3[:, 0:8, :], src[:, 0:8, :], writes=[sname + "fa"], shared=[sname + "a", sname + "b"])
                P.dma("pool", s3[:, 8:16, :], src[:, 8:16, :], writes=[sname + "fb"], shared=[sname + "a", sname + "b"])
                for fi in range(16):
                    fc = fg * 16 + fi
                    for j in range(4):
                        P.op("pe", lambda e, fi=fi, fc=fc, j=j, s3=s3: e.matmul(psum[banks[j]][:], lhsT=s3[:, fi, j * 128:(j + 1) * 128], rhs=hid[:, fc, :],
                                                                                 start=(fc == 0), stop=(fc == 63)),
                             reads=[sname + ("fa" if fi < 8 else "fb"), "hid"], writes=[psn[banks[j]]])
                if fg == 0:
                    flush()
            for j in range(4):
                c = cbk * 4 + j
                yt, ytnames = yT[j]
                P.op("act", lambda e, j=j, yt=yt, c=c, banks=banks: e.activation(out=yt, in_=psum[banks[j]][:], func=AF.Identity, scale=modcols[:, 80 + c:81 + c]),
                     reads=[psn[banks[j]], "modcols"], writes=ytnames)

                def later(j=j, c=c, yt=yt, ytnames=ytnames, banks=banks):
                    for tt in range(4):
                        P.op("pe", lambda e, tt=tt: e.transpose(out=psum[banks[j]][:, tt * 128:(tt + 1) * 128], in_=yt[:, tt * 128:(tt + 1) * 128], identity=ident_f[:]),
                             reads=ytnames + ["ident_f"], writes=[psn[banks[j]]])
                    xs = xmid[:, :, c * 128:(c + 1) * 128]
                    P.op("dve", lambda e: e.tensor_tensor(out=xs, in0=xs, in1=psum[banks[j]][:].rearrange("p (t c) -> p t c", t=4), op=ALU.add),
                         reads=[psn[banks[j]]] + ["xmid%d" % tt for tt in range(4)], writes=["xmid%d" % tt for tt in range(4)])
                defer(later)
        flush()
        for tt in range(4):
            c = tt
            P.op("act", lambda e, tt=tt, c=c: e.activation(out=junk[:], in_=xmid[:, tt, :], func=AF.Square, accum_out=ss[:, c:c + 1]), reads=["xmid%d" % tt], writes=["junk", "ss%d" % c])
            P.op("act", lambda e, c=c: e.activation(out=ms[:, c:c + 1], in_=ss[:, c:c + 1], func=AF.Sqrt, bias=epscol, scale=1.0 / D), reads=["ss%d" % c, "epscol"], writes=["ms%d" % c])
            P.op("dve", lambda e, c=c: e.reciprocal(out=rstd[:, c:c + 1], in_=ms[:, c:c + 1]), reads=["ms%d" % c], writes=["rstd%d" % c])
            P.op("dve", lambda e, tt=tt, c=c: e.scalar_tensor_tensor(out=xmid[:, tt, :], in0=xmid[:, tt, :], scalar=rstd[:, c:c + 1], in1=gfin_bc[:], op0=ALU.mult, op1=ALU.mult),
                 reads=["xmid%d" % tt, "rstd%d" % c, "gfin_bc"], writes=["xmid%d" % tt])
            r0 = blk * 512 + tt * 128
            P.dma("sp", y_d[r0:r0 + 128, :], xmid[:, tt, :], reads=["xmid%d" % tt])
    P.finish("sp")
    return nc, P


def _const_mats():
    m = np.zeros((128, 384), np.float32)
    m[:, 0:128] = np.eye(128, dtype=np.float32)
    Rd = np.zeros((128, 128), np.float32)
    for c in range(2):
        for d in range(8):
            mlo = c * 64 + d
            Rd[mlo + 8, mlo] = -1.0
            Rd[mlo, mlo + 8] = 1.0
    Rm = np.zeros((128, 128), np.float32)
    for g in range(2):
        for d in range(32):
            mlo = g * 64 + d
            Rm[mlo + 32, mlo] = -1.0
            Rm[mlo, mlo + 32] = 1.0
    m[:, 128:256] = Rd
    m[:, 256:384] = Rm
    return m


def _inv_cols():
    inv_d = (1.0 / (np.float32(500000.0) ** (np.arange(0, 16, 2, dtype=np.float32) / np.float32(16)))).astype(np.float32)
    inv_m = (1.0 / (np.float32(10000.0) ** (np.arange(0, 64, 2, dtype=np.float32) / np.float32(64)))).astype(np.float32)
    cd = np.zeros(128, np.float32)
    cm = np.zeros(128, np.float32)
    for p in range(128):
        d = p % 64
        if d < 16:
            cd[p] = inv_d[d % 8]
        cm[p] = inv_m[d % 32]
    return cd, cm


_CACHE = {}


def kernel(x, c, positions, w_ada, b_ada, g_norm_mix, w_in, lambda_q1, lambda_k1, lambda_q2, lambda_k2,
           g_diff_sub, g_q_a, w_q_b, g_kv_a, w_kv_b, w_out, g_norm_ffn, w_ff1, w_ff2, g_final, _debug=False, _stop=None):
    f32 = np.float32
    x = np.asarray(x, f32)
    colmaj = lambda v: np.ascontiguousarray(np.asarray(v, f32).reshape(-1, 128).T)
    cd, cm = _inv_cols()
    cols = np.concatenate([colmaj(g_norm_mix[0]), colmaj(g_norm_ffn[0]), colmaj(g_q_a[0]), colmaj(g_kv_a[0]),
                           cd[:, None], cm[:, None], colmaj(b_ada[0])], axis=1).astype(f32)
    assert cols.shape == (128, 136)
    cmat = _const_mats()
    lam = np.stack([lambda_q1[0], lambda_k1[0], lambda_q2[0], lambda_k2[0]]).astype(f32)
    w_in0 = np.asarray(w_in[0], f32)
    w_in_ext = np.ascontiguousarray(np.concatenate([w_in0, w_in0[:, 3840:3904]], axis=1))
    wq = np.asarray(w_q_b[0], f32).reshape(512, 8, 192)
    w_qb = np.ascontiguousarray(np.concatenate([wq[:, :, :128].reshape(512, 1024), wq[:, :, 128:].reshape(512, 512)], axis=1))
    wkv = np.asarray(w_kv_b[0], f32).reshape(256, 8, 256)
    w_kvb = np.ascontiguousarray(np.concatenate([wkv[:, :, :128].reshape(256, 1024), wkv[:, :, 128:].reshape(256, 1024)], axis=1))
    shared = {
        "cols": cols, "cmat": cmat, "lam": lam, "gdiff": np.asarray(g_diff_sub, f32).reshape(1, 128),
        "gfin": np.asarray(g_final, f32).reshape(1, D), "w_ada": np.ascontiguousarray(w_ada[0], dtype=f32), "w_in": w_in_ext,
        "w_qb": w_qb, "w_kvb": w_kvb, "w_out": np.ascontiguousarray(w_out[0], dtype=f32),
        "w_ff1": np.ascontiguousarray(w_ff1[0], dtype=f32), "w_ff2": np.ascontiguousarray(w_ff2[0], dtype=f32),
    }
    in_maps = []
    for core in range(8):
        b, hf = core // 2, core % 2
        own = slice(hf * NOWN, (hf + 1) * NOWN)
        oth = slice((1 - hf) * NOWN, (2 - hf) * NOWN)
        m = dict(shared)
        m["x"] = np.ascontiguousarray(np.concatenate([x[b, own], x[b, oth]], axis=0))
        m["pos"] = np.ascontiguousarray(np.concatenate([positions[b, own], positions[b, oth]]).astype(np.int32).reshape(1, NTOK))
        m["c"] = colmaj(c[b])
        in_maps.append(m)
    key = (_debug, _stop)
    if key not in _CACHE:
        _CACHE[key] = build(debug=_debug, stop_after=_stop)
    nc, P = _CACHE[key]
    res = run_bass_kernel_spmd(nc, in_maps, core_ids=list(range(8)))
    if _debug:
        return res
    out = np.empty((4, 2048, D), f32)
    for core in range(8):
        b, hf = core // 2, core % 2
        out[b, hf * NOWN:(hf + 1) * NOWN] = res.results[core]["y"]
    return out
```

```python
import numpy as np
import concourse.bass as bass
import concourse.mybir as mybir
from concourse.bass_utils import run_bass_kernel_spmd

F32 = mybir.dt.float32
BF16 = mybir.dt.bfloat16
I32 = mybir.dt.int32
AF = mybir.ActivationFunctionType
ALU = mybir.AluOpType
AX = mybir.AxisListType

D = 2048
KC = 16
NTOK = 2048
NOWN = 1024
EPS = 1e-6
W_IN_COLS = 3968
TWO_PI = float(2.0 * np.pi)
C1 = 6.28125
C2 = float(2.0 * np.pi - 6.28125)
PI_LO = 3.1415925


class Res:
    __slots__ = ("name", "w", "ws", "r")

    def __init__(self, name):
        self.name = name
        self.w = None
        self.ws = {}
        self.r = {}


class Eng:
    def __init__(self, name, handle, sem, is_pe=False):
        self.name = name
        self.h = handle
        self.sem = sem
        self.count = 0
        self.waited = {}
        self.is_pe = is_pe
        self.nwaits = 0
        self.nins = 0


class Prog:
    def __init__(self, nc, n_dma_sems=24):
        self.nc = nc
        self.sems = {}
        self.engs = {}
        for nm, h in (("pe", nc.tensor), ("act", nc.scalar), ("dve", nc.vector), ("pool", nc.gpsimd), ("sp", nc.sync)):
            s = nc.alloc_semaphore("s_" + nm)
            self.sems[nm] = s
            self.engs[nm] = Eng(nm, h, s, is_pe=(nm == "pe"))
        self.dma_sems = []
        self.dma_pools = {"sp": [], "pool": []}
        for q, n in (("sp", n_dma_sems), ("pool", 8)):
            for i in range(n):
                k = "d%s%d" % (q, i)
                self.sems[k] = nc.alloc_semaphore("s_" + k)
                ent = [k, 0]
                self.dma_sems.append(ent)
                self.dma_pools[q].append(ent)
        self.dma_i = {"sp": 0, "pool": 0}
        self.res = {}

    def R(self, name):
        r = self.res.get(name)
        if r is None:
            r = Res(name)
            self.res[name] = r
        return r

    def _deps(self, reads, writes, shared=(), me=None):
        deps = {}

        def add(k, v):
            if deps.get(k, 0) < v:
                deps[k] = v
        for r in reads:
            r = self.R(r)
            if r.w is not None:
                add(*r.w)
            for k, v in r.ws.items():
                add(k, v)
            if r.name.startswith("ps") and me is not None:
                for k, v in r.r.items():
                    if k != me:
                        add(k, v)
        for w in writes:
            w = self.R(w)
            if w.w is not None:
                add(*w.w)
            for k, v in w.ws.items():
                add(k, v)
            for k, v in w.r.items():
                add(k, v)
        for w in shared:
            w = self.R(w)
            if w.w is not None:
                add(*w.w)
            for k, v in w.r.items():
                add(k, v)
        return deps

    def _wait(self, e, deps):
        for k, v in deps.items():
            if e.is_pe and k == "pe":
                continue
            if e.waited.get(k, 0) >= v:
                continue
            e.h.wait_ge(self.sems[k], v)
            e.waited[k] = v
            e.nwaits += 1

    def _mark(self, reads, writes, key, val, shared=()):
        for r in reads:
            r = self.R(r)
            if r.r.get(key, 0) < val:
                r.r[key] = val
        for w in writes:
            w = self.R(w)
            w.w = (key, val)
            w.ws = {}
            w.r = {}
        for w in shared:
            w = self.R(w)
            if w.ws.get(key, 0) < val:
                w.ws[key] = val

    def op(self, eng, fn, reads=(), writes=(), shared=()):
        e = self.engs[eng]
        self._wait(e, self._deps(reads, writes, shared, me=eng))
        ins = fn(e.h)
        e.count += 1
        e.nins += 1
        ins.then_inc(e.sem, 1)
        self._mark(reads, writes, eng, e.count, shared)
        return ins

    def dma(self, queue, out, in_, reads=(), writes=(), shared=()):
        e = self.engs[queue]
        pool = self.dma_pools[queue]
        slot = pool[self.dma_i[queue] % len(pool)]
        self.dma_i[queue] += 1
        k, uses = slot
        deps = self._deps(reads, writes, shared)
        if uses > 0:
            deps[k] = max(deps.get(k, 0), 16 * uses)
        self._wait(e, deps)
        e.h.dma_start(out=out, in_=in_).then_inc(self.sems[k], 16)
        slot[1] = uses + 1
        e.nins += 1
        self._mark(reads, writes, k, 16 * (uses + 1), shared)

    def barrier(self, engs=("pe", "act", "dve", "pool", "sp")):
        for nm in engs:
            e = self.engs[nm]
            deps = {}
            for o in ("pe", "act", "dve", "pool"):
                if o != nm and self.engs[o].count > 0:
                    deps[o] = self.engs[o].count
            for k, uses in self.dma_sems:
                if uses > 0:
                    deps[k] = 16 * uses
            self._wait(e, deps)

    def finish(self, eng="sp"):
        e = self.engs[eng]
        deps = {}
        for k, uses in self.dma_sems:
            if uses > 0:
                deps[k] = 16 * uses
        for nm, en in self.engs.items():
            if nm != eng and en.count > 0:
                deps[nm] = en.count
        self._wait(e, deps)


class Region:
    def __init__(self, nc, name, base, size):
        self.nc, self.name, self.base, self.size, self.off = nc, name, base, size, 0
        self.n = 0

    def reset(self):
        self.off = 0

    def alloc(self, name, shape, dtype):
        esz = 2 if dtype == BF16 else 4
        nbytes = esz
        for s in shape[1:]:
            nbytes *= s
        off = (self.off + 31) // 32 * 32
        assert off + nbytes <= self.size, (self.name, name, off, nbytes, self.size)
        self.off = off + nbytes
        self.n += 1
        return self.nc.alloc_sbuf_tensor_at("%s_%s_%d" % (self.name, name, self.n), list(shape), dtype, offset=self.base + off)


def build(debug=False, stop_after=None):
    nc = bass.Bass("TRN2", target_bir_lowering=False)

    def dram(name, shape, dt, out=False):
        return nc.dram_tensor(name, list(shape), dt, kind="ExternalOutput" if out else "ExternalInput").ap()

    x_d = dram("x", [NTOK, D], F32)
    pos_d = dram("pos", [1, NTOK], I32)
    c_d = dram("c", [128, KC], F32)
    cols_d = dram("cols", [128, 136], F32)
    cmat_d = dram("cmat", [128, 384], F32)
    lam_d = dram("lam", [4, 64], F32)
    gdiff_d = dram("gdiff", [1, 128], F32)
    gfin_d = dram("gfin", [1, D], F32)
    w_ada_d = dram("w_ada", [D, 6 * D], F32)
    w_in_d = dram("w_in", [D, W_IN_COLS], F32)
    w_qb_d = dram("w_qb", [512, 1536], F32)
    w_kvb_d = dram("w_kvb", [256, 2048], F32)
    w_out_d = dram("w_out", [D, D], F32)
    w_ff1_d = dram("w_ff1", [D, 4 * D], F32)
    w_ff2_d = dram("w_ff2", [4 * D, D], F32)
    y_d = dram("y", [NOWN, D], F32, out=True)
    dbg = {}
    if debug:
        dbg["hT"] = dram("dbg_hT", [128, KC * 1024], BF16, out=True)
        dbg["K"] = dram("dbg_K", [128, 8 * 2048], BF16, out=True)
        dbg["Q"] = dram("dbg_Q", [128, 8 * 1024], BF16, out=True)
        dbg["V"] = dram("dbg_V", [128, 16 * 8 * 130], BF16, out=True)
        dbg["L"] = dram("dbg_L", [128, 10240], BF16, out=True)
        dbg["O"] = dram("dbg_O", [128, 2 * KC * 512], BF16, out=True)
        dbg["QR"] = dram("dbg_QR", [128, 4 * 1024], BF16, out=True)
        dbg["mod"] = dram("dbg_mod", [128, 96], F32, out=True)

    P = Prog(nc)
    base = (nc.sbuf_base + 63) // 64 * 64
    RP = Region(nc, "P", base + 0, 6144)
    RO = Region(nc, "O", base + 6144, 32768)
    RK = Region(nc, "K", base + 38912, 32768)
    RQ = Region(nc, "Q", base + 71680, 16384)
    RV = Region(nc, "V", base + 88064, 33280)
    RL = Region(nc, "L", base + 121344, 20480)
    RW = Region(nc, "W", base + 141824, 32768)
    RX = Region(nc, "X", base + 174592, 36864)
    assert base + 174592 + 36864 <= nc.sbuf_top, (base, nc.sbuf_top)

    bfm = RP.alloc("bfm", [128, 384], BF16)
    ident_b, Rd_b, Rm_b = bfm[:, 0:128], bfm[:, 128:256], bfm[:, 256:384]
    ident_f = RP.alloc("ident_f", [128, 128], F32)
    ones_f = RP.alloc("ones_f", [128, 128], F32)
    modcols = RP.alloc("modcols", [128, 96], F32)
    cols = RP.alloc("cols", [128, 136], F32)
    a1 = RP.alloc("a1", [128, KC], F32)
    a2 = RP.alloc("a2", [128, KC], F32)
    cT = RP.alloc("cT", [128, KC], F32)
    silu_b = RP.alloc("silu_b", [128, KC], BF16)
    g08_bc = RP.alloc("g08", [128, 128], F32)
    lamt = RP.alloc("lamt", [128, 4, 64], F32)
    lc = RP.alloc("lc", [128, 8], F32)
    ss = RP.alloc("ss", [128, 16], F32)
    ms = RP.alloc("ms", [128, 16], F32)
    rstd = RP.alloc("rstd", [128, 16], F32)
    neghalf = RP.alloc("neghalf", [128, 16], F32)
    rcp = RP.alloc("rcp", [128, 16], F32)

    psum_all = nc.alloc_psum_tensor("psall", [128, 4096], F32)
    psum_all_b = psum_all.bitcast(BF16).reshape([128, 64, 128])
    psum = [psum_all[:, i * 512:(i + 1) * 512] for i in range(8)]
    psum_b = [psum_all_b[:, i * 8:(i + 1) * 8, :] for i in range(8)]
    psn = ["ps%d" % i for i in range(8)]
    bank_rr = [0]

    def next_bank(allowed=(0, 1, 2, 3, 4, 5, 6, 7)):
        b = allowed[bank_rr[0] % len(allowed)]
        bank_rr[0] += 1
        return b

    Kt = RK.alloc("K", [128, 8, NTOK], BF16)
    Qt = RQ.alloc("Q", [128, 8, NOWN], BF16)
    Vt = RV.alloc("V", [128, 16, 8, 130], BF16)
    cqT = RL.alloc("cqT", [128, 4, NOWN], BF16)
    ckvT = RL.alloc("ckvT", [128, 2, NTOK], BF16)
    krT = RL.alloc("krT", [128, NTOK], BF16)
    wslots = [RW.alloc("w%d" % i, [128, 8192], BF16) for i in range(2)]
    wslot_i = [0]

    def next_wslot():
        i = wslot_i[0] % 2
        wslot_i[0] += 1
        return wslots[i], "wslot%d" % i

    pend = []

    def defer(fn):
        pend.append(fn)

    def flush(keep=0):
        while len(pend) > keep:
            pend.pop(0)()

    def load_w_cols(s3, sname, src, ncols=512):
        h = 256
        P.dma("pool", s3[:, :, 0:h], src[:, :, 0:h], writes=[sname + "a"], shared=[sname + "fa", sname + "fb"])
        P.dma("pool", s3[:, :, h:ncols], src[:, :, h:ncols], writes=[sname + "b"], shared=[sname + "fa", sname + "fb"])

    def wn(sname, c0):
        return sname + ("a" if c0 < 256 else "b")

    RX.reset()
    cstage = nc.alloc_sbuf_tensor_at("K_cstage", [128, 384], F32, offset=RK.base)
    P.dma("sp", cstage[:], cmat_d, writes=["cstage"])
    P.dma("sp", cols[:], cols_d, writes=["cols"])
    P.dma("sp", cT[:], c_d, writes=["cT"])
    for i in range(4):
        P.dma("sp", lamt[:, i, :], lam_d[i:i + 1, :].partition_broadcast(128), writes=["lamt"])
    P.dma("sp", g08_bc[:], gdiff_d.partition_broadcast(128), writes=["g08"])
    P.op("dve", lambda e: e.tensor_copy(out=bfm[:], in_=cstage[:]), reads=["cstage"], writes=["bfm"])
    P.op("dve", lambda e: e.tensor_copy(out=ident_f[:], in_=cstage[:, 0:128]), reads=["cstage"], writes=["ident_f"])
    P.op("pool", lambda e: e.memset(ones_f[:], 1.0), writes=["ones_f"])
    P.op("pool", lambda e: e.memset(neghalf[:], -0.5), writes=["neghalf"])
    P.op("pool", lambda e: e.memset(modcols[:], 0.0), writes=["modcols"])
    P.op("pool", lambda e: e.memset(Vt[:, :, :, 128:130], 1.0), writes=["Vones"])
    P.op("act", lambda e: e.activation(out=silu_b[:], in_=cT[:], func=AF.Silu), reads=["cT"], writes=["silu_b"])
    P.op("dve", lambda e: e.tensor_tensor(out=lamt[:, 0, :], in0=lamt[:, 0, :], in1=lamt[:, 1, :], op=ALU.mult), reads=["lamt"], writes=["lamt"])
    P.op("dve", lambda e: e.tensor_tensor(out=lamt[:, 2, :], in0=lamt[:, 2, :], in1=lamt[:, 3, :], op=ALU.mult), reads=["lamt"], writes=["lamt"])
    P.op("dve", lambda e: e.reduce_sum(out=lc[:, 0:1], in_=lamt[:, 0, :], axis=AX.X), reads=["lamt"], writes=["lc"])
    P.op("dve", lambda e: e.reduce_sum(out=lc[:, 1:2], in_=lamt[:, 2, :], axis=AX.X), reads=["lamt"], writes=["lc"])
    P.op("act", lambda e: e.activation(out=lc[:, 2:4], in_=lc[:, 0:2], func=AF.Exp), reads=["lc"], writes=["lc"])
    P.op("dve", lambda e: e.tensor_tensor(out=lc[:, 4:5], in0=lc[:, 2:3], in1=lc[:, 3:4], op=ALU.subtract), reads=["lc"], writes=["lc"])
    P.op("dve", lambda e: e.tensor_scalar(out=lc[:, 5:6], in0=lc[:, 4:5], scalar1=-1.0, scalar2=-0.2, op0=ALU.mult, op1=ALU.add), reads=["lc"], writes=["lc"])
    neglam = lc[:, 5:6]
    P.op("dve", lambda e: e.tensor_scalar(out=g08_bc[:], in0=g08_bc[:], scalar1=0.8, scalar2=None, op0=ALU.mult), reads=["g08"], writes=["g08"])

    w_ada_v = w_ada_d.rearrange("(kc p) n -> p kc n", p=128)
    w_in_v = w_in_d.rearrange("(kc p) n -> p kc n", p=128)

    ada_slots = {}

    def ada_load(t):
        slot, sname = next_wslot()
        s3 = slot.reshape([128, KC, 512])
        load_w_cols(s3, sname, w_ada_v[:, :, t * 512:(t + 1) * 512])
        ada_slots[t] = (s3, sname)

    def ada_compute(t, bank):
        s3, sname = ada_slots.pop(t)
        for cc in range(4):
            for kc in range(KC):
                P.op("pe", lambda e, cc=cc, kc=kc: e.matmul(psum[bank][:, cc:cc + 1], lhsT=s3[:, kc, cc * 128:(cc + 1) * 128],
                                                            rhs=silu_b[:, kc:kc + 1], start=(kc == 0), stop=(kc == KC - 1)),
                     reads=[wn(sname, cc * 128), "silu_b"], writes=[psn[bank]])
        P.op("dve", lambda e: e.tensor_tensor(out=modcols[:, t * 4:(t + 1) * 4], in0=psum[bank][:, 0:4],
                                              in1=cols[:, 40 + t * 4:40 + (t + 1) * 4], op=ALU.add),
             reads=[psn[bank], "cols"], shared=["modcols"])

    def ada_row_x(t, bank, row):
        s3, sname = ada_slots[t]
        for kc in range(KC):
            P.op("pe", lambda e, kc=kc: e.matmul(psum[bank][0:1, :], lhsT=silu_b[:, kc:kc + 1], rhs=s3[:, kc, :], start=(kc == 0), stop=(kc == KC - 1)),
                 reads=[sname + "a", sname + "b", "silu_b"], writes=[psn[bank]])
        P.op("dve", lambda e: e.tensor_copy(out=row, in_=psum[bank][0:1, :]), reads=[psn[bank]], writes=["ob0", "ob1", "ob2", "ob3"])

    def ada_row_y(t, bank, row):
        ada_slots.pop(t)
        for cc in range(4):
            P.op("pe", lambda e, cc=cc: e.matmul(psum[bank][:, cc:cc + 1], lhsT=row[0:1, cc * 128:(cc + 1) * 128], rhs=ones_f[0:1, 0:1], start=True, stop=True),
                 reads=["ob0", "ob1", "ob2", "ob3", "ones_f"], writes=[psn[bank]])
        P.op("dve", lambda e: e.tensor_tensor(out=modcols[:, t * 4:(t + 1) * 4], in0=psum[bank][:, 0:4],
                                              in1=cols[:, 40 + t * 4:40 + (t + 1) * 4], op=ALU.add),
             reads=[psn[bank], "cols"], shared=["modcols"])

    ada_load(0)

    preloaded = {}

    def inproj_load(t):
        slot, sname = next_wslot()
        ncols = 512 if t < 7 else 384
        s3 = slot.reshape([128, KC, 512])
        load_w_cols(s3, sname, w_in_v[:, :, t * 512:t * 512 + ncols], ncols)
        return s3, sname

    def ada_hook(i):
        if i + 1 < 8:
            ada_load(i + 1)
        else:
            preloaded["other"] = inproj_load(4)
        ada_compute(i, next_bank())

    def rsqrt_cols(n, dim, c0=8):
        P.op("dve", lambda e: e.tensor_scalar(out=ms[:, c0:c0 + n], in0=ss[:, c0:c0 + n], scalar1=1.0 / dim, scalar2=EPS, op0=ALU.mult, op1=ALU.add),
             reads=["ssA"], writes=["msA"])
        P.op("pool", lambda e: e.tensor_tensor(out=rstd[:, c0:c0 + n], in0=ms[:, c0:c0 + n], in1=neghalf[:, 0:n], op=ALU.pow),
             reads=["msA", "neghalf"], writes=["rstdA"])

    hT_ctr = [0]

    def make_hT(src_tiles, dst, dst_tok0, acols, shcols, acname, xn_bufs, junk, dst_name, plain=False, hook=None):
        cols_of = {}

        def stage_a(i):
            xt, xname, loader = src_tiles[i]
            if loader is not None:
                loader()
            c = hT_ctr[0] % 8
            hT_ctr[0] += 1
            cols_of[i] = c
            sc, mc, rc = "ss%d" % c, "ms%d" % c, "rstd%d" % c
            P.op("act", lambda e: e.activation(out=junk[:], in_=xt, func=AF.Square, accum_out=ss[:, c:c + 1]), reads=[xname], writes=["junk", sc])
            P.op("act", lambda e: e.activation(out=ms[:, c:c + 1], in_=ss[:, c:c + 1], func=AF.Sqrt, bias=epscol, scale=1.0 / D),
                 reads=[sc, "epscol"], writes=[mc])
            P.op("dve", lambda e: e.reciprocal(out=rstd[:, c:c + 1], in_=ms[:, c:c + 1]), reads=[mc], writes=[rc])
            xn, xnnames = xn_bufs[i % len(xn_bufs)]
            P.op("dve", lambda e: e.tensor_scalar(out=xn, in0=xt, scalar1=rstd[:, c:c + 1], scalar2=None, op0=ALU.mult),
                 reads=[xname, rc], writes=xnnames)

        def stage_b(i):
            xn, xnnames = xn_bufs[i % len(xn_bufs)]
            for g in range(2):
                b = next_bank()
                for j in range(8):
                    kc = g * 8 + j
                    P.op("pe", lambda e, j=j, kc=kc: e.transpose(out=psum_b[b][:, j, :], in_=xn[:, kc * 128:(kc + 1) * 128], identity=ident_b),
                         reads=xnnames + ["bfm"], writes=[psn[b]])
                for j in range(8):
                    kc = g * 8 + j
                    o = dst[:, kc, dst_tok0 + i * 128:dst_tok0 + (i + 1) * 128]
                    if plain:
                        if g == 0:
                            P.op("dve", lambda e, j=j, o=o: e.tensor_copy(out=o, in_=psum_b[b][:, j, :]), reads=[psn[b]], shared=[dst_name])
                        else:
                            P.op("act", lambda e, j=j, o=o: e.activation(out=o, in_=psum_b[b][:, j, :], func=AF.Copy), reads=[psn[b]], shared=[dst_name])
                    elif g == 0:
                        P.op("dve", lambda e, j=j, kc=kc, o=o: e.tensor_scalar(out=o, in0=psum_b[b][:, j, :], scalar1=acols[:, kc:kc + 1],
                                                                               scalar2=shcols[:, kc:kc + 1], op0=ALU.mult, op1=ALU.add),
                             reads=[psn[b], acname, "modcols"], shared=[dst_name])
                    else:
                        P.op("act", lambda e, j=j, kc=kc, o=o: e.activation(out=o, in_=psum_b[b][:, j, :], func=AF.Identity,
                                                                            bias=shcols[:, kc:kc + 1], scale=acols[:, kc:kc + 1]),
                             reads=[psn[b], acname, "modcols"], shared=[dst_name])

        n = len(src_tiles)
        stage_a(0)
        for i in range(n):
            if i + 1 < n:
                stage_a(i + 1)
            if hook is not None:
                hook(i)
            stage_b(i)

    def rope_tables(tok0, blk, invcol, ctab, stab, posi, posf, t1, t2, t3, tabname, ki=None, sin_later=False):
        t0 = tok0 + blk * 512
        P.dma("sp", posi[:], pos_d[0:1, t0:t0 + 512].partition_broadcast(128), writes=["posi"])
        P.op("dve", lambda e: e.tensor_copy(out=posf[:], in_=posi[:]), reads=["posi"], writes=["posf"])
        P.op("dve", lambda e: e.tensor_scalar(out=t1[:], in0=posf[:], scalar1=invcol, scalar2=None, op0=ALU.mult), reads=["posf", "cols"], writes=["t1"])
        P.op("dve", lambda e: e.tensor_scalar(out=t2[:], in0=t1[:], scalar1=1.0 / TWO_PI, scalar2=None, op0=ALU.mult), reads=["t1"], writes=["t2"])
        kt_ = ki if ki is not None else posi
        kn_ = "ki" if ki is not None else "posi"
        P.op("dve", lambda e: e.tensor_copy(out=kt_[:], in_=t2[:]), reads=["t2"], writes=[kn_])
        P.op("dve", lambda e: e.tensor_copy(out=t2[:], in_=kt_[:]), reads=[kn_], writes=["t2"])
        P.op("dve", lambda e: e.scalar_tensor_tensor(out=t1[:], in0=t2[:], scalar=-C1, in1=t1[:], op0=ALU.mult, op1=ALU.add), reads=["t1", "t2"], writes=["t1"])
        P.op("dve", lambda e: e.scalar_tensor_tensor(out=t1[:], in0=t2[:], scalar=-C2, in1=t1[:], op0=ALU.mult, op1=ALU.add), reads=["t1", "t2"], writes=["t1"])
        for which, tab in ((0, stab), (1, ctab)):
            if which == 1:
                P.op("dve", lambda e: e.tensor_scalar(out=t1[:], in0=t1[:], scalar1=float(np.pi / 2), scalar2=None, op0=ALU.add), reads=["t1"], writes=["t1"])
            P.op("dve", lambda e: e.tensor_scalar(out=t3[:], in0=t1[:], scalar1=float(np.pi), scalar2=None, op0=ALU.is_gt), reads=["t1"], writes=["t3", "posf"])
            P.op("dve", lambda e: e.scalar_tensor_tensor(out=t2[:], in0=t3[:], scalar=-TWO_PI, in1=t1[:], op0=ALU.mult, op1=ALU.add), reads=["t1", "t3", "posf"], writes=["t2"])
            P.op("dve", lambda e: e.tensor_scalar(out=t3[:], in0=t2[:], scalar1=float(-np.pi), scalar2=None, op0=ALU.is_lt), reads=["t2"], writes=["t3", "posf"])
            P.op("dve", lambda e: e.scalar_tensor_tensor(out=t2[:], in0=t3[:], scalar=TWO_PI, in1=t2[:], op0=ALU.mult, op1=ALU.add), reads=["t2", "t3", "posf"], writes=["t2"])
            P.op("dve", lambda e, tab=tab: e.tensor_scalar(out=tab[:], in0=t2[:], scalar1=PI_LO, scalar2=-PI_LO, op0=ALU.min, op1=ALU.max), reads=["t2"], writes=[tabname + ("c" if which else "s"), tabname])

        def sin_part():
            for which, tab in ((0, stab), (1, ctab)):
                nm = tabname + ("c" if which else "s")
                P.op("act", lambda e, tab=tab: e.activation(out=tab[:], in_=tab[:], func=AF.Sin), reads=[nm], writes=[nm, tabname])
        if sin_later:
            return sin_part
        sin_part()
        return None

    def rope_evac(b, dst, dst_name, ctab, stab, tabname, Rmat, abufs, banks=None):
        ab, abname = abufs[0]
        abufs.append(abufs.pop(0))
        P.op("dve", lambda e: e.tensor_tensor(out=ab[:, 0, :], in0=psum[b][:], in1=ctab[:], op=ALU.mult), reads=[psn[b], tabname], writes=[abname + "c"])
        P.op("dve", lambda e: e.tensor_tensor(out=ab[:, 1, :], in0=psum[b][:], in1=stab[:], op=ALU.mult), reads=[psn[b], tabname], writes=[abname + "s"])

        def later():
            b2 = next_bank(banks) if banks is not None else next_bank()
            P.op("pe", lambda e: e.matmul(psum[b2][:], lhsT=ident_b, rhs=ab[:, 0, :], start=True, stop=False), reads=[abname + "c", "bfm"], writes=[psn[b2]])
            P.op("pe", lambda e: e.matmul(psum[b2][:], lhsT=Rmat, rhs=ab[:, 1, :], start=False, stop=True), reads=[abname + "s", "bfm"], writes=[psn[b2]])
            P.op("act", lambda e: e.activation(out=dst, in_=psum[b2][:], func=AF.Copy), reads=[psn[b2]], shared=[dst_name])
        defer(later)

    evac_alt = [0]

    def copy_evac(out, in_, reads, writes, shared=(), eng=None):
        evac_alt[0] += 1
        if eng is None:
            eng = "act" if evac_alt[0] % 2 == 0 else "dve"
        if eng == "act":
            P.op("act", lambda e: e.activation(out=out, in_=in_, func=AF.Copy), reads=reads, writes=writes, shared=shared)
        else:
            P.op("dve", lambda e: e.tensor_copy(out=out, in_=in_), reads=reads, writes=writes, shared=shared)

    def latent_norm(nch, b_list, latf, sq, rbc, gcol0, dst, tok_sl, dst_name, dim):
        for cc, b in enumerate(b_list):
            P.op("act", lambda e, cc=cc, b=b: e.activation(out=latf[:, cc, :], in_=psum[b][:], func=AF.Copy), reads=[psn[b]], writes=["latf"])
        bs = next_bank()
        for cc in range(nch):
            P.op("dve", lambda e, cc=cc: e.tensor_tensor(out=sq[:], in0=latf[:, cc, :], in1=latf[:, cc, :], op=ALU.mult), reads=["latf"], writes=["sq"])
            P.op("pe", lambda e, cc=cc: e.matmul(psum[bs][:], lhsT=ones_f[:], rhs=sq[:], start=(cc == 0), stop=(cc == nch - 1)),
                 reads=["sq", "ones_f"], writes=[psn[bs]])
        P.op("act", lambda e: e.activation(out=rbc[:], in_=psum[bs][:], func=AF.Sqrt, bias=epscol, scale=1.0 / dim), reads=[psn[bs], "epscol"], writes=["rbc"])
        P.op("dve", lambda e: e.reciprocal(out=rbc[:], in_=rbc[:]), reads=["rbc"], writes=["rbc"])
        for cc in range(nch):
            P.op("dve", lambda e, cc=cc: e.scalar_tensor_tensor(out=dst[:, cc, tok_sl], in0=latf[:, cc, :], scalar=cols[:, gcol0 + cc:gcol0 + cc + 1],
                                                                in1=rbc[:], op0=ALU.mult, op1=ALU.mult),
                 reads=["latf", "rbc", "cols"], writes=[dst_name])

    epscol_t = RP.alloc("epscol", [128, 1], F32)
    epscol = epscol_t[:, 0:1]
    P.op("pool", lambda e: e.memset(epscol_t[:], EPS), writes=["epscol"])

    hT = RO.alloc("hT", [128, KC, 1024], BF16)

    def proj_fm(s3, sname, c0, blk, b):
        for kc in range(KC):
            P.op("pe", lambda e, kc=kc: e.matmul(psum[b][:], lhsT=s3[:, kc, c0:c0 + 128], rhs=hT[:, kc, blk * 512:(blk + 1) * 512],
                                                 start=(kc == 0), stop=(kc == KC - 1)),
                 reads=[wn(sname, c0), "hT"], writes=[psn[b]])
        flush()

    for pas, tok0 in (("other", NOWN), ("own", 0)):
        RX.reset()
        xst = [(RX.alloc("xst%d" % i, [128, D], F32), "xst%d" % i) for i in range(2)]
        xnb = [(RX.alloc("xn%d" % i, [128, D], BF16), "xn%d" % i) for i in range(2)]
        junk = RX.alloc("junk", [128, D], BF16)
        if pas == "own":
            preloaded["own"] = inproj_load(4)
            P.barrier()
        src = []
        for i in range(8):
            xt, xname = xst[i % 2]

            def loader(i=i, xt=xt, xname=xname):
                P.dma("sp", xt[:], x_d[tok0 + i * 128:tok0 + (i + 1) * 128, :], writes=[xname])
            src.append((xt[:], xname, loader))
        if pas == "other":
            make_hT(src, hT, 0, a1, modcols[:, 0:16], "a1", [(xb[:], [xbn]) for xb, xbn in xnb], junk, "hT", plain=True, hook=ada_hook)
            P.op("dve", lambda e: e.scalar_tensor_tensor(out=a1[:], in0=modcols[:, 16:32], scalar=1.0, in1=cols[:, 0:16], op0=ALU.add, op1=ALU.mult),
                 reads=["modcols", "cols"], writes=["a1"])
            for kc in range(KC):
                P.op("dve", lambda e, kc=kc: e.tensor_scalar(out=hT[:, kc, :], in0=hT[:, kc, :], scalar1=a1[:, kc:kc + 1], scalar2=modcols[:, kc:kc + 1],
                                                             op0=ALU.mult, op1=ALU.add), reads=["hT", "a1", "modcols"], writes=["hT"])
        else:
            make_hT(src, hT, 0, a1, modcols[:, 0:16], "a1", [(xb[:], [xbn]) for xb, xbn in xnb], junk, "hT")
        P.barrier()
        RX.reset()
        ctab = [RX.alloc("ctab%d" % i, [128, 512], F32) for i in range(2)]
        stab = [RX.alloc("stab%d" % i, [128, 512], F32) for i in range(2)]
        posi = RX.alloc("posi", [128, 512], I32)
        posf = RX.alloc("posf", [128, 512], F32)
        t1 = RX.alloc("t1", [128, 512], F32)
        t2 = RX.alloc("t2", [128, 512], F32)
        t3 = posf
        kiA = RX.alloc("kiA", [128, 512], I32)
        abufs = [(RX.alloc("ab%d" % i, [128, 2, 512], BF16), "ab%d" % i) for i in range(2)]
        latf = RX.alloc("latf", [128, 4, 512], F32)
        sq = RX.alloc("sq", [128, 512], F32)
        rbc = RX.alloc("rbc", [128, 512], F32)
        tiles = [4, 7, 5, 6, 0, 1, 2, 3] if pas == "own" else [4, 7, 5, 2, 3]
        for blk in range(2):
            rope_tables(tok0, blk, cols[:, 39:40], ctab[blk], stab[blk], posi, posf, t1, t2, t3, "tab%d" % blk, ki=kiA)
        for ti, t in enumerate(tiles):
            if ti == 0:
                assert t == 4
                s3, sname = preloaded.pop(pas)
            else:
                s3, sname = inproj_load(t)
            if t == 7:
                for blk in range(2):
                    bl = []
                    for cc in range(2):
                        b = next_bank()
                        proj_fm(s3, sname, cc * 128, blk, b)
                        bl.append(b)
                    latent_norm(2, bl, latf, sq, rbc, 36, ckvT, slice(tok0 + blk * 512, tok0 + (blk + 1) * 512), "ckvT", 256)
                    b = next_bank()
                    proj_fm(s3, sname, 256, blk, b)
                    rope_evac(b, krT[:, tok0 + blk * 512:tok0 + (blk + 1) * 512], "krT", ctab[blk], stab[blk], "tab%d" % blk, Rm_b, abufs)
                for blk in range(2):
                    rope_tables(tok0, blk, cols[:, 38:39], ctab[blk], stab[blk], posi, posf, t1, t2, t3, "tab%d" % blk, ki=kiA)
            elif t == 6:
                for blk in range(2):
                    bl = []
                    for cc in range(4):
                        b = next_bank()
                        proj_fm(s3, sname, cc * 128, blk, b)
                        bl.append(b)
                    latent_norm(4, bl, latf, sq, rbc, 32, cqT, slice(blk * 512, (blk + 1) * 512), "cqT", 512)
            elif t < 4:
                for cc in range(4):
                    h = (t % 2) * 4 + cc
                    for blk in range(2):
                        b = next_bank()
                        proj_fm(s3, sname, cc * 128, blk, b)
                        if t < 2:
                            dst, dname = Qt[:, h, blk * 512:(blk + 1) * 512], "Q%d" % h
                        else:
                            dst, dname = Kt[:, h, tok0 + blk * 512:tok0 + (blk + 1) * 512], "K%d" % h
                        rope_evac(b, dst, dname, ctab[blk], stab[blk], "tab%d" % blk, Rd_b, abufs)
            else:
                for tt in range(8):
                    b = next_bank()
                    kt = (tok0 // 128) + tt
                    for kc in range(KC):
                        P.op("pe", lambda e, kc=kc, tt=tt, b=b: e.matmul(psum[b][:], lhsT=hT[:, kc, tt * 128:(tt + 1) * 128], rhs=s3[:, kc, :],
                                                                         start=(kc == 0), stop=(kc == KC - 1)),
                             reads=[sname + "a", sname + "b", "hT"], writes=[psn[b]])
                    flush()
                    h0 = (t - 4) * 4
                    copy_evac(Vt[:, kt, h0:h0 + 4, 0:128], psum[b][:].rearrange("p (h e) -> p h e", h=4), [psn[b]], [], shared=["V%d" % kt], eng="act")
        flush()

    if debug:
        P.barrier()
        P.dma("sp", dbg["hT"], hT[:].rearrange("p a b -> p (a b)"), reads=["hT"])
        P.dma("sp", dbg["K"], Kt[:].rearrange("p a b -> p (a b)"), reads=["K%d" % h for h in range(8)])
        P.dma("sp", dbg["Q"], Qt[:].rearrange("p a b -> p (a b)"), reads=["Q%d" % h for h in range(8)])
        P.dma("sp", dbg["V"], Vt[:].rearrange("p a b c -> p (a b c)"), reads=["V%d" % k for k in range(16)] + ["Vones"])
        P.dma("sp", dbg["L"][:, 0:4096], cqT[:].rearrange("p a b -> p (a b)"), reads=["cqT"])
        P.dma("sp", dbg["L"][:, 4096:8192], ckvT[:].rearrange("p a b -> p (a b)"), reads=["ckvT"])
        P.dma("sp", dbg["L"][:, 8192:10240], krT[:], reads=["krT"])
        P.dma("sp", dbg["mod"], modcols[:], reads=["modcols"])
    if stop_after == "A":
        P.finish("sp")
        return nc, P

    P.barrier()
    RO.reset()
    oT = RO.alloc("oT", [128, 2, KC, 512], BF16)
    RX.reset()
    pt = [(RX.alloc("pt%d" % i, [128, 1024], BF16), "pt%d" % i) for i in range(3)]
    osb = RX.alloc("osb", [128, 8, 130], F32)
    ob = RX.alloc("ob", [128, 4, 128], F32)
    onb = RX.alloc("onb", [128, 8, 128], BF16)
    jk2 = RX.alloc("jk2", [128, 128], BF16)
    SB = [[0, 1], [2, 3]]
    OB = [4, 5, 6]
    TB = 7

    def acc_ap(a):
        return psum[OB[a // 3]][:, (a % 3) * 130:(a % 3) * 130 + 129]

    stages = {}

    def stage(kt, fn):
        stages.setdefault(kt, []).append(fn)

    def run_stages(kt):
        for fn in stages.pop(kt, []):
            fn()

    def attention_unit(qk_emit, scale, finalize):
        def QK(kt):
            qk_emit(kt, SB[kt % 2])

        QK(0)
        QK(1)
        started = set()
        for kt in range(16):
            pb, pbname = pt[kt % 3]
            b0 = SB[kt % 2][0]
            P.op("act", lambda e, b0=b0, pb=pb: e.activation(out=pb[:], in_=psum_all[:, b0 * 512:(b0 + 2) * 512], func=AF.Exp, scale=scale),
                 reads=[psn[b0], psn[b0 + 1]], writes=[pbname])
            for c in range(2):
                for j in range(4):
                    a = j * 2 + c
                    hv = vh[c]
                    st = (kt == 0 and OB[a // 3] not in started)
                    started.add(OB[a // 3])
                    P.op("pe", lambda e, a=a, c=c, j=j, kt=kt, hv=hv, pb=pb, st=st: e.matmul(
                        acc_ap(a), lhsT=pb[:, c * 512 + j * 128:c * 512 + (j + 1) * 128], rhs=Vt[:, kt, hv, 0:129],
                        start=st, stop=(kt == 15), skip_group_check=True),
                        reads=[pbname, "V%d" % kt, "Vones"], writes=[psn[OB[a // 3]]])
            if kt + 2 < 16:
                QK(kt + 2)
            run_stages(kt)
        for i, b in enumerate(OB):
            n = 3 if i < 2 else 2
            o = osb[:, i * 3:i * 3 + n, :]
            src = psum[b][:, 0:n * 130].rearrange("p (a e) -> p a e", a=n)
            P.op("dve", lambda e, o=o, src=src: e.tensor_copy(out=o, in_=src), reads=[psn[b]], shared=["osb"])
        finalize()

    vh = [0, 0]

    wq = wslots[0][:, 0:6144].rearrange("p (a b) -> p a b", a=4)
    wkv = wslots[1][:, 0:4096].rearrange("p (a b) -> p a b", a=2)
    qrT = nc.alloc_sbuf_tensor_at("W_qrT", [128, 4, NOWN], BF16, offset=RW.base + 16384 + 8192)
    ctabC = [RX.alloc("ctabC%d" % i, [128, 512], F32) for i in range(2)]
    stabC = [RX.alloc("stabC%d" % i, [128, 512], F32) for i in range(2)]
    posiC = RX.alloc("posiC", [128, 512], I32)
    kiC = RX.alloc("kiC", [128, 512], I32)
    posfC = RX.alloc("posfC", [128, 512], F32)
    t1C = RX.alloc("t1C", [128, 512], F32)
    t2C = RX.alloc("t2C", [128, 512], F32)
    abufsC = [(RX.alloc("abC%d" % i, [128, 2, 512], BF16), "abC%d" % i) for i in range(1)]

    def mla_prefetch_weights():
        P.dma("pool", wq, w_qb_d.rearrange("(kc p) n -> p kc n", p=128), writes=["wslot0"], shared=["wslot0a", "wslot0b"])
        P.dma("pool", wkv, w_kvb_d.rearrange("(kc p) n -> p kc n", p=128), writes=["wslot1"], shared=["wslot1a", "wslot1b"])

    mla_sin = []

    def mla_tables(blk):
        mla_sin.append(rope_tables(0, blk, cols[:, 39:40], ctabC[blk], stabC[blk], posiC, posfC, t1C, t2C, posfC, "tabC%d" % blk, ki=kiC, sin_later=True))

    def mla_tables_sin():
        for fn in mla_sin:
            fn()

    for h in range(8):
        for qb in range(2):
            def qk_emit(kt, banks, h=h, qb=qb):
                for c in range(2):
                    P.op("pe", lambda e, c=c: e.matmul(psum[banks[c]][:], lhsT=Kt[c * 64:(c + 1) * 64, h, kt * 128:(kt + 1) * 128],
                                                       rhs=Qt[c * 64:(c + 1) * 64, h, qb * 512:(qb + 1) * 512], start=True, stop=True),
                         reads=["K%d" % h, "Q%d" % h], writes=[psn[banks[c]]])

            def finalize(h=h, qb=qb):
                def s1():
                    P.op("dve", lambda e: e.reciprocal(out=rcp[:, 0:8], in_=osb[:, :, 128]), reads=["osb"], writes=["rcp"])
                    P.op("dve", lambda e: e.tensor_scalar(out=rcp[:, 8:12], in0=rcp[:, 0:8].rearrange("p (j c) -> p j c", c=2)[:, :, 1], scalar1=neglam, scalar2=None, op0=ALU.mult),
                         reads=["rcp", "lc"], writes=["rcp2"])
                    for j in range(4):
                        P.op("dve", lambda e, j=j: e.tensor_scalar(out=ob[:, j, :], in0=osb[:, 2 * j + 1, 0:128], scalar1=rcp[:, 8 + j:9 + j], scalar2=None, op0=ALU.mult),
                             reads=["osb", "rcp2"], writes=["ob%d" % j])
                        P.op("dve", lambda e, j=j: e.scalar_tensor_tensor(out=ob[:, j, :], in0=osb[:, 2 * j, 0:128], scalar=rcp[:, 2 * j:2 * j + 1], in1=ob[:, j, :],
                                                                          op0=ALU.mult, op1=ALU.add),
                             reads=["osb", "rcp", "ob%d" % j], writes=["ob%d" % j])

                def s2():
                    for j in range(4):
                        P.op("act", lambda e, j=j: e.activation(out=jk2[:], in_=ob[:, j, :], func=AF.Square, accum_out=ss[:, 8 + j:9 + j]),
                             reads=["ob%d" % j], writes=["jk2"], shared=["ssA"])

                def s3():
                    rsqrt_cols(4, 128)
                    for j in range(4):
                        P.op("dve", lambda e, j=j: e.scalar_tensor_tensor(out=onb[:, j, :], in0=ob[:, j, :], scalar=rstd[:, 8 + j:9 + j], in1=g08_bc[:], op0=ALU.mult, op1=ALU.mult),
                             reads=["ob%d" % j, "rstdA", "g08"], shared=["onb"])

                def s4():
                    for j in range(4):
                        P.op("pe", lambda e, j=j: e.transpose(out=psum_b[TB][:, j, :], in_=onb[:, j, :], identity=ident_b), reads=["onb", "bfm"], writes=[psn[TB]])
                    P.op("dve", lambda e: e.tensor_copy(out=oT[:, qb, h, :], in_=psum_b[TB][:, 0:4, :].rearrange("p a b -> p (a b)")), reads=[psn[TB]], shared=["oT"])
                stage(1, s1)
                stage(4, s2)
                stage(7, s3)
                stage(10, s4)

            vh[0] = h
            vh[1] = h
            it = h * 2 + qb
            if it == 0:
                ada_load(8)
            if it < 8:
                arow = ob[0:1].rearrange("p a b -> p (a b)")
                stage(0, lambda it=it: ada_load(9 + 2 * it))
                stage(7, lambda it=it: ada_row_x(8 + 2 * it, TB, arow))
                if it < 7:
                    stage(8, lambda it=it: ada_load(10 + 2 * it))
                stage(9, lambda it=it: ada_row_y(8 + 2 * it, TB, arow))
                stage(13, lambda it=it: ada_row_x(9 + 2 * it, TB, arow))
                stage(15, lambda it=it: ada_row_y(9 + 2 * it, TB, arow))
            if it == 13:
                stage(0, mla_prefetch_weights)
            if it == 14:
                stage(2, lambda: mla_tables(0))
                stage(9, lambda: mla_tables(1))
            if it == 15:
                stage(12, mla_tables_sin)
            attention_unit(qk_emit, 0.125, finalize)
    if stop_after == "B":
        for kt in range(16):
            run_stages(kt)

    if stop_after == "B":
        if debug:
            P.barrier()
            for qb in range(2):
                P.dma("sp", dbg["O"][:, qb * 8192:qb * 8192 + 4096], oT[:, qb, 0:8, :].rearrange("p b c -> p (b c)"), reads=["oT"])
            P.dma("sp", dbg["mod"], modcols[:], reads=["modcols"])
        P.finish("sp")
        return nc, P

    sname, sname2 = "wslot0", "wslot1"
    ctab, stab, abufs = ctabC, stabC, abufsC
    NB = (0, 1, 2, 3)
    for h in range(8):
        for blk in range(2):
            b = next_bank(NB)
            for kc in range(4):
                P.op("pe", lambda e, kc=kc, h=h, blk=blk, b=b: e.matmul(psum[b][:], lhsT=wq[:, kc, h * 128:(h + 1) * 128], rhs=cqT[:, kc, blk * 512:(blk + 1) * 512],
                                                                        start=(kc == 0), stop=(kc == 3)), reads=[sname, "cqT"], writes=[psn[b]])
            copy_evac(Qt[:, h, blk * 512:(blk + 1) * 512], psum[b][:], [psn[b]], ["Q%d" % h])
            run_stages(h * 2 + blk)
    for hp in range(4):
        for blk in range(2):
            b = next_bank(NB)
            for kc in range(4):
                P.op("pe", lambda e, kc=kc, hp=hp, blk=blk, b=b: e.matmul(psum[b][:], lhsT=wq[:, kc, 1024 + hp * 128:1024 + (hp + 1) * 128],
                                                                          rhs=cqT[:, kc, blk * 512:(blk + 1) * 512], start=(kc == 0), stop=(kc == 3)),
                     reads=[sname, "cqT"], writes=[psn[b]])
            ab, abname = abufs[0]
            abufs.append(abufs.pop(0))
            P.op("dve", lambda e, b=b, ab=ab, blk=blk: e.tensor_tensor(out=ab[:, 0, :], in0=psum[b][:], in1=ctab[blk][:], op=ALU.mult), reads=[psn[b], "tabC%d" % blk], writes=[abname])
            P.op("dve", lambda e, b=b, ab=ab, blk=blk: e.tensor_tensor(out=ab[:, 1, :], in0=psum[b][:], in1=stab[blk][:], op=ALU.mult), reads=[psn[b], "tabC%d" % blk], writes=[abname])
            b2 = next_bank(NB)
            P.op("pe", lambda e, b2=b2, ab=ab: e.matmul(psum[b2][:], lhsT=ident_b, rhs=ab[:, 0, :], start=True, stop=False), reads=[abname, "bfm"], writes=[psn[b2]])
            P.op("pe", lambda e, b2=b2, ab=ab: e.matmul(psum[b2][:], lhsT=Rm_b, rhs=ab[:, 1, :], start=False, stop=True), reads=[abname, "bfm"], writes=[psn[b2]])
            P.op("act", lambda e, b2=b2, hp=hp, blk=blk: e.activation(out=qrT[:, hp, blk * 512:(blk + 1) * 512], in_=psum[b2][:], func=AF.Copy), reads=[psn[b2]],
                 shared=["qrT", "wslot1a", "wslot1b"])
    for h in range(8):
        for blk in range(4):
            b = next_bank(NB)
            for kc in range(2):
                P.op("pe", lambda e, kc=kc, h=h, blk=blk, b=b: e.matmul(psum[b][:], lhsT=wkv[:, kc, h * 128:(h + 1) * 128], rhs=ckvT[:, kc, blk * 512:(blk + 1) * 512],
                                                                        start=(kc == 0), stop=(kc == 1)), reads=[sname2, "ckvT"], writes=[psn[b]])
            copy_evac(Kt[:, h, blk * 512:(blk + 1) * 512], psum[b][:], [psn[b]], ["K%d" % h])
    for tt in range(16):
        for half in range(2):
            b = next_bank(NB)
            for kc in range(2):
                P.op("pe", lambda e, kc=kc, tt=tt, half=half, b=b: e.matmul(psum[b][:], lhsT=ckvT[:, kc, tt * 128:(tt + 1) * 128],
                                                                            rhs=wkv[:, kc, 1024 + half * 512:1024 + (half + 1) * 512], start=(kc == 0), stop=(kc == 1)),
                     reads=[sname2, "ckvT"], writes=[psn[b]])
            copy_evac(Vt[:, tt, half * 4:half * 4 + 4, 0:128], psum[b][:].rearrange("p (h e) -> p h e", h=4), [psn[b]], ["V%d" % tt])

    flush()
    mla_scale = float(192.0 ** -0.5)
    for hp in range(4):
        for qb in range(2):
            def qk_emit(kt, banks, hp=hp, qb=qb):
                for c in range(2):
                    h = 2 * hp + c
                    P.op("pe", lambda e, c=c, h=h: e.matmul(psum[banks[c]][:], lhsT=Kt[:, h, kt * 128:(kt + 1) * 128], rhs=Qt[:, h, qb * 512:(qb + 1) * 512],
                                                            start=True, stop=False), reads=["K%d" % h, "Q%d" % h], writes=[psn[banks[c]]])
                for c in range(2):
                    P.op("pe", lambda e, c=c: e.matmul(psum[banks[c]][:], lhsT=krT[c * 64:(c + 1) * 64, kt * 128:(kt + 1) * 128],
                                                       rhs=qrT[c * 64:(c + 1) * 64, hp, qb * 512:(qb + 1) * 512], start=False, stop=True),
                         reads=["krT", "qrT"], writes=[psn[banks[c]]])

            def finalize(hp=hp, qb=qb):
                def s1():
                    P.op("dve", lambda e: e.reciprocal(out=rcp[:, 0:8], in_=osb[:, :, 128]), reads=["osb"], writes=["rcp"])
                    for c in range(2):
                        for j in range(4):
                            a = j * 2 + c
                            P.op("dve", lambda e, a=a, j=j, c=c: e.tensor_scalar(out=onb[:, c * 4 + j, :], in0=osb[:, a, 0:128], scalar1=rcp[:, a:a + 1], scalar2=None, op0=ALU.mult),
                                 reads=["osb", "rcp"], shared=["onb"])

                def s2():
                    for c in range(2):
                        for j in range(4):
                            P.op("pe", lambda e, j=j, c=c: e.transpose(out=psum_b[TB][:, c * 4 + j, :], in_=onb[:, c * 4 + j, :], identity=ident_b),
                                 reads=["onb", "bfm"], writes=[psn[TB]])
                    for c in range(2):
                        P.op("dve", lambda e, c=c: e.tensor_copy(out=oT[:, qb, 8 + 2 * hp + c, :], in_=psum_b[TB][:, c * 4:c * 4 + 4, :].rearrange("p a b -> p (a b)")),
                             reads=[psn[TB]], shared=["oT"])
                stage(1, s1)
                stage(5, s2)

            vh[0] = 2 * hp
            vh[1] = 2 * hp + 1
            attention_unit(qk_emit, mla_scale, finalize)
    for kt in range(16):
        run_stages(kt)

    if debug:
        P.barrier()
        P.dma("sp", dbg["O"], oT[:].rearrange("p a b c -> p (a b c)"), reads=["oT"])
        P.dma("sp", dbg["QR"], qrT[:].rearrange("p a b -> p (a b)"), reads=["qrT"])
        P.dma("sp", dbg["mod"], modcols[:], reads=["modcols"])
    if stop_after == "C":
        P.finish("sp")
        return nc, P

    P.barrier()
    P.op("dve", lambda e: e.scalar_tensor_tensor(out=a2[:], in0=modcols[:, 64:80], scalar=1.0, in1=cols[:, 16:32], op0=ALU.add, op1=ALU.mult),
         reads=["modcols", "cols"], writes=["a2"])
    RD = Region(nc, "D", base + 38912, 32768 + 16384 + 33280)
    hid = RD.alloc("hid", [128, 64, 512], BF16)
    gate_bc = RD.alloc("gate_bc", [128, D], F32)
    gfin_bc = RD.alloc("gfin_bc", [128, D], F32)
    RE = Region(nc, "E", base + 121344, 20480 + 32768 + 36864)
    xmid = RE.alloc("xmid", [128, 4, D], F32)
    wsl = [RE.alloc("wd%d" % i, [128, 8192], BF16) for i in range(2)]
    junk = RE.alloc("junkD", [128, D], BF16)
    xnb = [(RE.alloc("xnD", [128, D], BF16), "xnD")]
    yT = [(RE.alloc("yT%d" % i, [128, 512], F32)[:], ["yT%d" % i]) for i in range(2)]
    yT.append((junk.bitcast(F32)[:, 0:512], ["junk"]))
    yT.append((xnb[0][0].bitcast(F32)[:, 0:512], ["xnD"]))
    tmpd_all = RE.alloc("tmpd", [128, 1024], F32)
    tmpd = [(tmpd_all[:, i * 512:(i + 1) * 512], "tmpd%d" % i) for i in range(2)]
    xnD2 = tmpd_all.bitcast(BF16)
    rbf = [(RE.alloc("rbf%d" % i, [128, 512], BF16), "rbf%d" % i) for i in range(2)]
    diag = RE.alloc("diag", [128, 128], F32)
    wd_i = [0]

    def next_wd():
        i = wd_i[0] % 2
        wd_i[0] += 1
        return wsl[i], "wd%d" % i

    P.dma("sp", gfin_bc[:], gfin_d.partition_broadcast(128), writes=["gfin_bc"])
    for kc in range(KC):
        b = next_bank()
        P.op("dve", lambda e, kc=kc: e.tensor_scalar(out=diag[:], in0=ident_f[:], scalar1=modcols[:, 32 + kc:33 + kc], scalar2=None, op0=ALU.mult),
             reads=["ident_f", "modcols"], writes=["diag"])
        P.op("pe", lambda e, b=b: e.matmul(psum[b][:, 0:128], lhsT=ones_f[:], rhs=diag[:], start=True, stop=True), reads=["ones_f", "diag"], writes=[psn[b]])
        P.op("act", lambda e, b=b, kc=kc: e.activation(out=gate_bc[:, kc * 128:(kc + 1) * 128], in_=psum[b][:, 0:128], func=AF.Copy), reads=[psn[b]], writes=["gate_bc"])

    w_out_v = w_out_d.rearrange("(kc p) n -> p kc n", p=128)
    w_ff1_v = w_ff1_d.rearrange("(kc p) n -> p kc n", p=128)
    w_ff2_v = w_ff2_d.rearrange("(fc p) n -> p fc n", p=128)
    h2T = None
    for blk in range(2):
        for tt in range(4):
            r0 = blk * 512 + tt * 128
            P.dma("sp", xmid[:, tt, :], x_d[r0:r0 + 128, :], reads=[], writes=["xmid%d" % tt])
        for cb in range(4):
            slot, sname = next_wd()
            s3 = slot.reshape([128, KC, 512])
            load_w_cols(s3, sname, w_out_v[:, :, cb * 512:(cb + 1) * 512])
            for tt in range(4):
                b = next_bank()
                for kc in range(KC):
                    P.op("pe", lambda e, kc=kc, tt=tt, b=b, s3=s3: e.matmul(psum[b][:], lhsT=oT[:, blk, kc, tt * 128:(tt + 1) * 128], rhs=s3[:, kc, :],
                                                                            start=(kc == 0), stop=(kc == KC - 1)),
                         reads=[sname + "a", sname + "b", "oT"], writes=[psn[b]])
                tm, tmname = tmpd[(cb * 4 + tt) % 2]
                P.op("dve", lambda e, b=b, tm=tm, cb=cb: e.tensor_tensor(out=tm, in0=psum[b][:], in1=gate_bc[:, cb * 512:(cb + 1) * 512], op=ALU.mult),
                     reads=[psn[b], "gate_bc"], writes=[tmname])
                xs = xmid[:, tt, cb * 512:(cb + 1) * 512]
                P.op("dve", lambda e, xs=xs, tm=tm: e.tensor_tensor(out=xs, in0=xs, in1=tm, op=ALU.add), reads=[tmname, "xmid%d" % tt], writes=["xmid%d" % tt])
        h2T = oT[:, blk]
        make_hT([(xmid[:, tt, :], "xmid%d" % tt, None) for tt in range(4)], h2T, 0, a2, modcols[:, 48:64], "a2",
                [(xnb[0][0][:], ["xnD"]), (xnD2[:], ["tmpd0", "tmpd1"])], junk, "oT")
        for t in range(16):
            slot, sname = next_wd()
            s3 = slot.reshape([128, KC, 512])
            load_w_cols(s3, sname, w_ff1_v[:, :, t * 512:(t + 1) * 512])
            for cc in range(4):
                f = t * 4 + cc
                b = next_bank()
                for kc in range(KC):
                    P.op("pe", lambda e, kc=kc, cc=cc, b=b, s3=s3: e.matmul(psum[b][:], lhsT=s3[:, kc, cc * 128:(cc + 1) * 128], rhs=h2T[:, kc, :],
                                                                            start=(kc == 0), stop=(kc == KC - 1)),
                         reads=[wn(sname, cc * 128), "oT"], writes=[psn[b]])
                rb, rbname = rbf[f % 2]
                P.op("act", lambda e, b=b, rb=rb: e.activation(out=rb[:], in_=psum[b][:], func=AF.Relu), reads=[psn[b]], writes=[rbname])
                P.op("dve", lambda e, b=b, rb=rb, f=f: e.scalar_tensor_tensor(out=hid[:, f, :], in0=psum[b][:], scalar=0.0, in1=rb[:], op0=ALU.max, op1=ALU.mult),
                     reads=[psn[b], rbname], shared=["hid"])
        for cbk in range(4):
            banks = [(cbk % 2) * 4 + j for j in range(4)]
            for fg in range(4):
                slot, sname = next_wd()
                s3 = slot.reshape([128, 16, 512])
                src = w_ff2_v[:, fg * 16:(fg + 1) * 16, cbk * 512:(cbk + 1) * 512]
                P.dma("pool", s3[:, 0:8, :], src[:, 0:8, :], writes=[sname + "fa"], shared=[sname + "a", sname + "b"])
                P.dma("pool", s3[:, 8:16, :], src[:, 8:16, :], writes=[sname + "fb"], shared=[sname + "a", sname + "b"])
                for fi in range(16):
                    fc = fg * 16 + fi
                    for j in range(4):
                        P.op("pe", lambda e, fi=fi, fc=fc, j=j, s3=s3: e.matmul(psum[banks[j]][:], lhsT=s3[:, fi, j * 128:(j + 1) * 128], rhs=hid[:, fc, :],
                                                                                 start=(fc == 0), stop=(fc == 63)),
                             reads=[sname + ("fa" if fi < 8 else "fb"), "hid"], writes=[psn[banks[j]]])
                if fg == 0:
                    flush()
            for j in range(4):
                c = cbk * 4 + j
                yt, ytnames = yT[j]
                P.op("act", lambda e, j=j, yt=yt, c=c, banks=banks: e.activation(out=yt, in_=psum[banks[j]][:], func=AF.Identity, scale=modcols[:, 80 + c:81 + c]),
                     reads=[psn[banks[j]], "modcols"], writes=ytnames)

                def later(j=j, c=c, yt=yt, ytnames=ytnames, banks=banks):
                    for tt in range(4):
                        P.op("pe", lambda e, tt=tt: e.transpose(out=psum[banks[j]][:, tt * 128:(tt + 1) * 128], in_=yt[:, tt * 128:(tt + 1) * 128], identity=ident_f[:]),
                             reads=ytnames + ["ident_f"], writes=[psn[banks[j]]])
                    xs = xmid[:, :, c * 128:(c + 1) * 128]
                    P.op("dve", lambda e: e.tensor_tensor(out=xs, in0=xs, in1=psum[banks[j]][:].rearrange("p (t c) -> p t c", t=4), op=ALU.add),
                         reads=[psn[banks[j]]] + ["xmid%d" % tt for tt in range(4)], writes=["xmid%d" % tt for tt in range(4)])
                defer(later)
        flush()
        for tt in range(4):
            c = tt
            P.op("act", lambda e, tt=tt, c=c: e.activation(out=junk[:], in_=xmid[:, tt, :], func=AF.Square, accum_out=ss[:, c:c + 1]), reads=["xmid%d" % tt], writes=["junk", "ss%d" % c])
            P.op("act", lambda e, c=c: e.activation(out=ms[:, c:c + 1], in_=ss[:, c:c + 1], func=AF.Sqrt, bias=epscol, scale=1.0 / D), reads=["ss%d" % c, "epscol"], writes=["ms%d" % c])
            P.op("dve", lambda e, c=c: e.reciprocal(out=rstd[:, c:c + 1], in_=ms[:, c:c + 1]), reads=["ms%d" % c], writes=["rstd%d" % c])
            P.op("dve", lambda e, tt=tt, c=c: e.scalar_tensor_tensor(out=xmid[:, tt, :], in0=xmid[:, tt, :], scalar=rstd[:, c:c + 1], in1=gfin_bc[:], op0=ALU.mult, op1=ALU.mult),
                 reads=["xmid%d" % tt, "rstd%d" % c, "gfin_bc"], writes=["xmid%d" % tt])
            r0 = blk * 512 + tt * 128
            P.dma("sp", y_d[r0:r0 + 128, :], xmid[:, tt, :], reads=["xmid%d" % tt])
    P.finish("sp")
    return nc, P


def _const_mats():
    m = np.zeros((128, 384), np.float32)
    m[:, 0:128] = np.eye(128, dtype=np.float32)
    Rd = np.zeros((128, 128), np.float32)
    for c in range(2):
        for d in range(8):
            mlo = c * 64 + d
            Rd[mlo + 8, mlo] = -1.0
            Rd[mlo, mlo + 8] = 1.0
    Rm = np.zeros((128, 128), np.float32)
    for g in range(2):
        for d in range(32):
            mlo = g * 64 + d
            Rm[mlo + 32, mlo] = -1.0
            Rm[mlo, mlo + 32] = 1.0
    m[:, 128:256] = Rd
    m[:, 256:384] = Rm
    return m


def _inv_cols():
    inv_d = (1.0 / (np.float32(500000.0) ** (np.arange(0, 16, 2, dtype=np.float32) / np.float32(16)))).astype(np.float32)
    inv_m = (1.0 / (np.float32(10000.0) ** (np.arange(0, 64, 2, dtype=np.float32) / np.float32(64)))).astype(np.float32)
    cd = np.zeros(128, np.float32)
    cm = np.zeros(128, np.float32)
    for p in range(128):
        d = p % 64
        if d < 16:
            cd[p] = inv_d[d % 8]
        cm[p] = inv_m[d % 32]
    return cd, cm


_CACHE = {}


def kernel(x, c, positions, w_ada, b_ada, g_norm_mix, w_in, lambda_q1, lambda_k1, lambda_q2, lambda_k2,
           g_diff_sub, g_q_a, w_q_b, g_kv_a, w_kv_b, w_out, g_norm_ffn, w_ff1, w_ff2, g_final, _debug=False, _stop=None):
    f32 = np.float32
    x = np.asarray(x, f32)
    colmaj = lambda v: np.ascontiguousarray(np.asarray(v, f32).reshape(-1, 128).T)
    cd, cm = _inv_cols()
    cols = np.concatenate([colmaj(g_norm_mix[0]), colmaj(g_norm_ffn[0]), colmaj(g_q_a[0]), colmaj(g_kv_a[0]),
                           cd[:, None], cm[:, None], colmaj(b_ada[0])], axis=1).astype(f32)
    assert cols.shape == (128, 136)
    cmat = _const_mats()
    lam = np.stack([lambda_q1[0], lambda_k1[0], lambda_q2[0], lambda_k2[0]]).astype(f32)
    w_in0 = np.asarray(w_in[0], f32)
    w_in_ext = np.ascontiguousarray(np.concatenate([w_in0, w_in0[:, 3840:3904]], axis=1))
    wq = np.asarray(w_q_b[0], f32).reshape(512, 8, 192)
    w_qb = np.ascontiguousarray(np.concatenate([wq[:, :, :128].reshape(512, 1024), wq[:, :, 128:].reshape(512, 512)], axis=1))
    wkv = np.asarray(w_kv_b[0], f32).reshape(256, 8, 256)
    w_kvb = np.ascontiguousarray(np.concatenate([wkv[:, :, :128].reshape(256, 1024), wkv[:, :, 128:].reshape(256, 1024)], axis=1))
    shared = {
        "cols": cols, "cmat": cmat, "lam": lam, "gdiff": np.asarray(g_diff_sub, f32).reshape(1, 128),
        "gfin": np.asarray(g_final, f32).reshape(1, D), "w_ada": np.ascontiguousarray(w_ada[0], dtype=f32), "w_in": w_in_ext,
        "w_qb": w_qb, "w_kvb": w_kvb, "w_out": np.ascontiguousarray(w_out[0], dtype=f32),
        "w_ff1": np.ascontiguousarray(w_ff1[0], dtype=f32), "w_ff2": np.ascontiguousarray(w_ff2[0], dtype=f32),
    }
    in_maps = []
    for core in range(8):
        b, hf = core // 2, core % 2
        own = slice(hf * NOWN, (hf + 1) * NOWN)
        oth = slice((1 - hf) * NOWN, (2 - hf) * NOWN)
        m = dict(shared)
        m["x"] = np.ascontiguousarray(np.concatenate([x[b, own], x[b, oth]], axis=0))
        m["pos"] = np.ascontiguousarray(np.concatenate([positions[b, own], positions[b, oth]]).astype(np.int32).reshape(1, NTOK))
        m["c"] = colmaj(c[b])
        in_maps.append(m)
    key = (_debug, _stop)
    if key not in _CACHE:
        _CACHE[key] = build(debug=_debug, stop_after=_stop)
    nc, P = _CACHE[key]
    res = run_bass_kernel_spmd(nc, in_maps, core_ids=list(range(8)))
    if _debug:
        return res
    out = np.empty((4, 2048, D), f32)
    for core in range(8):
        b, hf = core // 2, core % 2
        out[b, hf * NOWN:(hf + 1) * NOWN] = res.results[core]["y"]
    return out
```

```python
import numpy as np
import concourse.bass as bass
import concourse.mybir as mybir
from concourse.bass_utils import run_bass_kernel_spmd

F32 = mybir.dt.float32
BF16 = mybir.dt.bfloat16
I32 = mybir.dt.int32
AF = mybir.ActivationFunctionType
ALU = mybir.AluOpType
AX = mybir.AxisListType

D = 2048
KC = 16
NTOK = 2048
NOWN = 1024
EPS = 1e-6
W_IN_COLS = 3968
TWO_PI = float(2.0 * np.pi)
C1 = 6.28125
C2 = float(2.0 * np.pi - 6.28125)
PI_LO = 3.1415925


class Res:
    __slots__ = ("name", "w", "ws", "r")

    def __init__(self, name):
        self.name = name
        self.w = None
        self.ws = {}
        self.r = {}


class Eng:
    def __init__(self, name, handle, sem, is_pe=False):
        self.name = name
        self.h = handle
        self.sem = sem
        self.count = 0
        self.waited = {}
        self.is_pe = is_pe
        self.nwaits = 0
        self.nins = 0


class Prog:
    def __init__(self, nc, n_dma_sems=24):
        self.nc = nc
        self.sems = {}
        self.engs = {}
        for nm, h in (("pe", nc.tensor), ("act", nc.scalar), ("dve", nc.vector), ("pool", nc.gpsimd), ("sp", nc.sync)):
            s = nc.alloc_semaphore("s_" + nm)
            self.sems[nm] = s
            self.engs[nm] = Eng(nm, h, s, is_pe=(nm == "pe"))
        self.dma_sems = []
        self.dma_pools = {"sp": [], "pool": []}
        for q, n in (("sp", n_dma_sems), ("pool", 8)):
            for i in range(n):
                k = "d%s%d" % (q, i)
                self.sems[k] = nc.alloc_semaphore("s_" + k)
                ent = [k, 0]
                self.dma_sems.append(ent)
                self.dma_pools[q].append(ent)
        self.dma_i = {"sp": 0, "pool": 0}
        self.res = {}

    def R(self, name):
        r = self.res.get(name)
        if r is None:
            r = Res(name)
            self.res[name] = r
        return r

    def _deps(self, reads, writes, shared=(), me=None):
        deps = {}

        def add(k, v):
            if deps.get(k, 0) < v:
                deps[k] = v
        for r in reads:
            r = self.R(r)
            if r.w is not None:
                add(*r.w)
            for k, v in r.ws.items():
                add(k, v)
            if r.name.startswith("ps") and me is not None:
                for k, v in r.r.items():
                    if k != me:
                        add(k, v)
        for w in writes:
            w = self.R(w)
            if w.w is not None:
                add(*w.w)
            for k, v in w.ws.items():
                add(k, v)
            for k, v in w.r.items():
                add(k, v)
        for w in shared:
            w = self.R(w)
            if w.w is not None:
                add(*w.w)
            for k, v in w.r.items():
                add(k, v)
        return deps

    def _wait(self, e, deps):
        for k, v in deps.items():
            if e.is_pe and k == "pe":
                continue
            if e.waited.get(k, 0) >= v:
                continue
            e.h.wait_ge(self.sems[k], v)
            e.waited[k] = v
            e.nwaits += 1

    def _mark(self, reads, writes, key, val, shared=()):
        for r in reads:
            r = self.R(r)
            if r.r.get(key, 0) < val:
                r.r[key] = val
        for w in writes:
            w = self.R(w)
            w.w = (key, val)
            w.ws = {}
            w.r = {}
        for w in shared:
            w = self.R(w)
            if w.ws.get(key, 0) < val:
                w.ws[key] = val

    def op(self, eng, fn, reads=(), writes=(), shared=()):
        e = self.engs[eng]
        self._wait(e, self._deps(reads, writes, shared, me=eng))
        ins = fn(e.h)
        e.count += 1
        e.nins += 1
        ins.then_inc(e.sem, 1)
        self._mark(reads, writes, eng, e.count, shared)
        return ins

    def dma(self, queue, out, in_, reads=(), writes=(), shared=()):
        e = self.engs[queue]
        pool = self.dma_pools[queue]
        slot = pool[self.dma_i[queue] % len(pool)]
        self.dma_i[queue] += 1
        k, uses = slot
        deps = self._deps(reads, writes, shared)
        if uses > 0:
            deps[k] = max(deps.get(k, 0), 16 * uses)
        self._wait(e, deps)
        e.h.dma_start(out=out, in_=in_).then_inc(self.sems[k], 16)
        slot[1] = uses + 1
        e.nins += 1
        self._mark(reads, writes, k, 16 * (uses + 1), shared)

    def barrier(self, engs=("pe", "act", "dve", "pool", "sp")):
        for nm in engs:
            e = self.engs[nm]
            deps = {}
            for o in ("pe", "act", "dve", "pool"):
                if o != nm and self.engs[o].count > 0:
                    deps[o] = self.engs[o].count
            for k, uses in self.dma_sems:
                if uses > 0:
                    deps[k] = 16 * uses
            self._wait(e, deps)

    def finish(self, eng="sp"):
        e = self.engs[eng]
        deps = {}
        for k, uses in self.dma_sems:
            if uses > 0:
                deps[k] = 16 * uses
        for nm, en in self.engs.items():
            if nm != eng and en.count > 0:
                deps[nm] = en.count
        self._wait(e, deps)


class Region:
    def __init__(self, nc, name, base, size):
        self.nc, self.name, self.base, self.size, self.off = nc, name, base, size, 0
        self.n = 0

    def reset(self):
        self.off = 0

    def alloc(self, name, shape, dtype):
        esz = 2 if dtype == BF16 else 4
        nbytes = esz
        for s in shape[1:]:
            nbytes *= s
        off = (self.off + 31) // 32 * 32
        assert off + nbytes <= self.size, (self.name, name, off, nbytes, self.size)
        self.off = off + nbytes
        self.n += 1
        return self.nc.alloc_sbuf_tensor_at("%s_%s_%d" % (self.name, name, self.n), list(shape), dtype, offset=self.base + off)


def build(debug=False, stop_after=None):
    nc = bass.Bass("TRN2", target_bir_lowering=False)

    def dram(name, shape, dt, out=False):
        return nc.dram_tensor(name, list(shape), dt, kind="ExternalOutput" if out else "ExternalInput").ap()

    x_d = dram("x", [NTOK, D], F32)
    pos_d = dram("pos", [1, NTOK], I32)
    c_d = dram("c", [128, KC], F32)
    cols_d = dram("cols", [128, 136], F32)
    cmat_d = dram("cmat", [128, 384], F32)
    lam_d = dram("lam", [4, 64], F32)
    gdiff_d = dram("gdiff", [1, 128], F32)
    gfin_d = dram("gfin", [1, D], F32)
    w_ada_d = dram("w_ada", [D, 6 * D], F32)
    w_in_d = dram("w_in", [D, W_IN_COLS], F32)
    w_qb_d = dram("w_qb", [512, 1536], F32)
    w_kvb_d = dram("w_kvb", [256, 2048], F32)
    w_out_d = dram("w_out", [D, D], F32)
    w_ff1_d = dram("w_ff1", [D, 4 * D], F32)
    w_ff2_d = dram("w_ff2", [4 * D, D], F32)
    y_d = dram("y", [NOWN, D], F32, out=True)
    dbg = {}
    if debug:
        dbg["hT"] = dram("dbg_hT", [128, KC * 1024], BF16, out=True)
        dbg["K"] = dram("dbg_K", [128, 8 * 2048], BF16, out=True)
        dbg["Q"] = dram("dbg_Q", [128, 8 * 1024], BF16, out=True)
        dbg["V"] = dram("dbg_V", [128, 16 * 8 * 130], BF16, out=True)
        dbg["L"] = dram("dbg_L", [128, 10240], BF16, out=True)
        dbg["O"] = dram("dbg_O", [128, 2 * KC * 512], BF16, out=True)
        dbg["QR"] = dram("dbg_QR", [128, 4 * 1024], BF16, out=True)
        dbg["mod"] = dram("dbg_mod", [128, 96], F32, out=True)

    P = Prog(nc)
    base = (nc.sbuf_base + 63) // 64 * 64
    RP = Region(nc, "P", base + 0, 6144)
    RO = Region(nc, "O", base + 6144, 32768)
    RK = Region(nc, "K", base + 38912, 32768)
    RQ = Region(nc, "Q", base + 71680, 16384)
    RV = Region(nc, "V", base + 88064, 33280)
    RL = Region(nc, "L", base + 121344, 20480)
    RW = Region(nc, "W", base + 141824, 32768)
    RX = Region(nc, "X", base + 174592, 36864)
    assert base + 174592 + 36864 <= nc.sbuf_top, (base, nc.sbuf_top)

    bfm = RP.alloc("bfm", [128, 384], BF16)
    ident_b, Rd_b, Rm_b = bfm[:, 0:128], bfm[:, 128:256], bfm[:, 256:384]
    ident_f = RP.alloc("ident_f", [128, 128], F32)
    ones_f = RP.alloc("ones_f", [128, 128], F32)
    modcols = RP.alloc("modcols", [128, 96], F32)
    cols = RP.alloc("cols", [128, 136], F32)
    a1 = RP.alloc("a1", [128, KC], F32)
    a2 = RP.alloc("a2", [128, KC], F32)
    cT = RP.alloc("cT", [128, KC], F32)
    silu_b = RP.alloc("silu_b", [128, KC], BF16)
    g08_bc = RP.alloc("g08", [128, 128], F32)
    lamt = RP.alloc("lamt", [128, 4, 64], F32)
    lc = RP.alloc("lc", [128, 8], F32)
    ss = RP.alloc("ss", [128, 16], F32)
    ms = RP.alloc("ms", [128, 16], F32)
    rstd = RP.alloc("rstd", [128, 16], F32)
    neghalf = RP.alloc("neghalf", [128, 16], F32)
    rcp = RP.alloc("rcp", [128, 16], F32)

    psum_all = nc.alloc_psum_tensor("psall", [128, 4096], F32)
    psum_all_b = psum_all.bitcast(BF16).reshape([128, 64, 128])
    psum = [psum_all[:, i * 512:(i + 1) * 512] for i in range(8)]
    psum_b = [psum_all_b[:, i * 8:(i + 1) * 8, :] for i in range(8)]
    psn = ["ps%d" % i for i in range(8)]
    bank_rr = [0]

    def next_bank(allowed=(0, 1, 2, 3, 4, 5, 6, 7)):
        b = allowed[bank_rr[0] % len(allowed)]
        bank_rr[0] += 1
        return b

    Kt = RK.alloc("K", [128, 8, NTOK], BF16)
    Qt = RQ.alloc("Q", [128, 8, NOWN], BF16)
    Vt = RV.alloc("V", [128, 16, 8, 130], BF16)
    cqT = RL.alloc("cqT", [128, 4, NOWN], BF16)
    ckvT = RL.alloc("ckvT", [128, 2, NTOK], BF16)
    krT = RL.alloc("krT", [128, NTOK], BF16)
    wslots = [RW.alloc("w%d" % i, [128, 8192], BF16) for i in range(2)]
    wslot_i = [0]

    def next_wslot():
        i = wslot_i[0] % 2
        wslot_i[0] += 1
        return wslots[i], "wslot%d" % i

    pend = []

    def defer(fn):
        pend.append(fn)

    def flush(keep=0):
        while len(pend) > keep:
            pend.pop(0)()

    def load_w_cols(s3, sname, src, ncols=512):
        h = 256
        P.dma("pool", s3[:, :, 0:h], src[:, :, 0:h], writes=[sname + "a"], shared=[sname + "fa", sname + "fb"])
        P.dma("pool", s3[:, :, h:ncols], src[:, :, h:ncols], writes=[sname + "b"], shared=[sname + "fa", sname + "fb"])

    def wn(sname, c0):
        return sname + ("a" if c0 < 256 else "b")

    RX.reset()
    cstage = nc.alloc_sbuf_tensor_at("K_cstage", [128, 384], F32, offset=RK.base)
    P.dma("sp", cstage[:], cmat_d, writes=["cstage"])
    P.dma("sp", cols[:], cols_d, writes=["cols"])
    P.dma("sp", cT[:], c_d, writes=["cT"])
    for i in range(4):
        P.dma("sp", lamt[:, i, :], lam_d[i:i + 1, :].partition_broadcast(128), writes=["lamt"])
    P.dma("sp", g08_bc[:], gdiff_d.partition_broadcast(128), writes=["g08"])
    P.op("dve", lambda e: e.tensor_copy(out=bfm[:], in_=cstage[:]), reads=["cstage"], writes=["bfm"])
    P.op("dve", lambda e: e.tensor_copy(out=ident_f[:], in_=cstage[:, 0:128]), reads=["cstage"], writes=["ident_f"])
    P.op("pool", lambda e: e.memset(ones_f[:], 1.0), writes=["ones_f"])
    P.op("pool", lambda e: e.memset(neghalf[:], -0.5), writes=["neghalf"])
    P.op("pool", lambda e: e.memset(modcols[:], 0.0), writes=["modcols"])
    P.op("pool", lambda e: e.memset(Vt[:, :, :, 128:130], 1.0), writes=["Vones"])
    P.op("act", lambda e: e.activation(out=silu_b[:], in_=cT[:], func=AF.Silu), reads=["cT"], writes=["silu_b"])
    P.op("dve", lambda e: e.tensor_tensor(out=lamt[:, 0, :], in0=lamt[:, 0, :], in1=lamt[:, 1, :], op=ALU.mult), reads=["lamt"], writes=["lamt"])
    P.op("dve", lambda e: e.tensor_tensor(out=lamt[:, 2, :], in0=lamt[:, 2, :], in1=lamt[:, 3, :], op=ALU.mult), reads=["lamt"], writes=["lamt"])
    P.op("dve", lambda e: e.reduce_sum(out=lc[:, 0:1], in_=lamt[:, 0, :], axis=AX.X), reads=["lamt"], writes=["lc"])
    P.op("dve", lambda e: e.reduce_sum(out=lc[:, 1:2], in_=lamt[:, 2, :], axis=AX.X), reads=["lamt"], writes=["lc"])
    P.op("act", lambda e: e.activation(out=lc[:, 2:4], in_=lc[:, 0:2], func=AF.Exp), reads=["lc"], writes=["lc"])
    P.op("dve", lambda e: e.tensor_tensor(out=lc[:, 4:5], in0=lc[:, 2:3], in1=lc[:, 3:4], op=ALU.subtract), reads=["lc"], writes=["lc"])
    P.op("dve", lambda e: e.tensor_scalar(out=lc[:, 5:6], in0=lc[:, 4:5], scalar1=-1.0, scalar2=-0.2, op0=ALU.mult, op1=ALU.add), reads=["lc"], writes=["lc"])
    neglam = lc[:, 5:6]
    P.op("dve", lambda e: e.tensor_scalar(out=g08_bc[:], in0=g08_bc[:], scalar1=0.8, scalar2=None, op0=ALU.mult), reads=["g08"], writes=["g08"])

    w_ada_v = w_ada_d.rearrange("(kc p) n -> p kc n", p=128)
    w_in_v = w_in_d.rearrange("(kc p) n -> p kc n", p=128)

    ada_slots = {}

    def ada_load(t):
        slot, sname = next_wslot()
        s3 = slot.reshape([128, KC, 512])
        load_w_cols(s3, sname, w_ada_v[:, :, t * 512:(t + 1) * 512])
        ada_slots[t] = (s3, sname)

    def ada_compute(t, bank):
        s3, sname = ada_slots.pop(t)
        for cc in range(4):
            for kc in range(KC):
                P.op("pe", lambda e, cc=cc, kc=kc: e.matmul(psum[bank][:, cc:cc + 1], lhsT=s3[:, kc, cc * 128:(cc + 1) * 128],
                                                            rhs=silu_b[:, kc:kc + 1], start=(kc == 0), stop=(kc == KC - 1)),
                     reads=[wn(sname, cc * 128), "silu_b"], writes=[psn[bank]])
        P.op("dve", lambda e: e.tensor_tensor(out=modcols[:, t * 4:(t + 1) * 4], in0=psum[bank][:, 0:4],
                                              in1=cols[:, 40 + t * 4:40 + (t + 1) * 4], op=ALU.add),
             reads=[psn[bank], "cols"], shared=["modcols"])

    def ada_row_x(t, bank, row):
        s3, sname = ada_slots[t]
        for kc in range(KC):
            P.op("pe", lambda e, kc=kc: e.matmul(psum[bank][0:1, :], lhsT=silu_b[:, kc:kc + 1], rhs=s3[:, kc, :], start=(kc == 0), stop=(kc == KC - 1)),
                 reads=[sname + "a", sname + "b", "silu_b"], writes=[psn[bank]])
        P.op("dve", lambda e: e.tensor_copy(out=row, in_=psum[bank][0:1, :]), reads=[psn[bank]], writes=["ob0", "ob1", "ob2", "ob3"])

    def ada_row_y(t, bank, row):
        ada_slots.pop(t)
        for cc in range(4):
            P.op("pe", lambda e, cc=cc: e.matmul(psum[bank][:, cc:cc + 1], lhsT=row[0:1, cc * 128:(cc + 1) * 128], rhs=ones_f[0:1, 0:1], start=True, stop=True),
                 reads=["ob0", "ob1", "ob2", "ob3", "ones_f"], writes=[psn[bank]])
        P.op("dve", lambda e: e.tensor_tensor(out=modcols[:, t * 4:(t + 1) * 4], in0=psum[bank][:, 0:4],
                                              in1=cols[:, 40 + t * 4:40 + (t + 1) * 4], op=ALU.add),
             reads=[psn[bank], "cols"], shared=["modcols"])

    ada_load(0)

    preloaded = {}

    def inproj_load(t):
        slot, sname = next_wslot()
        ncols = 512 if t < 7 else 384
        s3 = slot.reshape([128, KC, 512])
        load_w_cols(s3, sname, w_in_v[:, :, t * 512:t * 512 + ncols], ncols)
        return s3, sname

    def ada_hook(i):
        if i + 1 < 8:
            ada_load(i + 1)
        else:
            preloaded["other"] = inproj_load(4)
        ada_compute(i, next_bank())

    def rsqrt_cols(n, dim, c0=8):
        P.op("dve", lambda e: e.tensor_scalar(out=ms[:, c0:c0 + n], in0=ss[:, c0:c0 + n], scalar1=1.0 / dim, scalar2=EPS, op0=ALU.mult, op1=ALU.add),
             reads=["ssA"], writes=["msA"])
        P.op("pool", lambda e: e.tensor_tensor(out=rstd[:, c0:c0 + n], in0=ms[:, c0:c0 + n], in1=neghalf[:, 0:n], op=ALU.pow),
             reads=["msA", "neghalf"], writes=["rstdA"])

    hT_ctr = [0]

    def make_hT(src_tiles, dst, dst_tok0, acols, shcols, acname, xn_bufs, junk, dst_name, plain=False, hook=None):
        cols_of = {}

        def stage_a(i):
            xt, xname, loader = src_tiles[i]
            if loader is not None:
                loader()
            c = hT_ctr[0] % 8
            hT_ctr[0] += 1
            cols_of[i] = c
            sc, mc, rc = "ss%d" % c, "ms%d" % c, "rstd%d" % c
            P.op("act", lambda e: e.activation(out=junk[:], in_=xt, func=AF.Square, accum_out=ss[:, c:c + 1]), reads=[xname], writes=["junk", sc])
            P.op("act", lambda e: e.activation(out=ms[:, c:c + 1], in_=ss[:, c:c + 1], func=AF.Sqrt, bias=epscol, scale=1.0 / D),
                 reads=[sc, "epscol"], writes=[mc])
            P.op("dve", lambda e: e.reciprocal(out=rstd[:, c:c + 1], in_=ms[:, c:c + 1]), reads=[mc], writes=[rc])
            xn, xnnames = xn_bufs[i % len(xn_bufs)]
            P.op("dve", lambda e: e.tensor_scalar(out=xn, in0=xt, scalar1=rstd[:, c:c + 1], scalar2=None, op0=ALU.mult),
                 reads=[xname, rc], writes=xnnames)

        def stage_b(i):
            xn, xnnames = xn_bufs[i % len(xn_bufs)]
            for g in range(2):
                b = next_bank()
                for j in range(8):
                    kc = g * 8 + j
                    P.op("pe", lambda e, j=j, kc=kc: e.transpose(out=psum_b[b][:, j, :], in_=xn[:, kc * 128:(kc + 1) * 128], identity=ident_b),
                         reads=xnnames + ["bfm"], writes=[psn[b]])
                for j in range(8):
                    kc = g * 8 + j
                    o = dst[:, kc, dst_tok0 + i * 128:dst_tok0 + (i + 1) * 128]
                    if plain:
                        if g == 0:
                            P.op("dve", lambda e, j=j, o=o: e.tensor_copy(out=o, in_=psum_b[b][:, j, :]), reads=[psn[b]], shared=[dst_name])
                        else:
                            P.op("act", lambda e, j=j, o=o: e.activation(out=o, in_=psum_b[b][:, j, :], func=AF.Copy), reads=[psn[b]], shared=[dst_name])
                    elif g == 0:
                        P.op("dve", lambda e, j=j, kc=kc, o=o: e.tensor_scalar(out=o, in0=psum_b[b][:, j, :], scalar1=acols[:, kc:kc + 1],
                                                                               scalar2=shcols[:, kc:kc + 1], op0=ALU.mult, op1=ALU.add),
                             reads=[psn[b], acname, "modcols"], shared=[dst_name])
                    else:
                        P.op("act", lambda e, j=j, kc=kc, o=o: e.activation(out=o, in_=psum_b[b][:, j, :], func=AF.Identity,
                                                                            bias=shcols[:, kc:kc + 1], scale=acols[:, kc:kc + 1]),
                             reads=[psn[b], acname, "modcols"], shared=[dst_name])

        n = len(src_tiles)
        stage_a(0)
        for i in range(n):
            if i + 1 < n:
                stage_a(i + 1)
            if hook is not None:
                hook(i)
            stage_b(i)

    def rope_tables(tok0, blk, invcol, ctab, stab, posi, posf, t1, t2, t3, tabname, ki=None, sin_later=False):
        t0 = tok0 + blk * 512
        P.dma("sp", posi[:], pos_d[0:1, t0:t0 + 512].partition_broadcast(128), writes=["posi"])
        P.op("dve", lambda e: e.tensor_copy(out=posf[:], in_=posi[:]), reads=["posi"], writes=["posf"])
        P.op("dve", lambda e: e.tensor_scalar(out=t1[:], in0=posf[:], scalar1=invcol, scalar2=None, op0=ALU.mult), reads=["posf", "cols"], writes=["t1"])
        P.op("dve", lambda e: e.tensor_scalar(out=t2[:], in0=t1[:], scalar1=1.0 / TWO_PI, scalar2=None, op0=ALU.mult), reads=["t1"], writes=["t2"])
        kt_ = ki if ki is not None else posi
        kn_ = "ki" if ki is not None else "posi"
        P.op("dve", lambda e: e.tensor_copy(out=kt_[:], in_=t2[:]), reads=["t2"], writes=[kn_])
        P.op("dve", lambda e: e.tensor_copy(out=t2[:], in_=kt_[:]), reads=[kn_], writes=["t2"])
        P.op("dve", lambda e: e.scalar_tensor_tensor(out=t1[:], in0=t2[:], scalar=-C1, in1=t1[:], op0=ALU.mult, op1=ALU.add), reads=["t1", "t2"], writes=["t1"])
        P.op("dve", lambda e: e.scalar_tensor_tensor(out=t1[:], in0=t2[:], scalar=-C2, in1=t1[:], op0=ALU.mult, op1=ALU.add), reads=["t1", "t2"], writes=["t1"])
        for which, tab in ((0, stab), (1, ctab)):
            if which == 1:
                P.op("dve", lambda e: e.tensor_scalar(out=t1[:], in0=t1[:], scalar1=float(np.pi / 2), scalar2=None, op0=ALU.add), reads=["t1"], writes=["t1"])
            P.op("dve", lambda e: e.tensor_scalar(out=t3[:], in0=t1[:], scalar1=float(np.pi), scalar2=None, op0=ALU.is_gt), reads=["t1"], writes=["t3", "posf"])
            P.op("dve", lambda e: e.scalar_tensor_tensor(out=t2[:], in0=t3[:], scalar=-TWO_PI, in1=t1[:], op0=ALU.mult, op1=ALU.add), reads=["t1", "t3", "posf"], writes=["t2"])
            P.op("dve", lambda e: e.tensor_scalar(out=t3[:], in0=t2[:], scalar1=float(-np.pi), scalar2=None, op0=ALU.is_lt), reads=["t2"], writes=["t3", "posf"])
            P.op("dve", lambda e: e.scalar_tensor_tensor(out=t2[:], in0=t3[:], scalar=TWO_PI, in1=t2[:], op0=ALU.mult, op1=ALU.add), reads=["t2", "t3", "posf"], writes=["t2"])
            P.op("dve", lambda e, tab=tab: e.tensor_scalar(out=tab[:], in0=t2[:], scalar1=PI_LO, scalar2=-PI_LO, op0=ALU.min, op1=ALU.max), reads=["t2"], writes=[tabname + ("c" if which else "s"), tabname])

        def sin_part():
            for which, tab in ((0, stab), (1, ctab)):
                nm = tabname + ("c" if which else "s")
                P.op("act", lambda e, tab=tab: e.activation(out=tab[:], in_=tab[:], func=AF.Sin), reads=[nm], writes=[nm, tabname])
        if sin_later:
            return sin_part
        sin_part()
        return None

    def rope_evac(b, dst, dst_name, ctab, stab, tabname, Rmat, abufs, banks=None):
        ab, abname = abufs[0]
        abufs.append(abufs.pop(0))
        P.op("dve", lambda e: e.tensor_tensor(out=ab[:, 0, :], in0=psum[b][:], in1=ctab[:], op=ALU.mult), reads=[psn[b], tabname], writes=[abname + "c"])
        P.op("dve", lambda e: e.tensor_tensor(out=ab[:, 1, :], in0=psum[b][:], in1=stab[:], op=ALU.mult), reads=[psn[b], tabname], writes=[abname + "s"])

        def later():
            b2 = next_bank(banks) if banks is not None else next_bank()
            P.op("pe", lambda e: e.matmul(psum[b2][:], lhsT=ident_b, rhs=ab[:, 0, :], start=True, stop=False), reads=[abname + "c", "bfm"], writes=[psn[b2]])
            P.op("pe", lambda e: e.matmul(psum[b2][:], lhsT=Rmat, rhs=ab[:, 1, :], start=False, stop=True), reads=[abname + "s", "bfm"], writes=[psn[b2]])
            P.op("act", lambda e: e.activation(out=dst, in_=psum[b2][:], func=AF.Copy), reads=[psn[b2]], shared=[dst_name])
        defer(later)

    evac_alt = [0]

    def copy_evac(out, in_, reads, writes, shared=(), eng=None):
        evac_alt[0] += 1
        if eng is None:
            eng = "act" if evac_alt[0] % 2 == 0 else "dve"
        if eng == "act":
            P.op("act", lambda e: e.activation(out=out, in_=in_, func=AF.Copy), reads=reads, writes=writes, shared=shared)
        else:
            P.op("dve", lambda e: e.tensor_copy(out=out, in_=in_), reads=reads, writes=writes, shared=shared)

    def latent_norm(nch, b_list, latf, sq, rbc, gcol0, dst, tok_sl, dst_name, dim):
        for cc, b in enumerate(b_list):
            P.op("act", lambda e, cc=cc, b=b: e.activation(out=latf[:, cc, :], in_=psum[b][:], func=AF.Copy), reads=[psn[b]], writes=["latf"])
        bs = next_bank()
        for cc in range(nch):
            P.op("dve", lambda e, cc=cc: e.tensor_tensor(out=sq[:], in0=latf[:, cc, :], in1=latf[:, cc, :], op=ALU.mult), reads=["latf"], writes=["sq"])
            P.op("pe", lambda e, cc=cc: e.matmul(psum[bs][:], lhsT=ones_f[:], rhs=sq[:], start=(cc == 0), stop=(cc == nch - 1)),
                 reads=["sq", "ones_f"], writes=[psn[bs]])
        P.op("act", lambda e: e.activation(out=rbc[:], in_=psum[bs][:], func=AF.Sqrt, bias=epscol, scale=1.0 / dim), reads=[psn[bs], "epscol"], writes=["rbc"])
        P.op("dve", lambda e: e.reciprocal(out=rbc[:], in_=rbc[:]), reads=["rbc"], writes=["rbc"])
        for cc in range(nch):
            P.op("dve", lambda e, cc=cc: e.scalar_tensor_tensor(out=dst[:, cc, tok_sl], in0=latf[:, cc, :], scalar=cols[:, gcol0 + cc:gcol0 + cc + 1],
                                                                in1=rbc[:], op0=ALU.mult, op1=ALU.mult),
                 reads=["latf", "rbc", "cols"], writes=[dst_name])

    epscol_t = RP.alloc("epscol", [128, 1], F32)
    epscol = epscol_t[:, 0:1]
    P.op("pool", lambda e: e.memset(epscol_t[:], EPS), writes=["epscol"])

    hT = RO.alloc("hT", [128, KC, 1024], BF16)

    def proj_fm(s3, sname, c0, blk, b):
        for kc in range(KC):
            P.op("pe", lambda e, kc=kc: e.matmul(psum[b][:], lhsT=s3[:, kc, c0:c0 + 128], rhs=hT[:, kc, blk * 512:(blk + 1) * 512],
                                                 start=(kc == 0), stop=(kc == KC - 1)),
                 reads=[wn(sname, c0), "hT"], writes=[psn[b]])
        flush()

    for pas, tok0 in (("other", NOWN), ("own", 0)):
        RX.reset()
        xst = [(RX.alloc("xst%d" % i, [128, D], F32), "xst%d" % i) for i in range(2)]
        xnb = [(RX.alloc("xn%d" % i, [128, D], BF16), "xn%d" % i) for i in range(2)]
        junk = RX.alloc("junk", [128, D], BF16)
        if pas == "own":
            preloaded["own"] = inproj_load(4)
            P.barrier()
        src = []
        for i in range(8):
            xt, xname = xst[i % 2]

            def loader(i=i, xt=xt, xname=xname):
                P.dma("sp", xt[:], x_d[tok0 + i * 128:tok0 + (i + 1) * 128, :], writes=[xname])
            src.append((xt[:], xname, loader))
        if pas == "other":
            make_hT(src, hT, 0, a1, modcols[:, 0:16], "a1", [(xb[:], [xbn]) for xb, xbn in xnb], junk, "hT", plain=True, hook=ada_hook)
            P.op("dve", lambda e: e.scalar_tensor_tensor(out=a1[:], in0=modcols[:, 16:32], scalar=1.0, in1=cols[:, 0:16], op0=ALU.add, op1=ALU.mult),
                 reads=["modcols", "cols"], writes=["a1"])
            for kc in range(KC):
                P.op("dve", lambda e, kc=kc: e.tensor_scalar(out=hT[:, kc, :], in0=hT[:, kc, :], scalar1=a1[:, kc:kc + 1], scalar2=modcols[:, kc:kc + 1],
                                                             op0=ALU.mult, op1=ALU.add), reads=["hT", "a1", "modcols"], writes=["hT"])
        else:
            make_hT(src, hT, 0, a1, modcols[:, 0:16], "a1", [(xb[:], [xbn]) for xb, xbn in xnb], junk, "hT")
        P.barrier()
        RX.reset()
        ctab = [RX.alloc("ctab%d" % i, [128, 512], F32) for i in range(2)]
        stab = [RX.alloc("stab%d" % i, [128, 512], F32) for i in range(2)]
        posi = RX.alloc("posi", [128, 512], I32)
        posf = RX.alloc("posf", [128, 512], F32)
        t1 = RX.alloc("t1", [128, 512], F32)
        t2 = RX.alloc("t2", [128, 512], F32)
        t3 = posf
        kiA = RX.alloc("kiA", [128, 512], I32)
        abufs = [(RX.alloc("ab%d" % i, [128, 2, 512], BF16), "ab%d" % i) for i in range(2)]
        latf = RX.alloc("latf", [128, 4, 512], F32)
        sq = RX.alloc("sq", [128, 512], F32)
        rbc = RX.alloc("rbc", [128, 512], F32)
        tiles = [4, 7, 5, 6, 0, 1, 2, 3] if pas == "own" else [4, 7, 5, 2, 3]
        for blk in range(2):
            rope_tables(tok0, blk, cols[:, 39:40], ctab[blk], stab[blk], posi, posf, t1, t2, t3, "tab%d" % blk, ki=kiA)
        for ti, t in enumerate(tiles):
            if ti == 0:
                assert t == 4
                s3, sname = preloaded.pop(pas)
            else:
                s3, sname = inproj_load(t)
            if t == 7:
                for blk in range(2):
                    bl = []
                    for cc in range(2):
                        b = next_bank()
                        proj_fm(s3, sname, cc * 128, blk, b)
                        bl.append(b)
                    latent_norm(2, bl, latf, sq, rbc, 36, ckvT, slice(tok0 + blk * 512, tok0 + (blk + 1) * 512), "ckvT", 256)
                    b = next_bank()
                    proj_fm(s3, sname, 256, blk, b)
                    rope_evac(b, krT[:, tok0 + blk * 512:tok0 + (blk + 1) * 512], "krT", ctab[blk], stab[blk], "tab%d" % blk, Rm_b, abufs)
                for blk in range(2):
                    rope_tables(tok0, blk, cols[:, 38:39], ctab[blk], stab[blk], posi, posf, t1, t2, t3, "tab%d" % blk, ki=kiA)
            elif t == 6:
                for blk in range(2):
                    bl = []
                    for cc in range(4):
                        b = next_bank()
                        proj_fm(s3, sname, cc * 128, blk, b)
                        bl.append(b)
                    latent_norm(4, bl, latf, sq, rbc, 32, cqT, slice(blk * 512, (blk + 1) * 512), "cqT", 512)
            elif t < 4:
                for cc in range(4):
                    h = (t % 2) * 4 + cc
                    for blk in range(2):
                        b = next_bank()
                        proj_fm(s3, sname, cc * 128, blk, b)
                        if t < 2:
                            dst, dname = Qt[:, h, blk * 512:(blk + 1) * 512], "Q%d" % h
                        else:
                            dst, dname = Kt[:, h, tok0 + blk * 512:tok0 + (blk + 1) * 512], "K%d" % h
                        rope_evac(b, dst, dname, ctab[blk], stab[blk], "tab%d" % blk, Rd_b, abufs)
            else:
                for tt in range(8):
                    b = next_bank()
                    kt = (tok0 // 128) + tt
                    for kc in range(KC):
                        P.op("pe", lambda e, kc=kc, tt=tt, b=b: e.matmul(psum[b][:], lhsT=hT[:, kc, tt * 128:(tt + 1) * 128], rhs=s3[:, kc, :],
                                                                         start=(kc == 0), stop=(kc == KC - 1)),
                             reads=[sname + "a", sname + "b", "hT"], writes=[psn[b]])
                    flush()
                    h0 = (t - 4) * 4
                    copy_evac(Vt[:, kt, h0:h0 + 4, 0:128], psum[b][:].rearrange("p (h e) -> p h e", h=4), [psn[b]], [], shared=["V%d" % kt], eng="act")
        flush()

    if debug:
        P.barrier()
        P.dma("sp", dbg["hT"], hT[:].rearrange("p a b -> p (a b)"), reads=["hT"])
        P.dma("sp", dbg["K"], Kt[:].rearrange("p a b -> p (a b)"), reads=["K%d" % h for h in range(8)])
        P.dma("sp", dbg["Q"], Qt[:].rearrange("p a b -> p (a b)"), reads=["Q%d" % h for h in range(8)])
        P.dma("sp", dbg["V"], Vt[:].rearrange("p a b c -> p (a b c)"), reads=["V%d" % k for k in range(16)] + ["Vones"])
        P.dma("sp", dbg["L"][:, 0:4096], cqT[:].rearrange("p a b -> p (a b)"), reads=["cqT"])
        P.dma("sp", dbg["L"][:, 4096:8192], ckvT[:].rearrange("p a b -> p (a b)"), reads=["ckvT"])
        P.dma("sp", dbg["L"][:, 8192:10240], krT[:], reads=["krT"])
        P.dma("sp", dbg["mod"], modcols[:], reads=["modcols"])
    if stop_after == "A":
        P.finish("sp")
        return nc, P

    P.barrier()
    RO.reset()
    oT = RO.alloc("oT", [128, 2, KC, 512], BF16)
    RX.reset()
    pt = [(RX.alloc("pt%d" % i, [128, 1024], BF16), "pt%d" % i) for i in range(3)]
    osb = RX.alloc("osb", [128, 8, 130], F32)
    ob = RX.alloc("ob", [128, 4, 128], F32)
    onb = RX.alloc("onb", [128, 8, 128], BF16)
    jk2 = RX.alloc("jk2", [128, 128], BF16)
    SB = [[0, 1], [2, 3]]
    OB = [4, 5, 6]
    TB = 7

    def acc_ap(a):
        return psum[OB[a // 3]][:, (a % 3) * 130:(a % 3) * 130 + 129]

    stages = {}

    def stage(kt, fn):
        stages.setdefault(kt, []).append(fn)

    def run_stages(kt):
        for fn in stages.pop(kt, []):
            fn()

    def attention_unit(qk_emit, scale, finalize):
        def QK(kt):
            qk_emit(kt, SB[kt % 2])

        QK(0)
        QK(1)
        started = set()
        for kt in range(16):
            pb, pbname = pt[kt % 3]
            b0 = SB[kt % 2][0]
            P.op("act", lambda e, b0=b0, pb=pb: e.activation(out=pb[:], in_=psum_all[:, b0 * 512:(b0 + 2) * 512], func=AF.Exp, scale=scale),
                 reads=[psn[b0], psn[b0 + 1]], writes=[pbname])
            for c in range(2):
                for j in range(4):
                    a = j * 2 + c
                    hv = vh[c]
                    st = (kt == 0 and OB[a // 3] not in started)
                    started.add(OB[a // 3])
                    P.op("pe", lambda e, a=a, c=c, j=j, kt=kt, hv=hv, pb=pb, st=st: e.matmul(
                        acc_ap(a), lhsT=pb[:, c * 512 + j * 128:c * 512 + (j + 1) * 128], rhs=Vt[:, kt, hv, 0:129],
                        start=st, stop=(kt == 15), skip_group_check=True),
                        reads=[pbname, "V%d" % kt, "Vones"], writes=[psn[OB[a // 3]]])
            if kt + 2 < 16:
                QK(kt + 2)
            run_stages(kt)
        for i, b in enumerate(OB):
            n = 3 if i < 2 else 2
            o = osb[:, i * 3:i * 3 + n, :]
            src = psum[b][:, 0:n * 130].rearrange("p (a e) -> p a e", a=n)
            P.op("dve", lambda e, o=o, src=src: e.tensor_copy(out=o, in_=src), reads=[psn[b]], shared=["osb"])
        finalize()

    vh = [0, 0]

    wq = wslots[0][:, 0:6144].rearrange("p (a b) -> p a b", a=4)
    wkv = wslots[1][:, 0:4096].rearrange("p (a b) -> p a b", a=2)
    qrT = nc.alloc_sbuf_tensor_at("W_qrT", [128, 4, NOWN], BF16, offset=RW.base + 16384 + 8192)
    ctabC = [RX.alloc("ctabC%d" % i, [128, 512], F32) for i in range(2)]
    stabC = [RX.alloc("stabC%d" % i, [128, 512], F32) for i in range(2)]
    posiC = RX.alloc("posiC", [128, 512], I32)
    kiC = RX.alloc("kiC", [128, 512], I32)
    posfC = RX.alloc("posfC", [128, 512], F32)
    t1C = RX.alloc("t1C", [128, 512], F32)
    t2C = RX.alloc("t2C", [128, 512], F32)
    abufsC = [(RX.alloc("abC%d" % i, [128, 2, 512], BF16), "abC%d" % i) for i in range(1)]

    def mla_prefetch_weights():
        P.dma("pool", wq, w_qb_d.rearrange("(kc p) n -> p kc n", p=128), writes=["wslot0"], shared=["wslot0a", "wslot0b"])
        P.dma("pool", wkv, w_kvb_d.rearrange("(kc p) n -> p kc n", p=128), writes=["wslot1"], shared=["wslot1a", "wslot1b"])

    mla_sin = []

    def mla_tables(blk):
        mla_sin.append(rope_tables(0, blk, cols[:, 39:40], ctabC[blk], stabC[blk], posiC, posfC, t1C, t2C, posfC, "tabC%d" % blk, ki=kiC, sin_later=True))

    def mla_tables_sin():
        for fn in mla_sin:
            fn()

    for h in range(8):
        for qb in range(2):
            def qk_emit(kt, banks, h=h, qb=qb):
                for c in range(2):
                    P.op("pe", lambda e, c=c: e.matmul(psum[banks[c]][:], lhsT=Kt[c * 64:(c + 1) * 64, h, kt * 128:(kt + 1) * 128],
                                                       rhs=Qt[c * 64:(c + 1) * 64, h, qb * 512:(qb + 1) * 512], start=True, stop=True),
                         reads=["K%d" % h, "Q%d" % h], writes=[psn[banks[c]]])

            def finalize(h=h, qb=qb):
                def s1():
                    P.op("dve", lambda e: e.reciprocal(out=rcp[:, 0:8], in_=osb[:, :, 128]), reads=["osb"], writes=["rcp"])
                    P.op("dve", lambda e: e.tensor_scalar(out=rcp[:, 8:12], in0=rcp[:, 0:8].rearrange("p (j c) -> p j c", c=2)[:, :, 1], scalar1=neglam, scalar2=None, op0=ALU.mult),
                         reads=["rcp", "lc"], writes=["rcp2"])
                    for j in range(4):
                        P.op("dve", lambda e, j=j: e.tensor_scalar(out=ob[:, j, :], in0=osb[:, 2 * j + 1, 0:128], scalar1=rcp[:, 8 + j:9 + j], scalar2=None, op0=ALU.mult),
                             reads=["osb", "rcp2"], writes=["ob%d" % j])
                        P.op("dve", lambda e, j=j: e.scalar_tensor_tensor(out=ob[:, j, :], in0=osb[:, 2 * j, 0:128], scalar=rcp[:, 2 * j:2 * j + 1], in1=ob[:, j, :],
                                                                          op0=ALU.mult, op1=ALU.add),
                             reads=["osb", "rcp", "ob%d" % j], writes=["ob%d" % j])

                def s2():
                    for j in range(4):
                        P.op("act", lambda e, j=j: e.activation(out=jk2[:], in_=ob[:, j, :], func=AF.Square, accum_out=ss[:, 8 + j:9 + j]),
                             reads=["ob%d" % j], writes=["jk2"], shared=["ssA"])

                def s3():
                    rsqrt_cols(4, 128)
                    for j in range(4):
                        P.op("dve", lambda e, j=j: e.scalar_tensor_tensor(out=onb[:, j, :], in0=ob[:, j, :], scalar=rstd[:, 8 + j:9 + j], in1=g08_bc[:], op0=ALU.mult, op1=ALU.mult),
                             reads=["ob%d" % j, "rstdA", "g08"], shared=["onb"])

                def s4():
                    for j in range(4):
                        P.op("pe", lambda e, j=j: e.transpose(out=psum_b[TB][:, j, :], in_=onb[:, j, :], identity=ident_b), reads=["onb", "bfm"], writes=[psn[TB]])
                    P.op("dve", lambda e: e.tensor_copy(out=oT[:, qb, h, :], in_=psum_b[TB][:, 0:4, :].rearrange("p a b -> p (a b)")), reads=[psn[TB]], shared=["oT"])
                stage(1, s1)
                stage(4, s2)
                stage(7, s3)
                stage(10, s4)

            vh[0] = h
            vh[1] = h
            it = h * 2 + qb
            if it == 0:
                ada_load(8)
            if it < 8:
                arow = ob[0:1].rearrange("p a b -> p (a b)")
                stage(0, lambda it=it: ada_load(9 + 2 * it))
                stage(7, lambda it=it: ada_row_x(8 + 2 * it, TB, arow))
                if it < 7:
                    stage(8, lambda it=it: ada_load(10 + 2 * it))
                stage(9, lambda it=it: ada_row_y(8 + 2 * it, TB, arow))
                stage(13, lambda it=it: ada_row_x(9 + 2 * it, TB, arow))
                stage(15, lambda it=it: ada_row_y(9 + 2 * it, TB, arow))
            if it == 13:
                stage(0, mla_prefetch_weights)
            if it == 14:
                stage(2, lambda: mla_tables(0))
                stage(9, lambda: mla_tables(1))
            if it == 15:
                stage(12, mla_tables_sin)
            attention_unit(qk_emit, 0.125, finalize)
    if stop_after == "B":
        for kt in range(16):
            run_stages(kt)

    if stop_after == "B":
        if debug:
            P.barrier()
            for qb in range(2):
                P.dma("sp", dbg["O"][:, qb * 8192:qb * 8192 + 4096], oT[:, qb, 0:8, :].rearrange("p b c -> p (b c)"), reads=["oT"])
            P.dma("sp", dbg["mod"], modcols[:], reads=["modcols"])
        P.finish("sp")
        return nc, P

    sname, sname2 = "wslot0", "wslot1"
    ctab, stab, abufs = ctabC, stabC, abufsC
    NB = (0, 1, 2, 3)
    for h in range(8):
        for blk in range(2):
            b = next_bank(NB)
            for kc in range(4):
                P.op("pe", lambda e, kc=kc, h=h, blk=blk, b=b: e.matmul(psum[b][:], lhsT=wq[:, kc, h * 128:(h + 1) * 128], rhs=cqT[:, kc, blk * 512:(blk + 1) * 512],
                                                                        start=(kc == 0), stop=(kc == 3)), reads=[sname, "cqT"], writes=[psn[b]])
            copy_evac(Qt[:, h, blk * 512:(blk + 1) * 512], psum[b][:], [psn[b]], ["Q%d" % h])
            run_stages(h * 2 + blk)
    for hp in range(4):
        for blk in range(2):
            b = next_bank(NB)
            for kc in range(4):
                P.op("pe", lambda e, kc=kc, hp=hp, blk=blk, b=b: e.matmul(psum[b][:], lhsT=wq[:, kc, 1024 + hp * 128:1024 + (hp + 1) * 128],
                                                                          rhs=cqT[:, kc, blk * 512:(blk + 1) * 512], start=(kc == 0), stop=(kc == 3)),
                     reads=[sname, "cqT"], writes=[psn[b]])
            ab, abname = abufs[0]
            abufs.append(abufs.pop(0))
            P.op("dve", lambda e, b=b, ab=ab, blk=blk: e.tensor_tensor(out=ab[:, 0, :], in0=psum[b][:], in1=ctab[blk][:], op=ALU.mult), reads=[psn[b], "tabC%d" % blk], writes=[abname])
            P.op("dve", lambda e, b=b, ab=ab, blk=blk: e.tensor_tensor(out=ab[:, 1, :], in0=psum[b][:], in1=stab[blk][:], op=ALU.mult), reads=[psn[b], "tabC%d" % blk], writes=[abname])
            b2 = next_bank(NB)
            P.op("pe", lambda e, b2=b2, ab=ab: e.matmul(psum[b2][:], lhsT=ident_b, rhs=ab[:, 0, :], start=True, stop=False), reads=[abname, "bfm"], writes=[psn[b2]])
            P.op("pe", lambda e, b2=b2, ab=ab: e.matmul(psum[b2][:], lhsT=Rm_b, rhs=ab[:, 1, :], start=False, stop=True), reads=[abname, "bfm"], writes=[psn[b2]])
            P.op("act", lambda e, b2=b2, hp=hp, blk=blk: e.activation(out=qrT[:, hp, blk * 512:(blk + 1) * 512], in_=psum[b2][:], func=AF.Copy), reads=[psn[b2]],
                 shared=["qrT", "wslot1a", "wslot1b"])
    for h in range(8):
        for blk in range(4):
            b = next_bank(NB)
            for kc in range(2):
                P.op("pe", lambda e, kc=kc, h=h, blk=blk, b=b: e.matmul(psum[b][:], lhsT=wkv[:, kc, h * 128:(h + 1) * 128], rhs=ckvT[:, kc, blk * 512:(blk + 1) * 512],
                                                                        start=(kc == 0), stop=(kc == 1)), reads=[sname2, "ckvT"], writes=[psn[b]])
            copy_evac(Kt[:, h, blk * 512:(blk + 1) * 512], psum[b][:], [psn[b]], ["K%d" % h])
    for tt in range(16):
        for half in range(2):
            b = next_bank(NB)
            for kc in range(2):
                P.op("pe", lambda e, kc=kc, tt=tt, half=half, b=b: e.matmul(psum[b][:], lhsT=ckvT[:, kc, tt * 128:(tt + 1) * 128],
                                                                            rhs=wkv[:, kc, 1024 + half * 512:1024 + (half + 1) * 512], start=(kc == 0), stop=(kc == 1)),
                     reads=[sname2, "ckvT"], writes=[psn[b]])
            copy_evac(Vt[:, tt, half * 4:half * 4 + 4, 0:128], psum[b][:].rearrange("p (h e) -> p h e", h=4), [psn[b]], ["V%d" % tt])

    flush()
    mla_scale = float(192.0 ** -0.5)
    for hp in range(4):
        for qb in range(2):
            def qk_emit(kt, banks, hp=hp, qb=qb):
                for c in range(2):
                    h = 2 * hp + c
                    P.op("pe", lambda e, c=c, h=h: e.matmul(psum[banks[c]][:], lhsT=Kt[:, h, kt * 128:(kt + 1) * 128], rhs=Qt[:, h, qb * 512:(qb + 1) * 512],
                                                            start=True, stop=False), reads=["K%d" % h, "Q%d" % h], writes=[psn[banks[c]]])
                for c in range(2):
                    P.op("pe", lambda e, c=c: e.matmul(psum[banks[c]][:], lhsT=krT[c * 64:(c + 1) * 64, kt * 128:(kt + 1) * 128],
                                                       rhs=qrT[c * 64:(c + 1) * 64, hp, qb * 512:(qb + 1) * 512], start=False, stop=True),
                         reads=["krT", "qrT"], writes=[psn[banks[c]]])

            def finalize(hp=hp, qb=qb):
                def s1():
                    P.op("dve", lambda e: e.reciprocal(out=rcp[:, 0:8], in_=osb[:, :, 128]), reads=["osb"], writes=["rcp"])
                    for c in range(2):
                        for j in range(4):
                            a = j * 2 + c
                            P.op("dve", lambda e, a=a, j=j, c=c: e.tensor_scalar(out=onb[:, c * 4 + j, :], in0=osb[:, a, 0:128], scalar1=rcp[:, a:a + 1], scalar2=None, op0=ALU.mult),
                                 reads=["osb", "rcp"], shared=["onb"])

                def s2():
                    for c in range(2):
                        for j in range(4):
                            P.op("pe", lambda e, j=j, c=c: e.transpose(out=psum_b[TB][:, c * 4 + j, :], in_=onb[:, c * 4 + j, :], identity=ident_b),
                                 reads=["onb", "bfm"], writes=[psn[TB]])
                    for c in range(2):
                        P.op("dve", lambda e, c=c: e.tensor_copy(out=oT[:, qb, 8 + 2 * hp + c, :], in_=psum_b[TB][:, c * 4:c * 4 + 4, :].rearrange("p a b -> p (a b)")),
                             reads=[psn[TB]], shared=["oT"])
                stage(1, s1)
                stage(5, s2)

            vh[0] = 2 * hp
            vh[1] = 2 * hp + 1
            attention_unit(qk_emit, mla_scale, finalize)
    for kt in range(16):
        run_stages(kt)

    if debug:
        P.barrier()
        P.dma("sp", dbg["O"], oT[:].rearrange("p a b c -> p (a b c)"), reads=["oT"])
        P.dma("sp", dbg["QR"], qrT[:].rearrange("p a b -> p (a b)"), reads=["qrT"])
        P.dma("sp", dbg["mod"], modcols[:], reads=["modcols"])
    if stop_after == "C":
        P.finish("sp")
        return nc, P

    P.barrier()
    P.op("dve", lambda e: e.scalar_tensor_tensor(out=a2[:], in0=modcols[:, 64:80], scalar=1.0, in1=cols[:, 16:32], op0=ALU.add, op1=ALU.mult),
         reads=["modcols", "cols"], writes=["a2"])
    RD = Region(nc, "D", base + 38912, 32768 + 16384 + 33280)
    hid = RD.alloc("hid", [128, 64, 512], BF16)
    gate_bc = RD.alloc("gate_bc", [128, D], F32)
    gfin_bc = RD.alloc("gfin_bc", [128, D], F32)
    RE = Region(nc, "E", base + 121344, 20480 + 32768 + 36864)
    xmid = RE.alloc("xmid", [128, 4, D], F32)
    wsl = [RE.alloc("wd%d" % i, [128, 8192], BF16) for i in range(2)]
    junk = RE.alloc("junkD", [128, D], BF16)
    xnb = [(RE.alloc("xnD", [128, D], BF16), "xnD")]
    yT = [(RE.alloc("yT%d" % i, [128, 512], F32)[:], ["yT%d" % i]) for i in range(2)]
    yT.append((junk.bitcast(F32)[:, 0:512], ["junk"]))
    yT.append((xnb[0][0].bitcast(F32)[:, 0:512], ["xnD"]))
    tmpd_all = RE.alloc("tmpd", [128, 1024], F32)
    tmpd = [(tmpd_all[:, i * 512:(i + 1) * 512], "tmpd%d" % i) for i in range(2)]
    xnD2 = tmpd_all.bitcast(BF16)
    rbf = [(RE.alloc("rbf%d" % i, [128, 512], BF16), "rbf%d" % i) for i in range(2)]
    diag2 = [RE.alloc("diag%d" % i, [128, 128], F32) for i in range(2)]
    wd_i = [0]

    def next_wd():
        i = wd_i[0] % 2
        wd_i[0] += 1
        return wsl[i], "wd%d" % i

    P.dma("sp", gfin_bc[:], gfin_d.partition_broadcast(128), writes=["gfin_bc"])
    for kc in range(KC):
        b = next_bank()
        dg, dgn = diag2[kc % 2], "diag%d" % (kc % 2)
        P.op("dve", lambda e, kc=kc, dg=dg: e.tensor_scalar(out=dg[:], in0=ident_f[:], scalar1=modcols[:, 32 + kc:33 + kc], scalar2=None, op0=ALU.mult),
             reads=["ident_f", "modcols"], writes=[dgn])
        P.op("pe", lambda e, b=b, dg=dg: e.matmul(psum[b][:, 0:128], lhsT=ones_f[:], rhs=dg[:], start=True, stop=True), reads=["ones_f", dgn], writes=[psn[b]])
        P.op("act", lambda e, b=b, kc=kc: e.activation(out=gate_bc[:, kc * 128:(kc + 1) * 128], in_=psum[b][:, 0:128], func=AF.Copy), reads=[psn[b]], shared=["gate_bc"])

    w_out_v = w_out_d.rearrange("(kc p) n -> p kc n", p=128)
    w_ff1_v = w_ff1_d.rearrange("(kc p) n -> p kc n", p=128)
    w_ff2_v = w_ff2_d.rearrange("(fc p) n -> p fc n", p=128)
    h2T = None
    for blk in range(2):
        for tt in range(4):
            r0 = blk * 512 + tt * 128
            P.dma("sp", xmid[:, tt, :], x_d[r0:r0 + 128, :], reads=[], writes=["xmid%d" % tt])
        for cb in range(4):
            slot, sname = next_wd()
            s3 = slot.reshape([128, KC, 512])
            load_w_cols(s3, sname, w_out_v[:, :, cb * 512:(cb + 1) * 512])
            for tt in range(4):
                b = next_bank()
                for kc in range(KC):
                    P.op("pe", lambda e, kc=kc, tt=tt, b=b, s3=s3: e.matmul(psum[b][:], lhsT=oT[:, blk, kc, tt * 128:(tt + 1) * 128], rhs=s3[:, kc, :],
                                                                            start=(kc == 0), stop=(kc == KC - 1)),
                         reads=[sname + "a", sname + "b", "oT"], writes=[psn[b]])
                tm, tmname = tmpd[(cb * 4 + tt) % 2]
                P.op("dve", lambda e, b=b, tm=tm, cb=cb: e.tensor_tensor(out=tm, in0=psum[b][:], in1=gate_bc[:, cb * 512:(cb + 1) * 512], op=ALU.mult),
                     reads=[psn[b], "gate_bc"], writes=[tmname])
                xs = xmid[:, tt, cb * 512:(cb + 1) * 512]
                P.op("dve", lambda e, xs=xs, tm=tm: e.tensor_tensor(out=xs, in0=xs, in1=tm, op=ALU.add), reads=[tmname, "xmid%d" % tt], writes=["xmid%d" % tt])
        h2T = oT[:, blk]
        make_hT([(xmid[:, tt, :], "xmid%d" % tt, None) for tt in range(4)], h2T, 0, a2, modcols[:, 48:64], "a2",
                [(xnb[0][0][:], ["xnD"]), (xnD2[:], ["tmpd0", "tmpd1"])], junk, "oT")
        for t in range(16):
            slot, sname = next_wd()
            s3 = slot.reshape([128, KC, 512])
            load_w_cols(s3, sname, w_ff1_v[:, :, t * 512:(t + 1) * 512])
            for cc in range(4):
                f = t * 4 + cc
                b = next_bank()
                for kc in range(KC):
                    P.op("pe", lambda e, kc=kc, cc=cc, b=b, s3=s3: e.matmul(psum[b][:], lhsT=s3[:, kc, cc * 128:(cc + 1) * 128], rhs=h2T[:, kc, :],
                                                                            start=(kc == 0), stop=(kc == KC - 1)),
                         reads=[wn(sname, cc * 128), "oT"], writes=[psn[b]])
                rb, rbname = rbf[f % 2]
                P.op("act", lambda e, b=b, rb=rb: e.activation(out=rb[:], in_=psum[b][:], func=AF.Relu), reads=[psn[b]], writes=[rbname])
                P.op("dve", lambda e, b=b, rb=rb, f=f: e.scalar_tensor_tensor(out=hid[:, f, :], in0=psum[b][:], scalar=0.0, in1=rb[:], op0=ALU.max, op1=ALU.mult),
                     reads=[psn[b], rbname], shared=["hid"])
        for cbk in range(4):
            banks = [(cbk % 2) * 4 + j for j in range(4)]
            for fg in range(4):
                slot, sname = next_wd()
                s3 = slot.reshape([128, 16, 512])
                src = w_ff2_v[:, fg * 16:(fg + 1) * 16, cbk * 512:(cbk + 1) * 512]
                P.dma("pool", s3[:, 0:8, :], src[:, 0:8, :], writes=[sname + "fa"], shared=[sname + "a", sname + "b"])
                P.dma("pool", s3[:, 8:16, :], src[:, 8:16, :], writes=[sname + "fb"], shared=[sname + "a", sname + "b"])
                for fi in range(16):
                    fc = fg * 16 + fi
                    for j in range(4):
                        P.op("pe", lambda e, fi=fi, fc=fc, j=j, s3=s3: e.matmul(psum[banks[j]][:], lhsT=s3[:, fi, j * 128:(j + 1) * 128], rhs=hid[:, fc, :],
                                                                                 start=(fc == 0), stop=(fc == 63)),
                             reads=[sname + ("fa" if fi < 8 else "fb"), "hid"], writes=[psn[banks[j]]])
                if fg == 0:
                    flush()
            for j in range(4):
                c = cbk * 4 + j
                yt, ytnames = yT[j]
                P.op("act", lambda e, j=j, yt=yt, c=c, banks=banks: e.activation(out=yt, in_=psum[banks[j]][:], func=AF.Identity, scale=modcols[:, 80 + c:81 + c]),
                     reads=[psn[banks[j]], "modcols"], writes=ytnames)

                def later(j=j, c=c, yt=yt, ytnames=ytnames, banks=banks):
                    for tt in range(4):
                        P.op("pe", lambda e, tt=tt: e.transpose(out=psum[banks[j]][:, tt * 128:(tt + 1) * 128], in_=yt[:, tt * 128:(tt + 1) * 128], identity=ident_f[:]),
                             reads=ytnames + ["ident_f"], writes=[psn[banks[j]]])
                    xs = xmid[:, :, c * 128:(c + 1) * 128]
                    P.op("dve", lambda e: e.tensor_tensor(out=xs, in0=xs, in1=psum[banks[j]][:].rearrange("p (t c) -> p t c", t=4), op=ALU.add),
                         reads=[psn[banks[j]]] + ["xmid%d" % tt for tt in range(4)], writes=["xmid%d" % tt for tt in range(4)])
                defer(later)
        flush()
        for tt in range(4):
            c = tt
            P.op("act", lambda e, tt=tt, c=c: e.activation(out=junk[:], in_=xmid[:, tt, :], func=AF.Square, accum_out=ss[:, c:c + 1]), reads=["xmid%d" % tt], writes=["junk", "ss%d" % c])
            P.op("act", lambda e, c=c: e.activation(out=ms[:, c:c + 1], in_=ss[:, c:c + 1], func=AF.Sqrt, bias=epscol, scale=1.0 / D), reads=["ss%d" % c, "epscol"], writes=["ms%d" % c])
            P.op("dve", lambda e, c=c: e.reciprocal(out=rstd[:, c:c + 1], in_=ms[:, c:c + 1]), reads=["ms%d" % c], writes=["rstd%d" % c])
            P.op("dve", lambda e, tt=tt, c=c: e.scalar_tensor_tensor(out=xmid[:, tt, :], in0=xmid[:, tt, :], scalar=rstd[:, c:c + 1], in1=gfin_bc[:], op0=ALU.mult, op1=ALU.mult),
                 reads=["xmid%d" % tt, "rstd%d" % c, "gfin_bc"], writes=["xmid%d" % tt])
            r0 = blk * 512 + tt * 128
            P.dma("sp", y_d[r0:r0 + 128, :], xmid[:, tt, :], reads=["xmid%d" % tt])
    P.finish("sp")
    return nc, P


def _const_mats():
    m = np.zeros((128, 384), np.float32)
    m[:, 0:128] = np.eye(128, dtype=np.float32)
    Rd = np.zeros((128, 128), np.float32)
    for c in range(2):
        for d in range(8):
            mlo = c * 64 + d
            Rd[mlo + 8, mlo] = -1.0
            Rd[mlo, mlo + 8] = 1.0
    Rm = np.zeros((128, 128), np.float32)
    for g in range(2):
        for d in range(32):
            mlo = g * 64 + d
            Rm[mlo + 32, mlo] = -1.0
            Rm[mlo, mlo + 32] = 1.0
    m[:, 128:256] = Rd
    m[:, 256:384] = Rm
    return m


def _inv_cols():
    inv_d = (1.0 / (np.float32(500000.0) ** (np.arange(0, 16, 2, dtype=np.float32) / np.float32(16)))).astype(np.float32)
    inv_m = (1.0 / (np.float32(10000.0) ** (np.arange(0, 64, 2, dtype=np.float32) / np.float32(64)))).astype(np.float32)
    cd = np.zeros(128, np.float32)
    cm = np.zeros(128, np.float32)
    for p in range(128):
        d = p % 64
        if d < 16:
            cd[p] = inv_d[d % 8]
        cm[p] = inv_m[d % 32]
    return cd, cm


_CACHE = {}


def kernel(x, c, positions, w_ada, b_ada, g_norm_mix, w_in, lambda_q1, lambda_k1, lambda_q2, lambda_k2,
           g_diff_sub, g_q_a, w_q_b, g_kv_a, w_kv_b, w_out, g_norm_ffn, w_ff1, w_ff2, g_final, _debug=False, _stop=None):
    f32 = np.float32
    x = np.asarray(x, f32)
    colmaj = lambda v: np.ascontiguousarray(np.asarray(v, f32).reshape(-1, 128).T)
    cd, cm = _inv_cols()
    cols = np.concatenate([colmaj(g_norm_mix[0]), colmaj(g_norm_ffn[0]), colmaj(g_q_a[0]), colmaj(g_kv_a[0]),
                           cd[:, None], cm[:, None], colmaj(b_ada[0])], axis=1).astype(f32)
    assert cols.shape == (128, 136)
    cmat = _const_mats()
    lam = np.stack([lambda_q1[0], lambda_k1[0], lambda_q2[0], lambda_k2[0]]).astype(f32)
    w_in0 = np.asarray(w_in[0], f32)
    w_in_ext = np.ascontiguousarray(np.concatenate([w_in0, w_in0[:, 3840:3904]], axis=1))
    wq = np.asarray(w_q_b[0], f32).reshape(512, 8, 192)
    w_qb = np.ascontiguousarray(np.concatenate([wq[:, :, :128].reshape(512, 1024), wq[:, :, 128:].reshape(512, 512)], axis=1))
    wkv = np.asarray(w_kv_b[0], f32).reshape(256, 8, 256)
    w_kvb = np.ascontiguousarray(np.concatenate([wkv[:, :, :128].reshape(256, 1024), wkv[:, :, 128:].reshape(256, 1024)], axis=1))
    shared = {
        "cols": cols, "cmat": cmat, "lam": lam, "gdiff": np.asarray(g_diff_sub, f32).reshape(1, 128),
        "gfin": np.asarray(g_final, f32).reshape(1, D), "w_ada": np.ascontiguousarray(w_ada[0], dtype=f32), "w_in": w_in_ext,
        "w_qb": w_qb, "w_kvb": w_kvb, "w_out": np.ascontiguousarray(w_out[0], dtype=f32),
        "w_ff1": np.ascontiguousarray(w_ff1[0], dtype=f32), "w_ff2": np.ascontiguousarray(w_ff2[0], dtype=f32),
    }
    in_maps = []
    for core in range(8):
        b, hf = core // 2, core % 2
        own = slice(hf * NOWN, (hf + 1) * NOWN)
        oth = slice((1 - hf) * NOWN, (2 - hf) * NOWN)
        m = dict(shared)
        m["x"] = np.ascontiguousarray(np.concatenate([x[b, own], x[b, oth]], axis=0))
        m["pos"] = np.ascontiguousarray(np.concatenate([positions[b, own], positions[b, oth]]).astype(np.int32).reshape(1, NTOK))
        m["c"] = colmaj(c[b])
        in_maps.append(m)
    key = (_debug, _stop)
    if key not in _CACHE:
        _CACHE[key] = build(debug=_debug, stop_after=_stop)
    nc, P = _CACHE[key]
    res = run_bass_kernel_spmd(nc, in_maps, core_ids=list(range(8)))
    if _debug:
        return res
    out = np.empty((4, 2048, D), f32)
    for core in range(8):
        b, hf = core // 2, core % 2
        out[b, hf * NOWN:(hf + 1) * NOWN] = res.results[core]["y"]
    return out
```
